# Optimizing a Trainium2 kernel written in Bass

```python
import jax, jax.numpy as jnp
from jax import lax
import numpy as np

D_MODEL = 1024
BATCH = 4
SEQ = 4096
DEPTH = 1

N_META = 16
GRID_W = 64
M_HEADS = 4
M_DV = 256
M_DK = 128
M_INNER = M_HEADS * M_DV
M_CONV = 3
M_CHUNK = 64
NA_HEADS = 8
NA_DH = 64
NA_INNER = NA_HEADS * NA_DH
NA_WIN_H_MAX = 8
NA_WIN_W = 16
NA_SEG_W = NA_WIN_W
NA_REGION_W = 2 * NA_WIN_W
N_BRANCH = 2
D_FF = 4 * D_MODEL
EPS = 1e-6
NEG_LOG_GATE = -1e9
OFF_MX = 0
OFF_MO = OFF_MX + M_INNER
OFF_MG = OFF_MO + M_INNER
OFF_Q = OFF_MG + 4 * M_HEADS
OFF_K = OFF_Q + NA_INNER
OFF_V = OFF_K + NA_INNER
OFF_G = OFF_V + NA_INNER
D_IN_PROJ = OFF_G + N_BRANCH * D_MODEL

kernel_name = "hybrid_mlstm_natten_block"


def rms_norm(x, g):
    xf = x.astype(jnp.float32)
    y = xf * lax.rsqrt(jnp.mean(jnp.square(xf), axis=-1, keepdims=True) + EPS)
    return (y * g.astype(jnp.float32)).astype(x.dtype)


def centred_depthwise_conv(x, w):
    k = w.shape[0]
    return lax.conv_general_dilated(x, w.astype(x.dtype), (1,), [(k // 2, k // 2)],
                                    dimension_numbers=('NWC', 'WIO', 'NWC'),
                                    feature_group_count=x.shape[-1])


def mlstm_chunkwise(q, k, v, log_i, log_f):
    B, H, Lp, dk = q.shape
    dv = v.shape[-1]
    nc = Lp // M_CHUNK
    chunk = lambda a: jnp.moveaxis(a.reshape(B, H, nc, M_CHUNK, *a.shape[3:]), 2, 0)
    causal = jnp.tril(jnp.ones((M_CHUNK, M_CHUNK), dtype=bool))

    def step(carry, inp):
        C, n, m = carry
        qt, kt, vt, li, lf = inp
        b = jnp.cumsum(lf, axis=-1)
        d = jnp.where(causal, b[..., :, None] - b[..., None, :] + li[..., None, :], -jnp.inf)
        m_inter = b + m[..., None]
        m_t = jnp.maximum(m_inter, jnp.max(d, axis=-1))
        w_inter = jnp.exp(m_inter - m_t)
        s = jnp.einsum('bhtd,bhsd->bhts', qt, kt) * jnp.exp(d - m_t[..., None])
        num = w_inter[..., None] * jnp.einsum('bhtd,bhde->bhte', qt, C) + jnp.einsum('bhts,bhse->bhte', s, vt)
        den = w_inter * jnp.einsum('bhtd,bhd->bht', qt, n) + jnp.sum(s, axis=-1)
        h = num / jnp.maximum(jnp.abs(den), jnp.exp(-m_t))[..., None]
        b_end = b[..., -1]
        a = b_end[..., None] - b + li
        m_new = jnp.maximum(b_end + m, jnp.max(a, axis=-1))
        decay = jnp.exp(b_end + m - m_new)
        kw = kt * jnp.exp(a - m_new[..., None])[..., None]
        C_new = decay[..., None, None] * C + jnp.einsum('bhsd,bhse->bhde', kw, vt)
        n_new = decay[..., None] * n + jnp.sum(kw, axis=2)
        return (C_new, n_new, m_new), h

    init = (jnp.zeros((B, H, dk, dv), jnp.float32), jnp.zeros((B, H, dk), jnp.float32),
            jnp.zeros((B, H), jnp.float32))
    _, hs = lax.scan(step, init, (chunk(q), chunk(k), chunk(v), chunk(log_i), chunk(log_f)))
    return jnp.moveaxis(hs, 0, 2).reshape(B, H, Lp, dv)


def mlstm_bidirectional(q, k, v, gates):
    n_pad = M_CHUNK - N_META
    pad_t = lambda a: jnp.pad(a, ((0, 0), (0, 0), (n_pad, 0), (0, 0)))
    qp, kp, vp = pad_t(q), pad_t(k), pad_t(v)
    g = jnp.transpose(gates, (0, 2, 3, 1))
    pad_g = lambda a, c: jnp.pad(a, ((0, 0), (0, 0), (n_pad, 0)), constant_values=c)
    li_f = pad_g(g[:, 0], NEG_LOG_GATE)
    lf_f = pad_g(jax.nn.log_sigmoid(g[:, 1]), 0.0)
    li_b = pad_g(g[:, 2], NEG_LOG_GATE)
    lf_b = pad_g(jax.nn.log_sigmoid(g[:, 3]), 0.0)
    h_f = mlstm_chunkwise(qp, kp, vp, li_f, lf_f)
    ft = lambda a: jnp.flip(a, axis=2)
    h_b = ft(mlstm_chunkwise(ft(qp), ft(kp), ft(vp), jnp.flip(li_b, -1), jnp.flip(lf_b, -1)))
    return (h_f + h_b)[:, :, n_pad:]


def neighbourhood_attention(q, k, v, rpb, meta_bias, rows):
    B, H, _, dh = q.shape
    qm, km, vm = q[:, :, :N_META], k[:, :, :N_META], v[:, :, :N_META]
    qr, kr, vr = q[:, :, N_META:], k[:, :, N_META:], v[:, :, N_META:]
    wh = min(NA_WIN_H_MAX, rows)
    n_seg = GRID_W // NA_SEG_W
    nk = wh * NA_REGION_W
    seg = jnp.arange(n_seg)
    qcols = seg[:, None] * NA_SEG_W + jnp.arange(NA_SEG_W)[None, :]
    reg0 = jnp.clip(seg * NA_SEG_W - NA_WIN_W // 2, 0, GRID_W - NA_REGION_W)
    kcols = reg0[:, None] + jnp.arange(NA_REGION_W)[None, :]
    win0 = jnp.clip(qcols - NA_WIN_W // 2, 0, GRID_W - NA_WIN_W)
    col_ok = (kcols[:, None, :] >= win0[..., None]) & (kcols[:, None, :] < win0[..., None] + NA_WIN_W)
    mask = jnp.broadcast_to(col_ok[:, :, None, :], (n_seg, NA_SEG_W, wh, NA_REGION_W)).reshape(n_seg, NA_SEG_W, nk)
    dc = jnp.clip(kcols[:, None, :] - qcols[..., None], -(NA_WIN_W - 1), NA_WIN_W - 1) + NA_WIN_W - 1
    mb = meta_bias.astype(jnp.float32)

    def row_block(r):
        r0 = jnp.clip(r - wh // 2, 0, rows - wh)
        krows = r0 + jnp.arange(wh)
        dr = krows - r + NA_WIN_H_MAX - 1
        bias = rpb[:, dr[None, None, :, None], dc[:, :, None, :]].reshape(H, n_seg, NA_SEG_W, nk)
        kidx = (krows[None, :, None] * GRID_W + kcols[:, None, :]).reshape(-1)
        qidx = (r * GRID_W + qcols).reshape(-1)
        qb = jnp.take(qr, qidx, axis=2).reshape(B, H, n_seg, NA_SEG_W, dh)
        kb = jnp.take(kr, kidx, axis=2).reshape(B, H, n_seg, nk, dh)
        vb = jnp.take(vr, kidx, axis=2).reshape(B, H, n_seg, nk, dh)
        s_loc = jnp.einsum('bhnqd,bhnkd->bhnqk', qb, kb).astype(jnp.float32) + bias.astype(jnp.float32)
        s_loc = jnp.where(mask, s_loc, -jnp.inf)
        s_met = jnp.einsum('bhnqd,bhmd->bhnqm', qb, km).astype(jnp.float32) + mb[:, None, None, :]
        p = jax.nn.softmax(jnp.concatenate([s_loc, s_met], axis=-1), axis=-1).astype(v.dtype)
        return (jnp.einsum('bhnqk,bhnkd->bhnqd', p[..., :nk], vb)
                + jnp.einsum('bhnqm,bhmd->bhnqd', p[..., nk:], vm))

    o_r = lax.map(row_block, jnp.arange(rows))
    o_r = jnp.moveaxis(o_r, 0, 2).reshape(B, H, rows * GRID_W, dh)
    s_mm = jnp.einsum('bhqd,bhmd->bhqm', qm, km).astype(jnp.float32) + mb[:, None, :]
    o_m = jnp.einsum('bhqm,bhmd->bhqd', jax.nn.softmax(s_mm, axis=-1).astype(v.dtype), vm)
    return jnp.concatenate([o_m, o_r], axis=2)


def setup_inputs(seed: int = 0) -> dict:
    key = jax.random.key(seed)
    ks = jax.random.split(key, 24)
    nrm = lambda kk, shape, scale: jax.random.normal(kk, shape, jnp.float32) * scale
    f_bias = jnp.linspace(3.0, 6.0, M_HEADS, dtype=jnp.float32)
    gate_b = jnp.stack([nrm(ks[6], (M_HEADS,), 0.1),
                        f_bias + nrm(ks[7], (M_HEADS,), 0.1),
                        nrm(ks[8], (M_HEADS,), 0.1),
                        f_bias + nrm(ks[9], (M_HEADS,), 0.1)])
    return {
        'x': nrm(ks[0], (BATCH, SEQ, D_MODEL), 1.0),
        'meta_tokens': nrm(ks[1], (N_META, D_MODEL), 1.0),
        'norm1_g': 1.0 + nrm(ks[2], (D_MODEL,), 0.02),
        'w_in': nrm(ks[3], (D_MODEL, D_IN_PROJ), D_MODEL ** -0.5),
        'mlstm_conv_w': nrm(ks[4], (M_CONV, 1, M_INNER), M_CONV ** -0.5),
        'mlstm_conv_b': nrm(ks[5], (M_INNER,), 0.02),
        'mlstm_wq': nrm(ks[10], (M_HEADS, M_DV, M_DK), M_DV ** -0.5),
        'mlstm_wk': nrm(ks[11], (M_HEADS, M_DV, M_DK), M_DV ** -0.5),
        'mlstm_gate_b': gate_b,
        'mlstm_norm_g': 1.0 + nrm(ks[12], (M_HEADS, M_DV), 0.02),
        'mlstm_skip': 1.0 + nrm(ks[13], (M_INNER,), 0.02),
        'na_q_norm_g': 1.0 + nrm(ks[14], (NA_DH,), 0.02),
        'na_k_norm_g': 1.0 + nrm(ks[15], (NA_DH,), 0.02),
        'na_rpb': nrm(ks[16], (NA_HEADS, 2 * NA_WIN_H_MAX - 1, 2 * NA_WIN_W - 1), 0.02),
        'na_meta_bias': nrm(ks[17], (NA_HEADS, N_META), 0.02),
        'w_branch_a': nrm(ks[18], (M_INNER, D_MODEL), M_INNER ** -0.5),
        'w_branch_b': nrm(ks[19], (NA_INNER, D_MODEL), NA_INNER ** -0.5),
        'w_out': nrm(ks[20], (D_MODEL, D_MODEL), D_MODEL ** -0.5),
        'norm2_g': 1.0 + nrm(ks[21], (D_MODEL,), 0.02),
        'w_ff1': nrm(ks[22], (D_MODEL, D_FF), D_MODEL ** -0.5),
        'w_ff2': nrm(ks[23], (D_FF, D_MODEL), D_FF ** -0.5),
    }


def reference(x, meta_tokens, norm1_g, w_in, mlstm_conv_w, mlstm_conv_b, mlstm_wq, mlstm_wk,
              mlstm_gate_b, mlstm_norm_g, mlstm_skip, na_q_norm_g, na_k_norm_g, na_rpb, na_meta_bias,
              w_branch_a, w_branch_b, w_out, norm2_g, w_ff1, w_ff2):
    B, S, D = x.shape
    rows = S // GRID_W
    L = N_META + S
    h = jnp.concatenate([jnp.broadcast_to(meta_tokens[None].astype(x.dtype), (B, N_META, D)), x], axis=1)
    for _ in range(DEPTH):
        xn = rms_norm(h, norm1_g)
        u = xn @ w_in
        xm = u[..., OFF_MX:OFF_MO]
        xc = jax.nn.silu(centred_depthwise_conv(xm, mlstm_conv_w) + mlstm_conv_b)
        xch = xc.reshape(B, L, M_HEADS, M_DV)
        q_m = jnp.einsum('blhe,hed->bhld', xch, mlstm_wq).astype(jnp.float32)
        k_m = jnp.einsum('blhe,hed->bhld', xch, mlstm_wk).astype(jnp.float32) * (M_DK ** -0.5)
        v_m = xm.reshape(B, L, M_HEADS, M_DV).transpose(0, 2, 1, 3).astype(jnp.float32)
        gates = (u[..., OFF_MG:OFF_Q].reshape(B, L, 4, M_HEADS) + mlstm_gate_b).astype(jnp.float32)
        h_m = mlstm_bidirectional(q_m, k_m, v_m, gates)
        h_m = h_m * lax.rsqrt(jnp.mean(jnp.square(h_m), axis=-1, keepdims=True) + EPS)
        h_m = h_m * mlstm_norm_g.astype(jnp.float32)[None, :, None, :]
        h_m = h_m.transpose(0, 2, 1, 3).reshape(B, L, M_INNER).astype(x.dtype)
        y_a = jax.nn.sigmoid(u[..., OFF_MO:OFF_MG]) * (h_m + mlstm_skip * xc)
        q_n = rms_norm(u[..., OFF_Q:OFF_K].reshape(B, L, NA_HEADS, NA_DH), na_q_norm_g).transpose(0, 2, 1, 3) * (NA_DH ** -0.5)
        k_n = rms_norm(u[..., OFF_K:OFF_V].reshape(B, L, NA_HEADS, NA_DH), na_k_norm_g).transpose(0, 2, 1, 3)
        v_n = u[..., OFF_V:OFF_G].reshape(B, L, NA_HEADS, NA_DH).transpose(0, 2, 1, 3)
        o_n = neighbourhood_attention(q_n, k_n, v_n, na_rpb, na_meta_bias, rows)
        y_b = o_n.transpose(0, 2, 1, 3).reshape(B, L, NA_INNER)
        g_a = jax.nn.sigmoid(u[..., OFF_G:OFF_G + D_MODEL])
        g_b = jax.nn.sigmoid(u[..., OFF_G + D_MODEL:OFF_G + 2 * D_MODEL])
        mix = g_a * (y_a @ w_branch_a) + g_b * (y_b @ w_branch_b)
        h = h + mix @ w_out
        z = rms_norm(h, norm2_g) @ w_ff1
        h = h + jnp.square(jax.nn.relu(z)) @ w_ff2
    return h[:, N_META:]
```

```python
import contextlib
import numpy as np
import concourse.bass as bass
import concourse.mybir as mybir
from concourse.bass_utils import run_bass_kernel_spmd

F32 = mybir.dt.float32
BF16 = mybir.dt.bfloat16
AF = mybir.ActivationFunctionType
ALU = mybir.AluOpType
DSZ = {F32: 4, BF16: 2}

D = 1024
NOWN = 2048
NT = 16
OWN0 = 17
OTH0 = 17 + 2048
POST0 = 17 + 4096
XE_ROWS = 4130
NXC = 17 + 2048 + 256
H_M, DV, DK = 4, 256, 128
NH, DH = 8, 64
EPS = 1e-6
NEG = -30000.0
N_DMA_SEMS = 12
SAME_ENGINE_SYNC = True


class Sched:
    def __init__(self, nc):
        self.nc = nc
        self.engs = ('pe', 'act', 'dve', 'pool', 'sp')
        self.prog = {k: [] for k in self.engs}
        self.count = {k: 0 for k in self.engs}
        self.waited = {k: {} for k in self.engs}
        self.recs = {}
        self.dma_q = ('sp', 'pool', 'act')
        self.dma_uses = {q: [0] * N_DMA_SEMS for q in self.dma_q}
        self.dma_rr = {q: 0 for q in self.dma_q}
        self.n_inst = 0
        self.dram = set()

    def _box(self, ap):
        name = ap.tensor.name
        if name in self.dram:
            return None
        if name.startswith('ps'):
            return name, 0, 128, 0, 2048
        dims = ap.ap
        sz = DSZ[ap.dtype]
        shp = ap.tensor.shape
        row = 1
        for s in list(shp)[1:]:
            row *= int(s)
        off = int(ap.offset)
        p0 = off // row
        b0 = (off % row) * sz
        pc = int(dims[0][1])
        ext = 0
        for st, cn in dims[1:]:
            ext += (int(cn) - 1) * abs(int(st))
        b1 = b0 + (ext + 1) * sz
        return name, p0, p0 + pc, b0, b1

    def _deps(self, eng, ins, outs):
        deps = {}

        def add(sk, v):
            if deps.get(sk, 0) < v:
                deps[sk] = v
        rb = [b for b in (self._box(a) for a in ins) if b is not None]
        wb = [b for b in (self._box(a) for a in outs) if b is not None]
        wb = wb + [b for b in rb if b[0].startswith('ps') and b not in wb]
        rb = [b for b in rb if not b[0].startswith('ps')]
        for (name, p0, p1, b0, b1) in rb:
            for r in self.recs.get(name, ()):
                if r[0] < p1 and p0 < r[1] and r[2] < b1 and b0 < r[3]:
                    if r[4] is not None:
                        add(*r[4])
        for (name, p0, p1, b0, b1) in wb:
            for r in self.recs.get(name, ()):
                if r[0] < p1 and p0 < r[1] and r[2] < b1 and b0 < r[3]:
                    if r[4] is not None:
                        add(*r[4])
                    for sk, v in r[5].items():
                        add(sk, v)
        waits = []
        for sk, v in deps.items():
            if sk == eng and (eng == 'pe' or not SAME_ENGINE_SYNC):
                continue
            if self.waited[eng].get(sk, 0) >= v:
                continue
            self.waited[eng][sk] = v
            waits.append((sk, v))
        return waits, rb, wb

    def _commit(self, tok, rb, wb):
        for (name, p0, p1, b0, b1) in wb:
            lst = self.recs.setdefault(name, [])
            keep = []
            for r in lst:
                if r[0] >= p0 and r[1] <= p1 and r[2] >= b0 and r[3] <= b1:
                    continue
                keep.append(r)
            keep.append([p0, p1, b0, b1, tok, {}])
            self.recs[name] = keep
        for (name, p0, p1, b0, b1) in rb:
            lst = self.recs.setdefault(name, [])
            best = None
            for r in lst:
                if r[0] <= p0 and r[1] >= p1 and r[2] <= b0 and r[3] >= b1 and r[4] != tok:
                    sz = (r[1] - r[0]) * (r[3] - r[2])
                    if best is None or sz < best[0]:
                        best = (sz, r)
            if best is not None:
                r = best[1]
                if r[5].get(tok[0], 0) < tok[1]:
                    r[5][tok[0]] = tok[1]
                continue
            for r in lst:
                if r[0] < p1 and p0 < r[1] and r[2] < b1 and b0 < r[3]:
                    if r[4] == tok:
                        continue
                    if r[5].get(tok[0], 0) < tok[1]:
                        r[5][tok[0]] = tok[1]
            lst.append([p0, p1, b0, b1, None, {tok[0]: tok[1]}])

    def op(self, eng, fn, ins, outs):
        waits, rb, wb = self._deps(eng, ins, outs)
        self.count[eng] += 1
        tok = (eng, self.count[eng])
        self.prog[eng].append((waits, fn, tok))
        self._commit(tok, rb, wb)
        self.n_inst += 1
        return tok

    def dma(self, eng, out, in_):
        waits, rb, wb = self._deps(eng, [in_], [out])
        i = self.dma_rr[eng]
        self.dma_rr[eng] = (i + 1) % N_DMA_SEMS
        self.dma_uses[eng][i] += 1
        sk = ('dma', eng, i)
        prev = 16 * (self.dma_uses[eng][i] - 1)
        if prev > 0 and self.waited[eng].get(sk, 0) < prev:
            self.waited[eng][sk] = prev
            waits.append((sk, prev))
        tok = (sk, 16 * self.dma_uses[eng][i])
        self.prog[eng].append((waits, (lambda e, o=out, a=in_: e.dma_start(out=o, in_=a)), tok))
        self._commit(tok, rb, wb)
        self.n_inst += 1
        return tok

    def wait_tokens(self, eng, toks):
        waits = []
        for sk, v in toks:
            if self.waited[eng].get(sk, 0) >= v:
                continue
            self.waited[eng][sk] = v
            waits.append((sk, v))
        if waits:
            self.prog[eng].append((waits, None, None))

    def mm(self, out, lhsT, rhs, start=True, stop=True, sgc=False):
        if sgc:
            return self.op('pe', lambda e: e.matmul(out, lhsT=lhsT, rhs=rhs, start=start, stop=stop,
                                                    skip_group_check=True), [lhsT, rhs], [out])
        return self.op('pe', lambda e: e.matmul(out, lhsT=lhsT, rhs=rhs, start=start, stop=stop),
                       [lhsT, rhs], [out])

    def transpose(self, out, in_, ident):
        return self.op('pe', lambda e: e.transpose(out=out, in_=in_, identity=ident), [in_, ident], [out])

    def act(self, out, in_, func, bias=None, scale=None, accum=None):
        kw = {}
        ins = [in_]
        outs = [out]
        if bias is not None:
            kw['bias'] = bias
            if not isinstance(bias, (int, float)):
                ins.append(bias)
        if scale is not None:
            kw['scale'] = scale
            if not isinstance(scale, (int, float)):
                ins.append(scale)
        if accum is not None:
            kw['accum_out'] = accum
            outs.append(accum)
        return self.op('act', lambda e: e.activation(out=out, in_=in_, func=func, **kw), ins, outs)

    def ts(self, eng, out, in0, s1, s2, op0, op1=None):
        ins = [in0] + [s for s in (s1, s2) if s is not None and not isinstance(s, (int, float))]
        if op1 is None:
            fn = lambda e: e.tensor_scalar(out=out, in0=in0, scalar1=s1, scalar2=None, op0=op0)
        else:
            fn = lambda e: e.tensor_scalar(out=out, in0=in0, scalar1=s1, scalar2=s2, op0=op0, op1=op1)
        return self.op(eng, fn, ins, [out])

    def tt(self, eng, out, in0, in1, op):
        return self.op(eng, lambda e: e.tensor_tensor(out=out, in0=in0, in1=in1, op=op), [in0, in1], [out])

    def stt(self, out, in0, scalar, in1, op0, op1):
        ins = [in0, in1] + ([] if isinstance(scalar, (int, float)) else [scalar])
        return self.op('dve', lambda e: e.scalar_tensor_tensor(out=out, in0=in0, scalar=scalar, in1=in1,
                                                               op0=op0, op1=op1), ins, [out])

    def copy(self, eng, out, in_):
        if eng == 'act':
            return self.op('act', lambda e: e.copy(out=out, in_=in_), [in_], [out])
        return self.op(eng, lambda e: e.tensor_copy(out=out, in_=in_), [in_], [out])

    def recip(self, out, in_):
        return self.op('dve', lambda e: e.reciprocal(out=out, in_=in_), [in_], [out])

    def memset(self, eng, ap, val):
        return self.op(eng, lambda e: e.memset(ap, val), [], [ap])

    def emit(self):
        nc = self.nc
        engmap = {'pe': nc.tensor, 'act': nc.scalar, 'dve': nc.vector, 'pool': nc.gpsimd, 'sp': nc.sync}
        with contextlib.ExitStack() as st:
            sems = {}
            for e in self.engs:
                sems[e] = st.enter_context(nc.semaphore("sem_" + e))
            for q in self.dma_q:
                for i in range(N_DMA_SEMS):
                    sems[('dma', q, i)] = st.enter_context(nc.semaphore("sem_dma_%s%d" % (q, i)))
            block = st.enter_context(nc.Block())

            def replay(ename, e):
                for waits, fn, tok in self.prog[ename]:
                    for sk, v in waits:
                        e.wait_ge(sems[sk], v)
                    if fn is None:
                        continue
                    inst = fn(e)
                    if isinstance(tok[0], tuple):
                        inst.then_inc(sems[tok[0]], 16)
                    else:
                        inst.then_inc(sems[tok[0]], 1)

            block.tensor(lambda e: replay('pe', e))
            block.scalar(lambda e: replay('act', e))
            block.vector(lambda e: replay('dve', e))
            block.gpsimd(lambda e: replay('pool', e))
            block.sync(lambda e: replay('sp', e))


class Arena:
    def __init__(self, t, nbytes):
        self.t = t
        self.nbytes = nbytes
        self.off = 0
        self.stack = []

    def push(self):
        self.stack.append(self.off)

    def pop(self):
        self.off = self.stack.pop()

    def alloc(self, shape, dtype):
        n = 1
        for s in shape:
            n *= s
        nb = n * DSZ[dtype]
        nb = (nb + 63) // 64 * 64
        assert self.off + nb <= self.nbytes, ("arena overflow", self.off, nb, self.nbytes)
        a = self.t[:, self.off // 4:(self.off + nb) // 4]
        self.off += nb
        if dtype != F32:
            a = a.bitcast(dtype)
        a = a[:, 0:n]
        if len(shape) == 2:
            a = a.rearrange("p (a b) -> p a b", a=shape[0])
        elif len(shape) == 3:
            a = a.rearrange("p (a b c) -> p a b c", a=shape[0], b=shape[1])
        elif len(shape) == 4:
            a = a.rearrange("p (a b c d) -> p a b c d", a=shape[0], b=shape[1], c=shape[2])
        return a


def key_tiles(i):
    if i == 0:
        return [(j, j) for j in range(4)]
    if i == 1:
        return [(j, 4 + j) for j in range(4)]
    return [(i + d, 8 + d + 2) for d in range(-2, 3)]


def build_nc(debug=None, stop_after=None):
    nc = bass.Bass("TRN2", target_bir_lowering=False)
    S = Sched(nc)
    dbg_outs = {}

    def din(name, shape):
        t = nc.dram_tensor(name, list(shape), F32, kind="ExternalInput")
        S.dram.add(name)
        return t.ap()

    xe = din("xe", [XE_ROWS, D])
    d_vec = din("vecs", [128, 64])
    d_valid = din("valid", [128, 2])
    d_gateb = din("gate_b", [128, 16])
    d_cmask = din("cmask", [128, 6, 128])
    d_mb = din("mb", [32, 8])
    d_bt = din("bt", [4, 128, 13 * 2 * 128])
    d_wxm = din("wxm", [128, 8 * 1024])
    d_wo = din("wo", [8, 128, 8 * 128])
    d_wg = din("wg", [128, 8 * 16])
    d_wq = din("wqm", [128, 4 * 2 * 128])
    d_wk = din("wkm", [128, 4 * 2 * 128])
    d_wnq = din("wnq", [4, 128, 8 * 128])
    d_wnk = din("wnk", [4, 128, 8 * 128])
    d_wnv = din("wnv", [128, 8 * 512])
    d_wga = din("wga", [8, 128, 8 * 128])
    d_wgb = din("wgb", [8, 128, 8 * 128])
    d_wa = din("wa", [8, 128, 8 * 128])
    d_wb = din("wb", [8, 128, 4 * 128])
    d_wout = din("wout", [128, 8 * 1024])
    d_wf1 = din("wf1", [32, 128, 8 * 128])
    d_wf2 = din("wf2", [32, 128, 1024])
    y_out = nc.dram_tensor("y", [NOWN, D], F32, kind="ExternalOutput")
    S.dram.add("y")
    y_out = y_out.ap()

    with contextlib.ExitStack() as st:
        ARENA_BYTES = 207 * 1024
        arena_t = st.enter_context(nc.sbuf_tensor("arena", [128, ARENA_BYTES // 4], F32))
        AR = Arena(arena_t, ARENA_BYTES)
        ps = [st.enter_context(nc.psum_tensor("ps%d" % i, [128, 512], F32)) for i in range(8)]
        psb = [p[:].bitcast(BF16) for p in ps]

        def dbg(name, ap, shape):
            if debug is None or name not in debug:
                return
            t = nc.dram_tensor("dbg_" + name, list(shape), F32, kind="ExternalOutput")
            S.dram.add("dbg_" + name)
            dbg_outs[name] = S.dma('pool', t.ap(), ap)

        vec = AR.alloc([64], F32)
        valid = AR.alloc([2], F32)
        gateb = AR.alloc([16], F32)
        cmf = AR.alloc([4, 128], F32)
        cmb = AR.alloc([4, 128], BF16)
        mbc = AR.alloc([8], F32)
        S.dma('sp', vec, d_vec)
        S.dma('sp', valid, d_valid)
        S.dma('sp', gateb, d_gateb)
        S.dma('sp', cmf, d_cmask[:, 0:4, :])
        S.dma('pool', cmb[:, 0:2, :], d_cmask[:, 0:2, :])
        S.dma('pool', cmb[:, 2:4, :], d_cmask[:, 4:6, :])
        S.dma('sp', mbc[0:32, :], d_mb)
        triA_f, triB_f = cmf[:, 0, :], cmf[:, 1, :]
        ones_f = [cmf[:, 2, :], cmf[:, 3, :]]
        maskA, maskB, ident, blk64 = cmb[:, 0, :], cmb[:, 1, :], cmb[:, 2, :], cmb[:, 3, :]
        g1 = vec[:, 0:8]
        cw = [vec[:, 8:16], vec[:, 16:24], vec[:, 24:32]]
        cb = vec[:, 32:40]
        mng = vec[:, 40:48]
        skp = vec[:, 48:56]
        g2 = vec[:, 56:64]
        qkg = AR.alloc([2], F32)
        d_qkg = din("qkg", [128, 2])
        S.dma('sp', qkg, d_qkg)

        dcw = AR.alloc([3, 8, 128], BF16)
        for k3 in range(3):
            for et in range(8):
                S.ts('dve', dcw[:, k3, et, :], ident, cw[k3][:, et:et + 1], None, ALU.mult)
        U_A = AR.alloc([4, 257], F32)
        U_B = AR.alloc([4, 257], F32)
        dprevA = AR.alloc([4], F32)
        dprevB = AR.alloc([4], F32)
        OFF_XNT = AR.off
        xnT = AR.alloc([8, NXC + 1], BF16)
        S.memset('dve', xnT[:, :, 0:1], 0.0)
        xnTm = AR.alloc([8, 32], BF16)
        MIX_OFF = ARENA_BYTES - 8 * NOWN * 2

        wg = AR.alloc([8, 16], BF16)
        wq = AR.alloc([4, 2, 128], BF16)
        wk = AR.alloc([4, 2, 128], BF16)
        S.dma('pool', wg, d_wg.rearrange("p (a b) -> p a b", a=8))
        S.dma('pool', wq, d_wq.rearrange("p (a b c) -> p a b c", a=4, b=2))
        S.dma('pool', wk, d_wk.rearrange("p (a b c) -> p a b c", a=4, b=2))
        AR.push()
        xt_buf = [AR.alloc([D], F32) for _ in range(4)]
        xnb_buf = [AR.alloc([D], BF16) for _ in range(4)]
        xt_buf2 = [AR.alloc([D], F32) for _ in range(4)]
        xnb_buf2 = [AR.alloc([D], BF16) for _ in range(4)]
        ss_buf2 = AR.alloc([8], F32)
        sq_junk = AR.alloc([D], BF16)
        ss_buf = AR.alloc([8], F32)
        xcnt = [0]

        def emit_xnT_batch(items, gvec, xts=None, xnbs=None, ssb=None):
            nb = len(items)
            assert nb <= 4
            xts = xts or xt_buf
            xnbs = xnbs or xnb_buf
            ssb = ssb if ssb is not None else ss_buf
            nmax = max(n for _, n, _ in items)
            for i, (row0, n, dst) in enumerate(items):
                S.dma('sp', xts[i][0:n, :], xe[row0:row0 + n, :])
            for i, (row0, n, dst) in enumerate(items):
                S.act(sq_junk[0:n, :], xts[i][0:n, :], AF.Square, accum=ssb[0:n, i:i + 1])
            rs = ssb[0:nmax, 4:4 + nb]
            S.ts('dve', rs, ssb[0:nmax, 0:nb], 1.0 / D, EPS, ALU.mult, ALU.add)
            S.act(rs, rs, AF.Sqrt)
            S.recip(rs, rs)
            for i, (row0, n, dst) in enumerate(items):
                S.ts('dve', xnbs[i][0:n, :], xts[i][0:n, :], ssb[0:n, 4 + i:5 + i], None, ALU.mult)
            for i, (row0, n, dst) in enumerate(items):
                pb = psb[6 + (xcnt[0] % 2)]
                xcnt[0] += 1
                for dt in range(8):
                    S.transpose(pb[:, dt * 128:dt * 128 + n], xnbs[i][0:n, dt * 128:(dt + 1) * 128],
                                ident[0:n, 0:n])
                pv = pb[:, :].rearrange("p (a b) -> p a b", a=8)[:, :, 0:n]
                S.tt('dve', dst, pv, gvec.unsqueeze(2).to_broadcast([128, 8, n]), ALU.mult)

        xt_h = AR.alloc([D], F32)
        xnb_h = AR.alloc([D], BF16)
        ss_m = AR.alloc([8], F32)

        def emit_xnT(row0, n, dst, gvec):
            emit_xnT_batch([(row0, n, dst)], gvec, xts=[xt_h], xnbs=[xnb_h], ssb=ss_m)

        def emit_h1_xn2T(tile_rows, h1, dst):
            pass

        if stop_after == 'consts':
            dbg("vec", vec, [128, 64])
            return finish(nc, S, y_out, dbg_outs, None)
        if stop_after == 'x16':
            emit_xnT(1, 16, xnT[:, :, 1:17], g1)
            dbg("xnT", xnT[:, 0, 0:NXC], [128, NXC])
            return finish(nc, S, y_out, dbg_outs, None)
        if stop_after and stop_after.startswith('xn'):
            for k in range(int(stop_after[2:])):
                emit_xnT(OWN0 + 128 * k, 128, xnT[:, :, OWN0 + 128 * k:OWN0 + 128 * k + 128], g1)
            dbg("xnT", xnT[:, 0, 0:NXC], [128, NXC])
            return finish(nc, S, y_out, dbg_outs, None)
        if stop_after == 'x1':
            emit_xnT(OWN0, 128, xnT[:, :, OWN0:OWN0 + 128], g1)
            dbg("xnT", xnT[:, 0, 0:NXC], [128, NXC])
            return finish(nc, S, y_out, dbg_outs, None)
        emit_xnT_batch([(1, 16, xnT[:, :, 1:17]), (1, 16, xnTm[:, :, 0:16]), (POST0, 16, xnTm[:, :, 16:32])], g1)
        def p0gen():
            for k0 in range(0, NT + 2, 4):
                alt = (k0 // 4) % 2 == 1
                emit_xnT_batch([(OWN0 + 128 * k, 128, xnT[:, :, OWN0 + 128 * k:OWN0 + 128 * k + 128])
                                for k in range(k0, min(k0 + 4, NT + 2))], g1,
                               xts=xt_buf2 if alt else None, xnbs=xnb_buf2 if alt else None,
                               ssb=ss_buf2 if alt else None)
                yield

        AR.push()
        wxm = AR.alloc([8, 1024], BF16)
        for q4 in range(4):
            S.dma('pool', wxm[:, 2 * q4:2 * q4 + 2, :],
                  d_wxm.rearrange("p (a b) -> p a b", a=8)[:, 2 * q4:2 * q4 + 2, :])
        p1bufs = []
        for _ in range(2):
            p1bufs.append(dict(
                xoT=AR.alloc([8, 514], BF16), xmT=AR.alloc([8, 514], BF16), xcT=AR.alloc([8, 512], BF16),
                ktok1=AR.alloc([4, 512], BF16), vp1=AR.alloc([4, 4, 257], BF16),
                gsb=AR.alloc([4, 16], F32), sp1=AR.alloc([4, 4], F32), w1=AR.alloc([4, 4], F32),
                d1=AR.alloc([2, 4, 4], F32), tmp16=AR.alloc([4, 4], F32)))
        p1cnt = [0]
        carry = [None]

        def seq_block(direction, row0, ntile, npp, ncol, is_mini, vflag, first):
            ntok = ntile * npp
            B_ = p1bufs[p1cnt[0] % 2]
            p1cnt[0] += 1
            xoT, xmT, xcT, ktok1, vp1 = B_['xoT'], B_['xmT'], B_['xcT'], B_['ktok1'], B_['vp1']
            gsb, sp1, w1, d1, tmp16 = B_['gsb'], B_['sp1'], B_['w1'], B_['d1'], B_['tmp16']
            if direction == 'A':
                U, dprev, tri, li0, f0 = U_A, dprevA, triA_f, 0, 4
            else:
                U, dprev, tri, li0, f0 = U_B, dprevB, triB_f, 8, 12
            if is_mini:
                emit_xnT(row0, ncol, xoT[:, :, 0:ncol], g1)
            else:
                emit_xnT_batch([(row0 + 1 + 128 * j, 128, xoT[:, :, 1 + 128 * j:1 + 128 * j + 128])
                                for j in range(ntile)], g1)
                emit_xnT(row0, 1, xoT[:, :, 0:1], g1)
                S.copy('dve', xmT[:, :, 512:514], carry[0])
            carry[0] = xmT[:, :, 0:2]
            nmain = min(512, ncol)
            yield
            pg = ps[2]
            for j in range(ntile):
                for dt in range(8):
                    S.mm(pg[0:npp, 64 + 16 * j:64 + 16 * j + 16], xoT[:, dt, 1 + j * npp:1 + (j + 1) * npp],
                         wg[:, dt, :], start=(dt == 0), stop=(dt == 7))
            pgv = pg[0:npp, 64:64 + 16 * ntile].rearrange("p (a b) -> p a b", a=ntile)
            S.tt('dve', gsb[0:npp, 0:ntile, :], pgv, gateb[0:npp, :].unsqueeze(1).to_broadcast([npp, ntile, 16]),
                 ALU.add)
            spv = sp1[0:npp, 0:ntile, :]
            S.act(spv, gsb[0:npp, 0:ntile, f0:f0 + 4], AF.Exp, scale=-1.0)
            S.act(spv, spv, AF.Ln, bias=1.0, scale=1.0)
            pc = ps[3]
            S.mm(pc[0:npp, 128:128 + 4 * ntile], tri[0:npp, 0:npp], spv)
            nhb = 1 if is_mini else 2
            for hb in range(nhb):
                lh = ones_f[hb][0:npp, :] if not is_mini else ones_f[0][0:npp, :]
                S.mm(pc[:, 192 + 16 * hb:192 + 16 * hb + 4 * ntile], lh, spv)
            csv = pc[0:npp, 128:128 + 4 * ntile].rearrange("p (a b) -> p a b", a=ntile)
            S.tt('dve', tmp16[0:npp, 0:ntile, :], gsb[0:npp, 0:ntile, li0:li0 + 4], csv, ALU.add)
            S.act(w1[0:npp, 0:ntile, :], tmp16[0:npp, 0:ntile, :], AF.Exp)
            if vflag is not None:
                S.ts('dve', w1[0:npp, 0:ntile, :], w1[0:npp, 0:ntile, :], vflag[0:npp, :], None, ALU.mult)
            for hb in range(nhb):
                S.act(d1[:, hb, 0:ntile, :],
                      pc[:, 192 + 16 * hb:192 + 16 * hb + 4 * ntile].rearrange("p (a b) -> p a b", a=ntile),
                      AF.Exp, scale=-1.0)
            yield
            for et in range(8):
                pm = ps[et % 4]
                for dt in range(8):
                    S.mm(pm[:, 0:nmain], wxm[:, dt, et * 128:(et + 1) * 128], xoT[:, dt, 0:nmain],
                         start=(dt == 0), stop=(dt == 7))
                S.copy('act', xmT[:, et, 0:nmain], pm[:, 0:nmain])
            for j in range(ntile):
                for half in range(2):
                    pv_ = ps[4 + half]
                    for dt in range(8):
                        S.mm(pv_[0:npp, :], xoT[:, dt, 1 + j * npp:1 + (j + 1) * npp],
                             wxm[:, dt, 512 * half:512 * half + 512], start=(dt == 0), stop=(dt == 7))
                    S.tt('dve', vp1[0:npp, j, 2 * half:2 * half + 2, 0:256],
                         pv_[0:npp, :].rearrange("p (a b) -> p a b", a=2),
                         w1[0:npp, j, 2 * half:2 * half + 2].unsqueeze(2).to_broadcast([npp, 2, 256]), ALU.mult)
            S.copy('dve', vp1[0:npp, 0:ntile, :, 256], w1[0:npp, 0:ntile, :])
            yield
            for et in range(8):
                pcv = ps[6 + et % 2]
                for k3 in range(3):
                    S.mm(pcv[:, 0:ntok], dcw[:, k3, et, :], xmT[:, et, k3:k3 + ntok], start=(k3 == 0), stop=(k3 == 2))
                S.act(xcT[:, et, 0:ntok], pcv[:, 0:ntok], AF.Silu, bias=cb[:, et:et + 1], scale=1.0)
            yield
            for j in range(ntile):
                pk = ps[3]
                for h in range(4):
                    for e2 in range(2):
                        S.mm(pk[0:npp, h * 128:(h + 1) * 128], xcT[:, 2 * h + e2, j * npp:(j + 1) * npp],
                             wk[:, h, e2, :], start=(e2 == 0), stop=(e2 == 1))
                S.act(ktok1[0:npp, j, :], pk[0:npp, :], AF.Copy, scale=DK ** -0.5)
            yield
            order = []
            for j in range(ntile):
                for hb in range(nhb):
                    order.append((j, hb))
            if direction == 'B':
                order = order[::-1]
            for (j, hb) in order:
                p0 = 64 * hb
                cn = npp if is_mini else 64
                for h in range(4):
                    pst = ps[h]
                    S.mm(pst[:, 0:257], ktok1[p0:p0 + cn, j, h * 128:(h + 1) * 128], vp1[p0:p0 + cn, j, h, :])
                    if first:
                        S.copy('dve', U[:, h, :], pst[:, 0:257])
                    else:
                        S.stt(U[:, h, :], U[:, h, :], dprev[:, h:h + 1], pst[:, 0:257], ALU.mult, ALU.add)
                S.copy('dve', dprev, d1[:, hb, j, :])
                first = False

        gens = [seq_block('A', 0, 1, 16, 18, True, valid[:, 0:1], True),
                seq_block('B', POST0 - 1, 1, 16, 18, True, valid[:, 1:2], True), p0gen()]
        while gens:
            for g_ in list(gens):
                try:
                    next(g_)
                except StopIteration:
                    gens.remove(g_)
        dbg("xnT", xnT[:, 0, 0:NXC], [128, NXC])
        if stop_after == 'phase0':
            return finish(nc, S, y_out, dbg_outs, None)

        def big_bufs(bk):
            return p1bufs[(bk + 1) % 2]

        def X_pre(bk):
            row0 = OTH0 + 512 * bk - 1
            for j in range(4):
                S.dma('sp', xt_buf[j], xe[row0 + 1 + 128 * j:row0 + 1 + 128 * j + 128, :])
            S.dma('sp', xt_h[0:1, :], xe[row0:row0 + 1, :])
            for j in range(4):
                S.act(sq_junk, xt_buf[j], AF.Square, accum=ss_buf[:, j:j + 1])
            S.act(sq_junk[0:1, :], xt_h[0:1, :], AF.Square, accum=ss_h[0:1, 0:1])
            rs = ss_buf[:, 4:8]
            S.ts('dve', rs, ss_buf[:, 0:4], 1.0 / D, EPS, ALU.mult, ALU.add)
            S.act(rs, rs, AF.Sqrt)
            S.recip(rs, rs)
            rh = ss_h[0:1, 1:2]
            S.ts('dve', rh, ss_h[0:1, 0:1], 1.0 / D, EPS, ALU.mult, ALU.add)
            S.act(rh, rh, AF.Sqrt)
            S.recip(rh, rh)
            for j in range(4):
                S.ts('dve', xnb_buf[j], xt_buf[j], ss_buf[:, 4 + j:5 + j], None, ALU.mult)
            S.ts('dve', xnb_h[0:1, :], xt_h[0:1, :], rh, None, ALU.mult)

        def X_post(bk):
            B_ = big_bufs(bk)
            xoT, xmT = B_['xoT'], B_['xmT']
            for j in range(4):
                pb = psb[6 + (j % 2)]
                for dt in range(8):
                    S.transpose(pb[:, dt * 128:dt * 128 + 128], xnb_buf[j][:, dt * 128:(dt + 1) * 128], ident)
                S.tt('dve', xoT[:, :, 1 + 128 * j:1 + 128 * j + 128], pb[:, :].rearrange("p (a b) -> p a b", a=8),
                     g1.unsqueeze(2).to_broadcast([128, 8, 128]), ALU.mult)
            pb = psb[6]
            for dt in range(8):
                S.transpose(pb[:, dt * 128:dt * 128 + 1], xnb_h[0:1, dt * 128:(dt + 1) * 128], ident[0:1, 0:1])
            S.tt('dve', xoT[:, :, 0:1], pb[:, :].rearrange("p (a b) -> p a b", a=8)[:, :, 0:1],
                 g1.unsqueeze(2).to_broadcast([128, 8, 1]), ALU.mult)
            S.copy('dve', xmT[:, :, 512:514], carry[0])
            carry[0] = xmT[:, :, 0:2]

        def G_(bk):
            B_ = big_bufs(bk)
            xoT, gsb, sp1, w1, d1, tmp16 = B_['xoT'], B_['gsb'], B_['sp1'], B_['w1'], B_['d1'], B_['tmp16']
            pg = ps[0]
            for j in range(4):
                for dt in range(8):
                    S.mm(pg[:, 16 * j:16 * j + 16], xoT[:, dt, 1 + j * 128:1 + (j + 1) * 128], wg[:, dt, :],
                         start=(dt == 0), stop=(dt == 7))
            S.tt('dve', gsb, pg[:, 0:64].rearrange("p (a b) -> p a b", a=4),
                 gateb.unsqueeze(1).to_broadcast([128, 4, 16]), ALU.add)
            S.act(sp1, gsb[:, :, 12:16], AF.Exp, scale=-1.0)
            S.act(sp1, sp1, AF.Ln, bias=1.0, scale=1.0)
            pc = ps[1]
            S.mm(pc[:, 0:16], triB_f, sp1)
            for hb in range(2):
                S.mm(pc[:, 64 + 16 * hb:64 + 16 * hb + 16], ones_f[hb], sp1)
            S.tt('dve', tmp16, gsb[:, :, 8:12], pc[:, 0:16].rearrange("p (a b) -> p a b", a=4), ALU.add)
            S.act(w1, tmp16, AF.Exp)
            for hb in range(2):
                S.act(d1[:, hb, :, :], pc[:, 64 + 16 * hb:64 + 16 * hb + 16].rearrange("p (a b) -> p a b", a=4),
                      AF.Exp, scale=-1.0)

        def M_et(bk, et):
            B_ = big_bufs(bk)
            pm = ps[et % 4]
            for dt in range(8):
                S.mm(pm[:, :], wxm[:, dt, et * 128:(et + 1) * 128], B_['xoT'][:, dt, 0:512],
                     start=(dt == 0), stop=(dt == 7))
            S.copy('act', B_['xmT'][:, et, 0:512], pm[:, :])

        def V_(bk):
            B_ = big_bufs(bk)
            xmT, vp1, w1 = B_['xmT'], B_['vp1'], B_['w1']
            for j in range(4):
                pb = psb[4 + j % 2]
                for et in range(8):
                    S.transpose(pb[:, et * 128:(et + 1) * 128], xmT[:, et, 1 + 128 * j:1 + 128 * j + 128], ident)
                S.tt('dve', vp1[:, j, :, 0:256], pb[:, :].rearrange("p (a b) -> p a b", a=4),
                     w1[:, j, :].unsqueeze(2).to_broadcast([128, 4, 256]), ALU.mult)
            S.copy('dve', vp1[:, :, :, 256], w1)

        def C_(bk):
            B_ = big_bufs(bk)
            for et in range(8):
                pcv = ps[6 + et % 2]
                for k3 in range(3):
                    S.mm(pcv[:, :], dcw[:, k3, et, :], B_['xmT'][:, et, k3:k3 + 512], start=(k3 == 0), stop=(k3 == 2))
                S.act(B_['xcT'][:, et, :], pcv[:, :], AF.Silu, bias=cb[:, et:et + 1], scale=1.0)

        def K_(bk):
            B_ = big_bufs(bk)
            for j in range(4):
                pk = ps[4 + j % 2]
                for h in range(4):
                    for e2 in range(2):
                        S.mm(pk[:, h * 128:(h + 1) * 128], B_['xcT'][:, 2 * h + e2, j * 128:(j + 1) * 128],
                             wk[:, h, e2, :], start=(e2 == 0), stop=(e2 == 1))
                S.act(B_['ktok1'][:, j, :], pk[:, :], AF.Copy, scale=DK ** -0.5)

        def S_step(bk, i):
            B_ = big_bufs(bk)
            j, hb = [(jj, hh) for jj in range(4) for hh in range(2)][::-1][i]
            p0 = 64 * hb
            for h in range(4):
                pst = ps[4 + h]
                S.mm(pst[:, 0:257], B_['ktok1'][p0:p0 + 64, j, h * 128:(h + 1) * 128], B_['vp1'][p0:p0 + 64, j, h, :])
                S.stt(U_B[:, h, :], U_B[:, h, :], dprevB[:, h:h + 1], pst[:, 0:257], ALU.mult, ALU.add)
            S.copy('dve', dprevB, B_['d1'][:, hb, j, :])

        ss_h = AR.alloc([2], F32)
        bks = [3, 2, 1, 0]
        X_pre(3)
        X_post(3)
        G_(3)
        for bi, bk in enumerate(bks):
            prev = bks[bi - 1] if bi > 0 else None
            nxt = bks[bi + 1] if bi + 1 < 4 else None
            for et in range(8):
                M_et(bk, et)
                if prev is not None:
                    S_step(prev, et)
            V_(bk)
            if nxt is not None:
                X_pre(nxt)
            C_(bk)
            if nxt is not None:
                X_post(nxt)
            K_(bk)
            if nxt is not None:
                G_(nxt)
        for i in range(8):
            S_step(0, i)
        dbg("U_A", U_A[:, 0, :], [128, 257])
        dbg("U_B", U_B[:, 0, :], [128, 257])
        dbg("dprevA", dprevA, [128, 4])
        dbg("dprevB", dprevB, [128, 4])
        AR.pop()
        AR.pop()
        yaT = AR.alloc([8, NOWN], BF16)

        if stop_after == 'phase1':
            return finish(nc, S, y_out, dbg_outs, None)

        AR.push()
        gso = AR.alloc([NT, 16], F32)
        spo = AR.alloc([2, NT, 4], F32)
        wgt = AR.alloc([2, NT, 4], F32)
        einv = AR.alloc([2, NT, 4], F32)
        dch = AR.alloc([2, 2, NT, 4], F32)
        tmpg = AR.alloc([NT, 4], F32)
        pg = ps[2]
        for k in range(NT):
            for dt in range(8):
                S.mm(pg[:, 16 * k:16 * k + 16], xnT[:, dt, OWN0 + 128 * k:OWN0 + 128 * k + 128], wg[:, dt, :],
                     start=(dt == 0), stop=(dt == 7))
        S.tt('dve', gso, pg[:, 0:256].rearrange("p (a b) -> p a b", a=NT),
             gateb.unsqueeze(1).to_broadcast([128, NT, 16]), ALU.add)
        for di in range(2):
            f0 = 4 + 8 * di
            li0 = 8 * di
            S.act(spo[:, di, :, :], gso[:, :, f0:f0 + 4], AF.Exp, scale=-1.0)
            S.act(spo[:, di, :, :], spo[:, di, :, :], AF.Ln, bias=1.0, scale=1.0)
            pc = ps[3]
            S.mm(pc[:, 0:64], triA_f if di == 0 else triB_f, spo[:, di, :, :])
            for hb in range(2):
                S.mm(pc[:, 64 + 64 * hb:128 + 64 * hb], ones_f[hb], spo[:, di, :, :])
            csv = pc[:, 0:64].rearrange("p (a b) -> p a b", a=NT)
            S.tt('dve', tmpg, gso[:, :, li0:li0 + 4], csv, ALU.add)
            S.act(wgt[:, di, :, :], tmpg, AF.Exp)
            S.act(einv[:, di, :, :], csv, AF.Exp)
            for hb in range(2):
                S.act(dch[:, di, hb, :, :], pc[:, 64 + 64 * hb:128 + 64 * hb].rearrange("p (a b) -> p a b", a=NT),
                      AF.Exp, scale=-1.0)
        dbg("wgtA", wgt[:, 0, :, :], [128, NT, 4])
        dbg("dchA", dch[:, 0, :, :, :], [128, 2, NT, 4])

        wxmh = AR.alloc([8, 256], BF16)
        wob = [AR.alloc([8, 128], BF16) for _ in range(2)]
        xmt = [[AR.alloc([514], BF16) for _ in range(2)] for _ in range(2)]
        xcThs = [AR.alloc([2, NOWN], BF16) for _ in range(2)]
        vpA = AR.alloc([NT, 257], BF16)
        vpB = AR.alloc([NT, 257], BF16)
        qT = AR.alloc([NOWN], BF16)
        kT = AR.alloc([NOWN], BF16)
        ktok = AR.alloc([NT, 128], BF16)
        P_A = AR.alloc([NT, 128], BF16)
        P_B = AR.alloc([NT, 128], BF16)
        hrA = AR.alloc([NT, 257], BF16)
        hrB = AR.alloc([NT, 257], BF16)
        Cbf = [[AR.alloc([257], BF16) for _ in range(2)] for _ in range(2)]
        rr = AR.alloc([2, NT], F32)
        t1 = [AR.alloc([256], F32) for _ in range(2)]
        hn = [AR.alloc([256], BF16) for _ in range(4)]
        sqj = AR.alloc([256], BF16)
        st8 = AR.alloc([NT, 2], F32)
        sohs = [AR.alloc([2, NOWN], BF16) for _ in range(2)]
        wocnt = [0]

        def head_pre(h):
            xcTh = xcThs[h % 2]
            soh = sohs[h % 2]
            S.dma('pool', wxmh, d_wxm.rearrange("p (a b) -> p a b", a=8)[:, :, 256 * h:256 * h + 256])
            for tb in (3, 2, 1, 0):
                c0 = OWN0 + 512 * tb
                for e2 in range(2):
                    pm = ps[e2]
                    xb = xmt[e2][tb % 2]
                    for dt in range(8):
                        S.mm(pm[:, :], wxmh[:, dt, 128 * e2:128 * e2 + 128], xnT[:, dt, c0 - 1:c0 + 511],
                             start=(dt == 0), stop=(dt == 7))
                    if tb == 3:
                        pa = ps[2]
                        for dt in range(8):
                            S.mm(pa[:, 2 * e2:2 * e2 + 2], wxmh[:, dt, 128 * e2:128 * e2 + 128],
                                 xnT[:, dt, c0 + 511:c0 + 513], start=(dt == 0), stop=(dt == 7))
                        S.copy('act', xb[:, 512:514], pa[:, 2 * e2:2 * e2 + 2])
                    else:
                        S.copy('dve', xb[:, 512:514], xmt[e2][(tb + 1) % 2][:, 0:2])
                    S.copy('act', xb[:, 0:512], pm[:, :])
                pvb = psb[4 + (tb % 2)]
                for kk in range(4):
                    for e2 in range(2):
                        S.transpose(pvb[:, 256 * kk + 128 * e2:256 * kk + 128 * e2 + 128],
                                    xmt[e2][tb % 2][:, 1 + 128 * kk:1 + 128 * kk + 128], ident)
                for kk in range(4):
                    k = 4 * tb + kk
                    S.act(vpA[:, k, 0:256], pvb[:, 256 * kk:256 * kk + 256], AF.Copy, scale=wgt[:, 0, k, h:h + 1])
                    S.ts('dve', vpB[:, k, 0:256], pvb[:, 256 * kk:256 * kk + 256], wgt[:, 1, k, h:h + 1], None, ALU.mult)
                for e2 in range(2):
                    xb = xmt[e2][tb % 2]
                    et = 2 * h + e2
                    pcv = ps[2 + e2]
                    for k3 in range(3):
                        S.mm(pcv[:, :], dcw[:, k3, et, :], xb[:, k3:k3 + 512], start=(k3 == 0), stop=(k3 == 2))
                    S.act(xcTh[:, e2, 512 * tb:512 * tb + 512], pcv[:, :], AF.Silu, bias=cb[:, et:et + 1], scale=1.0)
                yield
            S.copy('dve', vpA[:, :, 256], wgt[:, 0, :, h])
            S.copy('dve', vpB[:, :, 256], wgt[:, 1, :, h])
            for e2 in range(2):
                S.dma('pool', wob[e2], d_wo[2 * h + e2].rearrange("p (a b) -> p a b", a=8))

            def emit_o(e2, k4):
                po = ps[4 + (k4 % 2)]
                for dt in range(8):
                    S.mm(po[:, :], wob[e2][:, dt, :], xnT[:, dt, OWN0 + 512 * k4:OWN0 + 512 * k4 + 512],
                         start=(dt == 0), stop=(dt == 7))
                S.act(soh[:, e2, 512 * k4:512 * k4 + 512], po[:, :], AF.Sigmoid)

            for tb in range(4):
                pq = ps[2 * (tb % 2)]
                pk = ps[2 * (tb % 2) + 1]
                for e2 in range(2):
                    S.mm(pq[:, :], wq[:, h, e2, :], xcTh[:, e2, 512 * tb:512 * tb + 512], start=(e2 == 0),
                         stop=(e2 == 1))
                for e2 in range(2):
                    S.mm(pk[:, :], wk[:, h, e2, :], xcTh[:, e2, 512 * tb:512 * tb + 512], start=(e2 == 0),
                         stop=(e2 == 1))
                S.copy('dve', qT[:, 512 * tb:512 * tb + 512], pq[:, :])
                S.ts('dve', kT[:, 512 * tb:512 * tb + 512], pk[:, :], DK ** -0.5, None, ALU.mult)
                emit_o(0, tb)
                yield
            for k4 in range(4):
                pk = ps[6 + (k4 % 2)]
                for kk in range(4):
                    k = 4 * k4 + kk
                    for e2 in range(2):
                        S.mm(pk[:, 128 * kk:128 * kk + 128], xcTh[:, e2, 128 * k:128 * k + 128], wk[:, h, e2, :],
                             start=(e2 == 0), stop=(e2 == 1))
                S.ts('dve', ktok[:, 4 * k4:4 * k4 + 4, :], pk[:, :].rearrange("p (a b) -> p a b", a=4), DK ** -0.5,
                     None, ALU.mult)
                emit_o(1, k4)
                yield
            for k4 in range(4):
                psc = ps[k4 % 2]
                for kk in range(4):
                    k = 4 * k4 + kk
                    S.mm(psc[:, 128 * kk:128 * kk + 128], kT[:, 128 * k:128 * k + 128], qT[:, 128 * k:128 * k + 128])
                pv3 = psc[:, :].rearrange("p (a b) -> p a b", a=4)
                S.tt('dve', P_A[:, 4 * k4:4 * k4 + 4, :], pv3, maskA.unsqueeze(1).to_broadcast([128, 4, 128]), ALU.mult)
                S.tt('dve', P_B[:, 4 * k4:4 * k4 + 4, :], pv3, maskB.unsqueeze(1).to_broadcast([128, 4, 128]), ALU.mult)
                yield
            if h == 0:
                dbg("xcTh", xcTh[:, 0, :], [128, NOWN])
                dbg("qT", qT, [128, NOWN])
                dbg("vpA", vpA, [128, NT, 257])
                dbg("P_A", P_A, [128, NT, 128])
                dbg("ktok", ktok, [128, NT, 128])
            for e2 in range(2):
                et = 2 * h + e2
                S.ts('dve', xcTh[:, e2, :], xcTh[:, e2, :], skp[:, et:et + 1], None, ALU.mult)

        def head_chain(h):
            S.act(Cbf[0][0], U_A[:, h, :], AF.Copy, scale=dprevA[:, h:h + 1])
            S.act(Cbf[1][0], U_B[:, h, :], AF.Copy, scale=dprevB[:, h:h + 1])

            def chunk_of(step, di):
                c = step if di == 0 else 31 - step
                return c // 2, c % 2

            def emit_st(step):
                for di in range(2):
                    k, hb = chunk_of(step, di)
                    p0 = 64 * hb
                    vp = vpA if di == 0 else vpB
                    S.mm(ps[2 * di + step % 2][:, 0:257], ktok[p0:p0 + 64, k, :], vp[p0:p0 + 64, k, :])

            emit_st(0)
            for step in range(32):
                if step + 1 < 32:
                    emit_st(step + 1)
                for di in range(2):
                    k, hb = chunk_of(step, di)
                    p0 = 64 * hb
                    vp = vpA if di == 0 else vpB
                    Pm = P_A if di == 0 else P_B
                    pout = ps[4 + 2 * di + step % 2]
                    S.mm(pout[:, 0:257], Pm[p0:p0 + 64, k, :], vp[p0:p0 + 64, k, :], start=True, stop=False)
                    S.mm(pout[:, 0:257], qT[:, 128 * k:128 * k + 128], Cbf[di][step % 2], start=False, stop=True)
                for di in range(2):
                    k, hb = chunk_of(step, di)
                    p0 = 64 * hb
                    U = U_A if di == 0 else U_B
                    hr = hrA if di == 0 else hrB
                    if step == 0:
                        dpv = (dprevA if di == 0 else dprevB)[:, h:h + 1]
                    else:
                        kp, hbp = chunk_of(step - 1, di)
                        dpv = dch[:, di, hbp, kp, h:h + 1]
                    S.stt(U[:, h, :], U[:, h, :], dpv, ps[2 * di + step % 2][:, 0:257], ALU.mult, ALU.add)
                    if step + 1 < 32:
                        S.act(Cbf[di][(step + 1) % 2], U[:, h, :], AF.Copy, scale=dch[:, di, hb, k, h:h + 1])
                lag = [step - 1] if step >= 1 else []
                if step == 31:
                    lag.append(31)
                for s2 in lag:
                    for di in range(2):
                        k2, hb2 = chunk_of(s2, di)
                        hr2 = hrA if di == 0 else hrB
                        S.copy('act' if di == 0 else 'dve', hr2[64 * hb2:64 * hb2 + 64, k2, :],
                               ps[4 + 2 * di + s2 % 2][64 * hb2:64 * hb2 + 64, 0:257])
            if h == 0:
                dbg("hrA", hrA, [128, NT, 257])
                dbg("hrB", hrB, [128, NT, 257])

        def head_post(h):
            uuh = xcThs[h % 2]
            soh = sohs[h % 2]
            for di in range(2):
                hr = hrA if di == 0 else hrB
                S.act(rr[:, di, :], hr[:, :, 256], AF.Abs)
                S.tt('dve', rr[:, di, :], rr[:, di, :], einv[:, di, :, h], ALU.max)
                S.recip(rr[:, di, :], rr[:, di, :])
            for k in range(NT):
                b = k % 2
                S.ts('dve', t1[b], hrA[:, k, 0:256], rr[:, 0, k:k + 1], None, ALU.mult)
                S.stt(hrA[:, k, 0:256], hrB[:, k, 0:256], rr[:, 1, k:k + 1], t1[b], ALU.mult, ALU.add)
                S.act(sqj, hrA[:, k, 0:256], AF.Square, accum=st8[:, k, 0:1])
                if k % 2 == 1:
                    yield
            S.ts('dve', st8[:, :, 1], st8[:, :, 0], 1.0 / DV, EPS, ALU.mult, ALU.add)
            S.act(st8[:, :, 1], st8[:, :, 1], AF.Sqrt)
            S.recip(st8[:, :, 1], st8[:, :, 1])
            for k4 in range(4):
                pT = psb[(k4 % 2)]
                for kk in range(4):
                    k = 4 * k4 + kk
                    b = k % 4
                    S.act(hn[b], hrA[:, k, 0:256], AF.Copy, scale=st8[:, k, 1:2])
                    for e2 in range(2):
                        S.transpose(pT[:, 512 * e2 + 128 * kk:512 * e2 + 128 * kk + 128],
                                    hn[b][:, 128 * e2:128 * e2 + 128], ident)
                for e2 in range(2):
                    et = 2 * h + e2
                    w = (2 * k4 + e2) % 2
                    yv = yaT[:, et, 512 * k4:512 * k4 + 512]
                    S.stt(yv, pT[:, 512 * e2:512 * e2 + 512], mng[:, et:et + 1], uuh[:, e2, 512 * k4:512 * k4 + 512],
                          ALU.mult, ALU.add)
                    S.tt('dve', yv, yv, soh[:, e2, 512 * k4:512 * k4 + 512], ALU.mult)
                yield

        def drive(gens):
            gens = list(gens)
            while gens:
                for g_ in list(gens):
                    try:
                        next(g_)
                    except StopIteration:
                        gens.remove(g_)

        drive([head_pre(0)])
        for h in range(H_M):
            head_chain(h)
            drive([head_post(h)] + ([head_pre(h + 1)] if h + 1 < H_M else []))
        dbg("yaT", yaT, [128, 8, NOWN])
        AR.pop()
        if stop_after == 'mstage':
            return finish(nc, S, y_out, dbg_outs, None)
        ybT = AR.alloc([4, NOWN], BF16)

        AR.push()
        vext2 = AR.alloc([19, 8, 128], BF16)
        AR.push()
        wnv = AR.alloc([8, 512], BF16)
        S.dma('pool', wnv, d_wnv.rearrange("p (a b) -> p a b", a=8))
        AR.pop()
        wqp = AR.alloc([8, 128], BF16)
        wkp = AR.alloc([8, 128], BF16)
        btp = AR.alloc([2, 13, 128], BF16)
        qnT = AR.alloc([NOWN], BF16)
        knT = AR.alloc([2304 + 32], BF16)
        sqT = [AR.alloc([512], BF16) for _ in range(4)]
        rsT4 = AR.alloc([4, 512], F32)
        rsT = [rsT4[:, i, :] for i in range(4)]
        PT = [AR.alloc([768], BF16) for _ in range(2)]
        PTm = AR.alloc([NOWN], BF16)
        lnb = [AR.alloc([512], F32) for _ in range(2)]
        recq = [AR.alloc([512], F32) for _ in range(2)]
        outsb = [rsT4, rsT4]
        zer = AR.alloc([128], BF16)
        qz = [AR.alloc([NOWN], BF16) for _ in range(2)]
        epsq = AR.alloc([2], F32)
        S.memset('dve', zer, 0.0)
        S.memset('dve', epsq[:, 0:1], DH * EPS)
        S.memset('dve', epsq[:, 1:2], EPS)
        vv = vext2[:, :, :, :].rearrange("p j (a b) c -> p j a b c", b=2)
        S.memset('dve', vv[:, :, :, 0, 64:128], 1.0)
        S.memset('dve', vv[:, :, :, 1, 0:64], 1.0)
        for j in range(19):
            npk = 128 if j < 18 else 32
            pv_ = ps[j % 2]
            for dt in range(8):
                lh = xnT[:, dt, OWN0 + 128 * j:OWN0 + 128 * j + 128] if j < 18 else xnTm[:, dt, :]
                S.mm(pv_[0:npk, :], lh, wnv[:, dt, :], start=(dt == 0), stop=(dt == 7))
            src = pv_[0:npk, :].rearrange("p (a b c) -> p a b c", a=4, b=2)
            dst = vext2[0:npk, j, :, :].rearrange("p (a b) c -> p a b c", b=2)
            S.copy('act', dst[:, :, 0, 0:64], src[:, :, 0, :])
            S.copy('dve', dst[:, :, 1, 64:128], src[:, :, 1, :])

        def qk_norm(dst, wts, col0, ncols, which, cnt):
            b = cnt % 4
            pq = ps[b]
            pss = ps[4 + b]
            for dt in range(8):
                S.mm(pq[:, 0:ncols], wts[:, dt, :], col0[dt], start=(dt == 0), stop=(dt == 7))
            S.act(sqT[b][:, 0:ncols], pq[:, 0:ncols], AF.Square)
            S.mm(pss[:, 0:ncols], blk64, sqT[b][:, 0:ncols])
            if which == 0:
                S.act(rsT[b][:, 0:ncols], pss[:, 0:ncols], AF.Ln, bias=epsq[:, 0:1], scale=1.0)
            else:
                S.act(rsT[b][:, 0:ncols], pss[:, 0:ncols], AF.Ln, bias=epsq[:, 1:2], scale=1.0 / DH)
            S.act(rsT[b][:, 0:ncols], rsT[b][:, 0:ncols], AF.Exp, scale=-0.5)
            S.stt(dst, pq[:, 0:ncols], qkg[:, which:which + 1], rsT[b][:, 0:ncols], ALU.mult, ALU.mult)

        def bias_pos(i, j):
            if i == 0:
                return j
            if i == 1:
                return 4 + j
            return 10 - (j - i)

        Iof = {j: [i for i in range(NT) if j in [jj for jj, _ in key_tiles(i)]] for j in range(18)}
        ncnt = 0
        sc = 0
        for pr in range(4):
            S.dma('pool', wqp, d_wnq[pr].rearrange("p (a b) -> p a b", a=8))
            S.dma('pool', wkp, d_wnk[pr].rearrange("p (a b) -> p a b", a=8))
            S.dma('pool', btp, d_bt[pr].rearrange("p (a b c) -> p a b c", a=2, b=13))
            for tb in range(4):
                c0 = OWN0 + 512 * tb
                qk_norm(qnT[:, 512 * tb:512 * tb + 512], wqp, [xnT[:, dt, c0:c0 + 512] for dt in range(8)], 512, 0, ncnt)
                ncnt += 1
            for tb in range(5):
                c0 = OWN0 + 512 * tb
                n = 512 if tb < 4 else 256
                qk_norm(knT[:, 512 * tb:512 * tb + n], wkp, [xnT[:, dt, c0:c0 + n] for dt in range(8)], n, 1, ncnt)
                ncnt += 1
            qk_norm(knT[:, 2304:2336], wkp, [xnTm[:, dt, :] for dt in range(8)], 32, 1, ncnt)
            ncnt += 1
            if pr == 0:
                dbg("qnT", qnT, [128, NOWN])
                dbg("knT", knT, [128, 2336])
            for hh in range(2):
                S.memset('dve', qz[hh][64 - 64 * hh:128 - 64 * hh, :], 0.0)
                S.copy('dve', qz[hh][64 * hh:64 * hh + 64, :], qnT[64 * hh:64 * hh + 64, :])
            for hh in range(2):
                h = 2 * pr + hh
                bp = 64 * hh
                for b in range(4):
                    S.mm(ps[b][:, :], zer, qnT[:, 512 * b:512 * b + 512], start=True, stop=True)
                for b in range(4):
                    sA = ps[4 + 2 * (sc % 2)]
                    sc += 1
                    S.mm(sA[0:32, :], knT[bp:bp + 64, 2304:2336], qnT[bp:bp + 64, 512 * b:512 * b + 512])
                    S.act(PTm[0:32, 512 * b:512 * b + 512], sA[0:32, :], AF.Exp, bias=mbc[0:32, h:h + 1], scale=1.0)
                    S.mm(ps[b][:, :], vext2[0:32, 18, h, :], PTm[0:32, 512 * b:512 * b + 512], start=False, stop=False,
                         sgc=True)
                def emit_scores(j, slot):
                    I = Iof[j]
                    i0, n = I[0], len(I)
                    nq = 128 * n
                    sA = ps[4 + 2 * slot]
                    sB = ps[5 + 2 * slot]
                    pt = PT[slot]
                    segs = [(sA, 0, min(nq, 512))] + ([(sB, 512, nq)] if nq > 512 else [])
                    for (bank, lo, hi) in segs:
                        S.mm(bank[:, 0:hi - lo], knT[:, 128 * j:128 * j + 128],
                             qz[hh][:, 128 * i0 + lo:128 * i0 + hi], start=True, stop=False)
                        idxs = [ix for ix in range(n) if lo <= 128 * ix < hi]
                        runs = []
                        for ix in idxs:
                            p_ = bias_pos(I[ix], j)
                            if runs and runs[-1][1] + runs[-1][2] == p_ and runs[-1][0] + runs[-1][2] == ix:
                                runs[-1][2] += 1
                            else:
                                runs.append([ix, p_, 1])
                        for ri, (ix, p_, r) in enumerate(runs):
                            S.mm(bank[:, 128 * ix - lo:128 * (ix + r) - lo], ident, btp[:, hh, p_:p_ + r, :],
                                 start=False, stop=(ri == len(runs) - 1))
                        S.act(pt[:, lo:hi], bank[:, 0:hi - lo], AF.Exp)

                def emit_pv(j, slot):
                    I = Iof[j]
                    i0, n = I[0], len(I)
                    pt = PT[slot]
                    for b in range(4):
                        ilo, ihi = max(i0, 4 * b), min(i0 + n, 4 * b + 4)
                        if ilo >= ihi:
                            continue
                        S.mm(ps[b][:, 128 * (ilo - 4 * b):128 * (ihi - 4 * b)], vext2[:, j, h, :],
                             pt[:, 128 * (ilo - i0):128 * (ihi - i0)], start=False, stop=False, sgc=True)

                emit_scores(0, 0)
                for j in range(18):
                    if j + 1 < 18:
                        emit_scores(j + 1, (j + 1) % 2)
                    emit_pv(j, j % 2)
                osb = outsb[h % 2]
                for b in range(4):
                    S.copy('act' if b % 2 == 0 else 'dve', osb[:, b, :], ps[b][:, :])
                for b in range(4):
                    bb = b % 2
                    num0, den0 = (0, 64) if hh == 0 else (64, 0)
                    S.act(lnb[bb][num0:num0 + 64, :], osb[den0:den0 + 64, b, :], AF.Ln)
                    S.act(recq[bb][num0:num0 + 64, :], lnb[bb][num0:num0 + 64, :], AF.Exp, scale=-1.0)
                    S.tt('dve', ybT[num0:num0 + 64, pr, 512 * b:512 * b + 512], osb[num0:num0 + 64, b, :],
                         recq[bb][num0:num0 + 64, :], ALU.mult)
        dbg("ybT", ybT, [128, 4, NOWN])
        AR.pop()
        if stop_after == 'na':
            return finish(nc, S, y_out, dbg_outs, None)

        AR.push()
        assert AR.off < MIX_OFF
        mixT = arena_t[:, MIX_OFF // 4:ARENA_BYTES // 4].bitcast(BF16).rearrange("p (a b) -> p a b", a=8)
        wgab = [AR.alloc([8, 128], BF16) for _ in range(2)]
        wgbb = [AR.alloc([8, 128], BF16) for _ in range(2)]
        wab = [AR.alloc([8, 128], BF16) for _ in range(2)]
        wbb = [AR.alloc([4, 128], BF16) for _ in range(2)]
        sga = [AR.alloc([512], F32) for _ in range(2)]
        sgb = [AR.alloc([512], F32) for _ in range(2)]
        tg1 = [AR.alloc([512], F32) for _ in range(2)]
        tg2 = [AR.alloc([512], F32) for _ in range(2)]
        WOUT_OFF = MIX_OFF - 8 * 1024 * 2
        assert AR.off <= WOUT_OFF, (AR.off, WOUT_OFF)
        wout = arena_t[:, WOUT_OFF // 4:MIX_OFF // 4].bitcast(BF16).rearrange("p (a b) -> p a b", a=8)
        for q4 in range(4):
            S.dma('pool', wout[:, 2 * q4:2 * q4 + 2, :],
                  d_wout.rearrange("p (a b) -> p a b", a=8)[:, 2 * q4:2 * q4 + 2, :])
        gc = 0
        for blk in range(8):
            w = blk % 2
            S.dma('pool', wgab[w], d_wga[blk].rearrange("p (a b) -> p a b", a=8))
            S.dma('pool', wgbb[w], d_wgb[blk].rearrange("p (a b) -> p a b", a=8))
            S.dma('pool', wab[w], d_wa[blk].rearrange("p (a b) -> p a b", a=8))
            S.dma('pool', wbb[w], d_wb[blk].rearrange("p (a b) -> p a b", a=4))
            for tb in range(4):
                b = gc % 2
                gc += 1
                c0 = OWN0 + 512 * tb
                pga, pgb, pa_, pb_ = ps[4 * b], ps[4 * b + 1], ps[4 * b + 2], ps[4 * b + 3]
                for dt in range(8):
                    S.mm(pga[:, :], wgab[w][:, dt, :], xnT[:, dt, c0:c0 + 512], start=(dt == 0), stop=(dt == 7))
                for dt in range(8):
                    S.mm(pgb[:, :], wgbb[w][:, dt, :], xnT[:, dt, c0:c0 + 512], start=(dt == 0), stop=(dt == 7))
                for et in range(8):
                    S.mm(pa_[:, :], wab[w][:, et, :], yaT[:, et, 512 * tb:512 * tb + 512], start=(et == 0), stop=(et == 7))
                for c4 in range(4):
                    S.mm(pb_[:, :], wbb[w][:, c4, :], ybT[:, c4, 512 * tb:512 * tb + 512], start=(c4 == 0), stop=(c4 == 3))
                S.act(sga[b], pga[:, :], AF.Sigmoid)
                S.act(sgb[b], pgb[:, :], AF.Sigmoid)
                S.tt('dve', tg1[b], sga[b], pa_[:, :], ALU.mult)
                S.tt('dve', tg2[b], sgb[b], pb_[:, :], ALU.mult)
                S.tt('dve', mixT[:, blk, 512 * tb:512 * tb + 512], tg1[b], tg2[b], ALU.add)
        dbg("mixT", mixT, [128, 8, NOWN])
        AR.pop()
        if stop_after == 'g':
            return finish(nc, S, y_out, dbg_outs, None)

        AR.off = OFF_XNT
        h1 = AR.alloc([NT, 1024], F32)
        xn2T = AR.alloc([8, NOWN], BF16)
        OFF_OTMP = AR.off
        xq = [AR.alloc([1024], F32) for _ in range(4)]
        xn2b = [AR.alloc([1024], BF16) for _ in range(4)]
        sqj2 = AR.alloc([1024], BF16)
        ss2 = AR.alloc([16], F32)
        assert AR.off <= MIX_OFF - 16 * 1024, AR.off
        OFF_FFN = AR.off
        tcnt = [0]

        def O_proj(t0):
            for i in range(4):
                t = t0 + i
                S.dma('sp', xq[i], xe[OWN0 + 128 * t:OWN0 + 128 * t + 128, :])
            for i in range(4):
                t = t0 + i
                for half in range(2):
                    po = ps[(2 * i + half) % 6]
                    for dt in range(8):
                        S.mm(po[:, :], mixT[:, dt, 128 * t:128 * t + 128], wout[:, dt, 512 * half:512 * half + 512],
                             start=(dt == 0), stop=(dt == 7))
                    S.tt('dve', h1[:, t, 512 * half:512 * half + 512], po[:, :], xq[i][:, 512 * half:512 * half + 512],
                         ALU.add)
                S.act(sqj2, h1[:, t, :], AF.Square, accum=ss2[:, 8 * ((t0 // 4) % 2) + i:8 * ((t0 // 4) % 2) + i + 1])

        def O_norm(t0):
            o8 = 8 * ((t0 // 4) % 2)
            rsv = ss2[:, o8 + 4:o8 + 8]
            S.ts('dve', rsv, ss2[:, o8:o8 + 4], 1.0 / D, EPS, ALU.mult, ALU.add)
            S.act(rsv, rsv, AF.Sqrt)
            S.recip(rsv, rsv)
            for i in range(4):
                t = t0 + i
                S.ts('dve', xn2b[i], h1[:, t, :], ss2[:, o8 + 4 + i:o8 + 5 + i], None, ALU.mult)

        def O_tr(t0):
            for i in range(4):
                t = t0 + i
                pb = psb[6 + (tcnt[0] % 2)]
                tcnt[0] += 1
                for dt in range(8):
                    S.transpose(pb[:, dt * 128:dt * 128 + 128], xn2b[i][:, dt * 128:(dt + 1) * 128], ident)
                S.tt('dve', xn2T[:, :, 128 * t:128 * t + 128], pb[:, :].rearrange("p (a b) -> p a b", a=8),
                     g2.unsqueeze(2).to_broadcast([128, 8, 128]), ALU.mult)

        O_proj(0)
        for t0 in range(0, NT, 4):
            O_norm(t0)
            if t0 + 4 < NT:
                O_proj(t0 + 4)
            O_tr(t0)
        dbg("h1", h1, [128, NT, 1024])
        AR.off = WOUT_OFF
        wf2r = [AR.alloc([4, 1024], BF16) for _ in range(2)]
        AR.off = OFF_OTMP
        wf1r = [AR.alloc([4, 8, 128], BF16) for _ in range(2)]
        AR.off = MIX_OFF
        zTg = [AR.alloc([4, NOWN], BF16) for _ in range(2)]
        AR.off = OFF_FFN
        rl = [AR.alloc([512], BF16) for _ in range(2)]
        assert AR.off <= WOUT_OFF

        def ffn_load(G):
            S.dma('pool', wf1r[G % 2], d_wf1[4 * G:4 * G + 4].rearrange("a p (b c) -> p a b c", b=8))
            S.dma('pool', wf2r[G % 2], d_wf2[4 * G:4 * G + 4].rearrange("a p c -> p a c"))

        out_toks = []
        ffn_load(0)
        zc = 0
        bc = 0
        for G in range(8):
            if G + 1 < 8:
                ffn_load(G + 1)
            g2_ = G % 2
            for fbi in range(4):
                for tb in range(4):
                    b = zc % 2
                    pz = ps[zc % 8]
                    zc += 1
                    for dt in range(8):
                        S.mm(pz[:, :], wf1r[g2_][:, fbi, dt, :], xn2T[:, dt, 512 * tb:512 * tb + 512],
                             start=(dt == 0), stop=(dt == 7))
                    S.act(rl[b], pz[:, :], AF.Relu)
                    S.tt('dve', zTg[g2_][:, fbi, 512 * tb:512 * tb + 512], rl[b], rl[b], ALU.mult)
            for ts_ in range(4):
                for half in range(2):
                    bs = 4 * (bc % 2)
                    bc += 1
                    for fbi in range(4):
                        for k4 in range(4):
                            t = 4 * ts_ + k4
                            S.mm(ps[bs + k4][:, :], zTg[g2_][:, fbi, 128 * t:128 * t + 128],
                                 wf2r[g2_][:, fbi, 512 * half:512 * half + 512], start=(fbi == 0), stop=(fbi == 3))
                    for k4 in range(4):
                        t = 4 * ts_ + k4
                        hv = h1[:, t, 512 * half:512 * half + 512]
                        S.tt('dve', hv, ps[bs + k4][:, :], hv, ALU.add)
                        if G == 7 and half == 1:
                            out_toks.append(S.dma('sp', y_out[128 * t:128 * t + 128, :], h1[:, t, :]))
        return finish(nc, S, y_out, dbg_outs, out_toks)


def finish(nc, S, y_out, dbg_outs, out_toks):
    toks = list(dbg_outs.values())
    if out_toks:
        toks += out_toks
    S.wait_tokens('sp', toks)
    S.emit()
    return nc


def _colvec(v):
    return np.ascontiguousarray(np.asarray(v, np.float32).reshape(8, 128).T)


def _rows_ptc(w):
    T = w.shape[0] // 128
    return np.ascontiguousarray(w.reshape(T, 128, w.shape[1]).transpose(1, 0, 2).reshape(128, -1))


def _col_blocks(w, c0, nblk, bw=128):
    return np.ascontiguousarray(np.stack([_rows_ptc(w[:, c0 + bw * i:c0 + bw * (i + 1)]) for i in range(nblk)]))


def _na_tables(hf, rpb, meta_bias):
    def tile(i, j):
        out = np.full((NH, 128, 128), NEG, np.float32)
        cl = np.arange(64)
        for kr in range(2):
            for qr in range(2):
                krow_l, qrow_l = 2 * j + kr, 2 * i + qr
                if hf == 0:
                    krow, qrow, kcol, qcol = krow_l, qrow_l, cl, cl
                else:
                    krow, qrow, kcol, qcol = 63 - krow_l, 63 - qrow_l, 63 - cl, 63 - cl
                r0 = min(max(qrow - 4, 0), 56)
                if not (r0 <= krow < r0 + 8):
                    continue
                win0 = np.clip(qcol - 8, 0, 48)
                ok = (kcol[:, None] >= win0[None, :]) & (kcol[:, None] < win0[None, :] + 16)
                dr = krow - qrow + 7
                dc = np.clip(kcol[:, None] - qcol[None, :], -15, 15) + 15
                vals = rpb[:, dr, dc]
                out[:, kr * 64:(kr + 1) * 64, qr * 64:(qr + 1) * 64] = np.where(ok[None], vals, NEG)
        return out
    kinds = [tile(0, j) for j in range(4)] + [tile(1, j) for j in range(4)] + [tile(8, 8 + d) for d in (2, 1, 0, -1, -2)]
    BT = np.stack(kinds)
    bt = BT.reshape(13, 4, 2, 128, 128).transpose(1, 3, 2, 0, 4).reshape(4, 128, 13 * 2 * 128)
    MB = np.full((NH, 32), NEG, np.float32)
    if hf == 0:
        MB[:, 0:16] = meta_bias
    else:
        MB[:, 16:32] = meta_bias[:, ::-1]
    return np.ascontiguousarray(bt), np.ascontiguousarray(MB.T)


def _const_masks():
    j = np.arange(128)[:, None]
    t = np.arange(128)[None, :]
    same = (j // 64) == (t // 64)
    cm = np.zeros((128, 6, 128), np.float32)
    cm[:, 0, :] = same & (j <= t)
    cm[:, 1, :] = same & (j >= t)
    cm[:, 2, :] = (j < 64) & (t >= 0)
    cm[:, 3, :] = (j >= 64) & (t >= 0)
    cm[:, 4, :] = (j == t)
    cm[:, 5, :] = same
    return cm


def prep_inputs(inp):
    f = lambda a: np.ascontiguousarray(np.asarray(a, np.float32))
    w_in = f(inp['w_in'])
    shared = {
        'wxm': _rows_ptc(w_in[:, 0:1024]),
        'wo': _col_blocks(w_in, 1024, 8),
        'wqm': np.ascontiguousarray(f(inp['mlstm_wq']).reshape(4, 2, 128, 128).transpose(2, 0, 1, 3).reshape(128, -1)),
        'wkm': np.ascontiguousarray(f(inp['mlstm_wk']).reshape(4, 2, 128, 128).transpose(2, 0, 1, 3).reshape(128, -1)),
        'wnq': _col_blocks(w_in, 2064, 4),
        'wnk': _col_blocks(w_in, 2576, 4),
        'wnv': _rows_ptc(w_in[:, 3088:3600]),
        'wga': _col_blocks(w_in, 3600, 8),
        'wgb': _col_blocks(w_in, 4624, 8),
        'wa': _col_blocks(f(inp['w_branch_a']), 0, 8),
        'wb': _col_blocks(f(inp['w_branch_b']), 0, 8),
        'wout': _rows_ptc(f(inp['w_out'])),
        'wf1': _col_blocks(f(inp['w_ff1']), 0, 32),
        'wf2': np.ascontiguousarray(f(inp['w_ff2']).reshape(32, 128, 1024)),
        'cmask': _const_masks(),
        'qkg': np.ascontiguousarray(np.stack([np.tile(f(inp['na_q_norm_g']), 2), np.tile(f(inp['na_k_norm_g']), 2)], axis=1)),
    }
    x = f(inp['x'])
    meta = f(inp['meta_tokens'])
    cwfull = f(inp['mlstm_conv_w'])[:, 0, :]
    gb = f(inp['mlstm_gate_b']).reshape(16)
    gcols = w_in[:, 2048:2064]
    zero1 = np.zeros((1, D), np.float32)
    z16 = np.zeros((16, D), np.float32)
    maps = []
    tabs = {}
    for core in range(8):
        b, hf = core // 2, core % 2
        m = dict(shared)
        if hf == 0:
            xe = np.concatenate([zero1, meta, x[b], z16, zero1])
            vl = (1.0, 0.0)
            cwl, gc, gbl = cwfull, gcols, gb
        else:
            xe = np.concatenate([zero1, z16, x[b][::-1], meta[::-1], zero1])
            vl = (0.0, 1.0)
            cwl = cwfull[::-1]
            gc = np.concatenate([gcols[:, 8:16], gcols[:, 0:8]], axis=1)
            gbl = np.concatenate([gb[8:16], gb[0:8]])
        m['xe'] = np.ascontiguousarray(xe)
        m['valid'] = np.ascontiguousarray(np.tile(np.array(vl, np.float32)[None, :], (128, 1)))
        m['gate_b'] = np.ascontiguousarray(np.tile(gbl[None, :], (128, 1)))
        m['wg'] = _rows_ptc(np.ascontiguousarray(gc))
        m['vecs'] = np.ascontiguousarray(np.concatenate(
            [_colvec(inp['norm1_g']), _colvec(cwl[0]), _colvec(cwl[1]), _colvec(cwl[2]), _colvec(inp['mlstm_conv_b']),
             _colvec(f(inp['mlstm_norm_g']).reshape(-1)), _colvec(inp['mlstm_skip']), _colvec(inp['norm2_g'])], axis=1))
        if hf not in tabs:
            tabs[hf] = _na_tables(hf, f(inp['na_rpb']), f(inp['na_meta_bias']))
        m['bt'], m['mb'] = tabs[hf]
        maps.append(m)
    return maps


_NC_CACHE = {}


def kernel(**inputs):
    maps = prep_inputs(inputs)
    if 'nc' not in _NC_CACHE:
        _NC_CACHE['nc'] = build_nc()
    nc = _NC_CACHE['nc']
    res = run_bass_kernel_spmd(nc, maps, core_ids=list(range(8)))
    out = np.zeros((4, 4096, D), np.float32)
    for core in range(8):
        b, hf = core // 2, core % 2
        y = np.asarray(res.results[core]["y"], np.float32)
        if hf == 0:
            out[b, 0:NOWN] = y
        else:
            out[b, NOWN:] = y[::-1]
    return out
```

```python
import contextlib
import numpy as np
import concourse.bass as bass
import concourse.mybir as mybir
from concourse.bass_utils import run_bass_kernel_spmd

F32 = mybir.dt.float32
BF16 = mybir.dt.bfloat16
AF = mybir.ActivationFunctionType
ALU = mybir.AluOpType
DSZ = {F32: 4, BF16: 2}

D = 1024
NOWN = 2048
NT = 16
OWN0 = 17
OTH0 = 17 + 2048
POST0 = 17 + 4096
XE_ROWS = 4130
NXC = 17 + 2048 + 256
H_M, DV, DK = 4, 256, 128
NH, DH = 8, 64
EPS = 1e-6
NEG = -30000.0
N_DMA_SEMS = 12
SAME_ENGINE_SYNC = True


class Sched:
    def __init__(self, nc):
        self.nc = nc
        self.engs = ('pe', 'act', 'dve', 'pool', 'sp')
        self.prog = {k: [] for k in self.engs}
        self.count = {k: 0 for k in self.engs}
        self.waited = {k: {} for k in self.engs}
        self.recs = {}
        self.dma_q = ('sp', 'pool', 'act')
        self.dma_uses = {q: [0] * N_DMA_SEMS for q in self.dma_q}
        self.dma_rr = {q: 0 for q in self.dma_q}
        self.n_inst = 0
        self.dram = set()

    def _box(self, ap):
        name = ap.tensor.name
        if name in self.dram:
            return None
        if name.startswith('ps'):
            return name, 0, 128, 0, 2048
        dims = ap.ap
        sz = DSZ[ap.dtype]
        shp = ap.tensor.shape
        row = 1
        for s in list(shp)[1:]:
            row *= int(s)
        off = int(ap.offset)
        p0 = off // row
        b0 = (off % row) * sz
        pc = int(dims[0][1])
        ext = 0
        for st, cn in dims[1:]:
            ext += (int(cn) - 1) * abs(int(st))
        b1 = b0 + (ext + 1) * sz
        return name, p0, p0 + pc, b0, b1

    def _deps(self, eng, ins, outs):
        deps = {}

        def add(sk, v):
            if deps.get(sk, 0) < v:
                deps[sk] = v
        rb = [b for b in (self._box(a) for a in ins) if b is not None]
        wb = [b for b in (self._box(a) for a in outs) if b is not None]
        wb = wb + [b for b in rb if b[0].startswith('ps') and b not in wb]
        rb = [b for b in rb if not b[0].startswith('ps')]
        for (name, p0, p1, b0, b1) in rb:
            for r in self.recs.get(name, ()):
                if r[0] < p1 and p0 < r[1] and r[2] < b1 and b0 < r[3]:
                    if r[4] is not None:
                        add(*r[4])
        for (name, p0, p1, b0, b1) in wb:
            for r in self.recs.get(name, ()):
                if r[0] < p1 and p0 < r[1] and r[2] < b1 and b0 < r[3]:
                    if r[4] is not None:
                        add(*r[4])
                    for sk, v in r[5].items():
                        add(sk, v)
        waits = []
        for sk, v in deps.items():
            if sk == eng and (eng == 'pe' or not SAME_ENGINE_SYNC):
                continue
            if self.waited[eng].get(sk, 0) >= v:
                continue
            self.waited[eng][sk] = v
            waits.append((sk, v))
        return waits, rb, wb

    def _commit(self, tok, rb, wb):
        for (name, p0, p1, b0, b1) in wb:
            lst = self.recs.setdefault(name, [])
            keep = []
            for r in lst:
                if r[0] >= p0 and r[1] <= p1 and r[2] >= b0 and r[3] <= b1:
                    continue
                keep.append(r)
            keep.append([p0, p1, b0, b1, tok, {}])
            self.recs[name] = keep
        for (name, p0, p1, b0, b1) in rb:
            lst = self.recs.setdefault(name, [])
            best = None
            for r in lst:
                if r[0] <= p0 and r[1] >= p1 and r[2] <= b0 and r[3] >= b1 and r[4] != tok:
                    sz = (r[1] - r[0]) * (r[3] - r[2])
                    if best is None or sz < best[0]:
                        best = (sz, r)
            if best is not None:
                r = best[1]
                if r[5].get(tok[0], 0) < tok[1]:
                    r[5][tok[0]] = tok[1]
                continue
            for r in lst:
                if r[0] < p1 and p0 < r[1] and r[2] < b1 and b0 < r[3]:
                    if r[4] == tok:
                        continue
                    if r[5].get(tok[0], 0) < tok[1]:
                        r[5][tok[0]] = tok[1]
            lst.append([p0, p1, b0, b1, None, {tok[0]: tok[1]}])

    def op(self, eng, fn, ins, outs):
        waits, rb, wb = self._deps(eng, ins, outs)
        self.count[eng] += 1
        tok = (eng, self.count[eng])
        self.prog[eng].append((waits, fn, tok))
        self._commit(tok, rb, wb)
        self.n_inst += 1
        return tok

    def dma(self, eng, out, in_):
        waits, rb, wb = self._deps(eng, [in_], [out])
        i = self.dma_rr[eng]
        self.dma_rr[eng] = (i + 1) % N_DMA_SEMS
        self.dma_uses[eng][i] += 1
        sk = ('dma', eng, i)
        prev = 16 * (self.dma_uses[eng][i] - 1)
        if prev > 0 and self.waited[eng].get(sk, 0) < prev:
            self.waited[eng][sk] = prev
            waits.append((sk, prev))
        tok = (sk, 16 * self.dma_uses[eng][i])
        self.prog[eng].append((waits, (lambda e, o=out, a=in_: e.dma_start(out=o, in_=a)), tok))
        self._commit(tok, rb, wb)
        self.n_inst += 1
        return tok

    def wait_tokens(self, eng, toks):
        waits = []
        for sk, v in toks:
            if self.waited[eng].get(sk, 0) >= v:
                continue
            self.waited[eng][sk] = v
            waits.append((sk, v))
        if waits:
            self.prog[eng].append((waits, None, None))

    def mm(self, out, lhsT, rhs, start=True, stop=True, sgc=False):
        if sgc:
            return self.op('pe', lambda e: e.matmul(out, lhsT=lhsT, rhs=rhs, start=start, stop=stop,
                                                    skip_group_check=True), [lhsT, rhs], [out])
        return self.op('pe', lambda e: e.matmul(out, lhsT=lhsT, rhs=rhs, start=start, stop=stop),
                       [lhsT, rhs], [out])

    def transpose(self, out, in_, ident):
        return self.op('pe', lambda e: e.transpose(out=out, in_=in_, identity=ident), [in_, ident], [out])

    def act(self, out, in_, func, bias=None, scale=None, accum=None):
        kw = {}
        ins = [in_]
        outs = [out]
        if bias is not None:
            kw['bias'] = bias
            if not isinstance(bias, (int, float)):
                ins.append(bias)
        if scale is not None:
            kw['scale'] = scale
            if not isinstance(scale, (int, float)):
                ins.append(scale)
        if accum is not None:
            kw['accum_out'] = accum
            outs.append(accum)
        return self.op('act', lambda e: e.activation(out=out, in_=in_, func=func, **kw), ins, outs)

    def ts(self, eng, out, in0, s1, s2, op0, op1=None):
        ins = [in0] + [s for s in (s1, s2) if s is not None and not isinstance(s, (int, float))]
        if op1 is None:
            fn = lambda e: e.tensor_scalar(out=out, in0=in0, scalar1=s1, scalar2=None, op0=op0)
        else:
            fn = lambda e: e.tensor_scalar(out=out, in0=in0, scalar1=s1, scalar2=s2, op0=op0, op1=op1)
        return self.op(eng, fn, ins, [out])

    def tt(self, eng, out, in0, in1, op):
        return self.op(eng, lambda e: e.tensor_tensor(out=out, in0=in0, in1=in1, op=op), [in0, in1], [out])

    def stt(self, out, in0, scalar, in1, op0, op1):
        ins = [in0, in1] + ([] if isinstance(scalar, (int, float)) else [scalar])
        return self.op('dve', lambda e: e.scalar_tensor_tensor(out=out, in0=in0, scalar=scalar, in1=in1,
                                                               op0=op0, op1=op1), ins, [out])

    def copy(self, eng, out, in_):
        if eng == 'act':
            return self.op('act', lambda e: e.copy(out=out, in_=in_), [in_], [out])
        return self.op(eng, lambda e: e.tensor_copy(out=out, in_=in_), [in_], [out])

    def recip(self, out, in_):
        return self.op('dve', lambda e: e.reciprocal(out=out, in_=in_), [in_], [out])

    def memset(self, eng, ap, val):
        return self.op(eng, lambda e: e.memset(ap, val), [], [ap])

    def emit(self):
        nc = self.nc
        engmap = {'pe': nc.tensor, 'act': nc.scalar, 'dve': nc.vector, 'pool': nc.gpsimd, 'sp': nc.sync}
        with contextlib.ExitStack() as st:
            sems = {}
            for e in self.engs:
                sems[e] = st.enter_context(nc.semaphore("sem_" + e))
            for q in self.dma_q:
                for i in range(N_DMA_SEMS):
                    sems[('dma', q, i)] = st.enter_context(nc.semaphore("sem_dma_%s%d" % (q, i)))
            block = st.enter_context(nc.Block())

            def replay(ename, e):
                for waits, fn, tok in self.prog[ename]:
                    for sk, v in waits:
                        e.wait_ge(sems[sk], v)
                    if fn is None:
                        continue
                    inst = fn(e)
                    if isinstance(tok[0], tuple):
                        inst.then_inc(sems[tok[0]], 16)
                    else:
                        inst.then_inc(sems[tok[0]], 1)

            block.tensor(lambda e: replay('pe', e))
            block.scalar(lambda e: replay('act', e))
            block.vector(lambda e: replay('dve', e))
            block.gpsimd(lambda e: replay('pool', e))
            block.sync(lambda e: replay('sp', e))


class Arena:
    def __init__(self, t, nbytes):
        self.t = t
        self.nbytes = nbytes
        self.off = 0
        self.stack = []

    def push(self):
        self.stack.append(self.off)

    def pop(self):
        self.off = self.stack.pop()

    def alloc(self, shape, dtype):
        n = 1
        for s in shape:
            n *= s
        nb = n * DSZ[dtype]
        nb = (nb + 63) // 64 * 64
        assert self.off + nb <= self.nbytes, ("arena overflow", self.off, nb, self.nbytes)
        a = self.t[:, self.off // 4:(self.off + nb) // 4]
        self.off += nb
        if dtype != F32:
            a = a.bitcast(dtype)
        a = a[:, 0:n]
        if len(shape) == 2:
            a = a.rearrange("p (a b) -> p a b", a=shape[0])
        elif len(shape) == 3:
            a = a.rearrange("p (a b c) -> p a b c", a=shape[0], b=shape[1])
        elif len(shape) == 4:
            a = a.rearrange("p (a b c d) -> p a b c d", a=shape[0], b=shape[1], c=shape[2])
        return a


def key_tiles(i):
    if i == 0:
        return [(j, j) for j in range(4)]
    if i == 1:
        return [(j, 4 + j) for j in range(4)]
    return [(i + d, 8 + d + 2) for d in range(-2, 3)]


def build_nc(debug=None, stop_after=None):
    nc = bass.Bass("TRN2", target_bir_lowering=False)
    S = Sched(nc)
    dbg_outs = {}

    def din(name, shape):
        t = nc.dram_tensor(name, list(shape), F32, kind="ExternalInput")
        S.dram.add(name)
        return t.ap()

    xe = din("xe", [XE_ROWS, D])
    d_vec = din("vecs", [128, 64])
    d_valid = din("valid", [128, 2])
    d_gateb = din("gate_b", [128, 16])
    d_cmask = din("cmask", [128, 6, 128])
    d_mb = din("mb", [32, 8])
    d_bt = din("bt", [4, 128, 13 * 2 * 128])
    d_wxm = din("wxm", [128, 8 * 1024])
    d_wo = din("wo", [8, 128, 8 * 128])
    d_wg = din("wg", [128, 8 * 16])
    d_wq = din("wqm", [128, 4 * 2 * 128])
    d_wk = din("wkm", [128, 4 * 2 * 128])
    d_wnq = din("wnq", [4, 128, 8 * 128])
    d_wnk = din("wnk", [4, 128, 8 * 128])
    d_wnv = din("wnv", [128, 8 * 512])
    d_wga = din("wga", [8, 128, 8 * 128])
    d_wgb = din("wgb", [8, 128, 8 * 128])
    d_wa = din("wa", [8, 128, 8 * 128])
    d_wb = din("wb", [8, 128, 4 * 128])
    d_wout = din("wout", [128, 8 * 1024])
    d_wf1 = din("wf1", [32, 128, 8 * 128])
    d_wf2 = din("wf2", [32, 128, 1024])
    y_out = nc.dram_tensor("y", [NOWN, D], F32, kind="ExternalOutput")
    S.dram.add("y")
    y_out = y_out.ap()

    with contextlib.ExitStack() as st:
        ARENA_BYTES = 207 * 1024
        arena_t = st.enter_context(nc.sbuf_tensor("arena", [128, ARENA_BYTES // 4], F32))
        AR = Arena(arena_t, ARENA_BYTES)
        ps = [st.enter_context(nc.psum_tensor("ps%d" % i, [128, 512], F32)) for i in range(8)]
        psb = [p[:].bitcast(BF16) for p in ps]

        def dbg(name, ap, shape):
            if debug is None or name not in debug:
                return
            t = nc.dram_tensor("dbg_" + name, list(shape), F32, kind="ExternalOutput")
            S.dram.add("dbg_" + name)
            dbg_outs[name] = S.dma('pool', t.ap(), ap)

        vec = AR.alloc([64], F32)
        valid = AR.alloc([2], F32)
        gateb = AR.alloc([16], F32)
        cmf = AR.alloc([4, 128], F32)
        cmb = AR.alloc([4, 128], BF16)
        mbc = AR.alloc([8], F32)
        S.dma('sp', vec, d_vec)
        S.dma('sp', valid, d_valid)
        S.dma('sp', gateb, d_gateb)
        S.dma('sp', cmf, d_cmask[:, 0:4, :])
        S.dma('pool', cmb[:, 0:2, :], d_cmask[:, 0:2, :])
        S.dma('pool', cmb[:, 2:4, :], d_cmask[:, 4:6, :])
        S.dma('sp', mbc[0:32, :], d_mb)
        triA_f, triB_f = cmf[:, 0, :], cmf[:, 1, :]
        ones_f = [cmf[:, 2, :], cmf[:, 3, :]]
        maskA, maskB, ident, blk64 = cmb[:, 0, :], cmb[:, 1, :], cmb[:, 2, :], cmb[:, 3, :]
        g1 = vec[:, 0:8]
        cw = [vec[:, 8:16], vec[:, 16:24], vec[:, 24:32]]
        cb = vec[:, 32:40]
        mng = vec[:, 40:48]
        skp = vec[:, 48:56]
        g2 = vec[:, 56:64]
        qkg = AR.alloc([2], F32)
        d_qkg = din("qkg", [128, 2])
        S.dma('sp', qkg, d_qkg)

        dcw = AR.alloc([3, 8, 128], BF16)
        for k3 in range(3):
            for et in range(8):
                S.ts('dve', dcw[:, k3, et, :], ident, cw[k3][:, et:et + 1], None, ALU.mult)
        U_A = AR.alloc([4, 257], F32)
        U_B = AR.alloc([4, 257], F32)
        dprevA = AR.alloc([4], F32)
        dprevB = AR.alloc([4], F32)
        OFF_XNT = AR.off
        xnT = AR.alloc([8, NXC + 1], BF16)
        S.memset('dve', xnT[:, :, 0:1], 0.0)
        xnTm = AR.alloc([8, 32], BF16)
        MIX_OFF = ARENA_BYTES - 8 * NOWN * 2

        wg = AR.alloc([8, 16], BF16)
        wq = AR.alloc([4, 2, 128], BF16)
        wk = AR.alloc([4, 2, 128], BF16)
        S.dma('pool', wg, d_wg.rearrange("p (a b) -> p a b", a=8))
        S.dma('pool', wq, d_wq.rearrange("p (a b c) -> p a b c", a=4, b=2))
        S.dma('pool', wk, d_wk.rearrange("p (a b c) -> p a b c", a=4, b=2))
        AR.push()
        xt_buf = [AR.alloc([D], F32) for _ in range(4)]
        xnb_buf = [AR.alloc([D], BF16) for _ in range(4)]
        xt_buf2 = [AR.alloc([D], F32) for _ in range(4)]
        xnb_buf2 = [AR.alloc([D], BF16) for _ in range(4)]
        ss_buf2 = AR.alloc([8], F32)
        sq_junk = AR.alloc([D], BF16)
        ss_buf = AR.alloc([8], F32)
        xcnt = [0]

        def emit_xnT_batch(items, gvec, xts=None, xnbs=None, ssb=None):
            nb = len(items)
            assert nb <= 4
            xts = xts or xt_buf
            xnbs = xnbs or xnb_buf
            ssb = ssb if ssb is not None else ss_buf
            nmax = max(n for _, n, _ in items)
            for i, (row0, n, dst) in enumerate(items):
                S.dma('sp', xts[i][0:n, :], xe[row0:row0 + n, :])
            for i, (row0, n, dst) in enumerate(items):
                S.act(sq_junk[0:n, :], xts[i][0:n, :], AF.Square, accum=ssb[0:n, i:i + 1])
            rs = ssb[0:nmax, 4:4 + nb]
            S.ts('dve', rs, ssb[0:nmax, 0:nb], 1.0 / D, EPS, ALU.mult, ALU.add)
            S.act(rs, rs, AF.Sqrt)
            S.recip(rs, rs)
            for i, (row0, n, dst) in enumerate(items):
                S.ts('dve', xnbs[i][0:n, :], xts[i][0:n, :], ssb[0:n, 4 + i:5 + i], None, ALU.mult)
            for i, (row0, n, dst) in enumerate(items):
                pb = psb[6 + (xcnt[0] % 2)]
                xcnt[0] += 1
                for dt in range(8):
                    S.transpose(pb[:, dt * 128:dt * 128 + n], xnbs[i][0:n, dt * 128:(dt + 1) * 128],
                                ident[0:n, 0:n])
                pv = pb[:, :].rearrange("p (a b) -> p a b", a=8)[:, :, 0:n]
                S.tt('dve', dst, pv, gvec.unsqueeze(2).to_broadcast([128, 8, n]), ALU.mult)

        xt_h = AR.alloc([D], F32)
        xnb_h = AR.alloc([D], BF16)
        ss_m = AR.alloc([8], F32)

        def emit_xnT(row0, n, dst, gvec):
            emit_xnT_batch([(row0, n, dst)], gvec, xts=[xt_h], xnbs=[xnb_h], ssb=ss_m)

        def emit_h1_xn2T(tile_rows, h1, dst):
            pass

        if stop_after == 'consts':
            dbg("vec", vec, [128, 64])
            return finish(nc, S, y_out, dbg_outs, None)
        if stop_after == 'x16':
            emit_xnT(1, 16, xnT[:, :, 1:17], g1)
            dbg("xnT", xnT[:, 0, 0:NXC], [128, NXC])
            return finish(nc, S, y_out, dbg_outs, None)
        if stop_after and stop_after.startswith('xn'):
            for k in range(int(stop_after[2:])):
                emit_xnT(OWN0 + 128 * k, 128, xnT[:, :, OWN0 + 128 * k:OWN0 + 128 * k + 128], g1)
            dbg("xnT", xnT[:, 0, 0:NXC], [128, NXC])
            return finish(nc, S, y_out, dbg_outs, None)
        if stop_after == 'x1':
            emit_xnT(OWN0, 128, xnT[:, :, OWN0:OWN0 + 128], g1)
            dbg("xnT", xnT[:, 0, 0:NXC], [128, NXC])
            return finish(nc, S, y_out, dbg_outs, None)
        emit_xnT_batch([(1, 16, xnT[:, :, 1:17]), (1, 16, xnTm[:, :, 0:16]), (POST0, 16, xnTm[:, :, 16:32])], g1)
        def p0gen():
            for k0 in range(0, NT + 2, 4):
                alt = (k0 // 4) % 2 == 1
                emit_xnT_batch([(OWN0 + 128 * k, 128, xnT[:, :, OWN0 + 128 * k:OWN0 + 128 * k + 128])
                                for k in range(k0, min(k0 + 4, NT + 2))], g1,
                               xts=xt_buf2 if alt else None, xnbs=xnb_buf2 if alt else None,
                               ssb=ss_buf2 if alt else None)
                yield

        AR.push()
        wxm = AR.alloc([8, 1024], BF16)
        for q4 in range(4):
            S.dma('pool', wxm[:, 2 * q4:2 * q4 + 2, :],
                  d_wxm.rearrange("p (a b) -> p a b", a=8)[:, 2 * q4:2 * q4 + 2, :])
        p1bufs = []
        for _ in range(2):
            p1bufs.append(dict(
                xoT=AR.alloc([8, 514], BF16), xmT=AR.alloc([8, 514], BF16), xcT=AR.alloc([8, 512], BF16),
                ktok1=AR.alloc([4, 512], BF16), vp1=AR.alloc([4, 4, 257], BF16),
                gsb=AR.alloc([4, 16], F32), sp1=AR.alloc([4, 4], F32), w1=AR.alloc([4, 4], F32),
                d1=AR.alloc([2, 4, 4], F32), tmp16=AR.alloc([4, 4], F32)))
        p1cnt = [0]
        carry = [None]

        def seq_block(direction, row0, ntile, npp, ncol, is_mini, vflag, first):
            ntok = ntile * npp
            B_ = p1bufs[p1cnt[0] % 2]
            p1cnt[0] += 1
            xoT, xmT, xcT, ktok1, vp1 = B_['xoT'], B_['xmT'], B_['xcT'], B_['ktok1'], B_['vp1']
            gsb, sp1, w1, d1, tmp16 = B_['gsb'], B_['sp1'], B_['w1'], B_['d1'], B_['tmp16']
            if direction == 'A':
                U, dprev, tri, li0, f0 = U_A, dprevA, triA_f, 0, 4
            else:
                U, dprev, tri, li0, f0 = U_B, dprevB, triB_f, 8, 12
            if is_mini:
                emit_xnT(row0, ncol, xoT[:, :, 0:ncol], g1)
            else:
                emit_xnT_batch([(row0 + 1 + 128 * j, 128, xoT[:, :, 1 + 128 * j:1 + 128 * j + 128])
                                for j in range(ntile)], g1)
                emit_xnT(row0, 1, xoT[:, :, 0:1], g1)
                S.copy('dve', xmT[:, :, 512:514], carry[0])
            carry[0] = xmT[:, :, 0:2]
            nmain = min(512, ncol)
            yield
            pg = ps[2]
            for j in range(ntile):
                for dt in range(8):
                    S.mm(pg[0:npp, 64 + 16 * j:64 + 16 * j + 16], xoT[:, dt, 1 + j * npp:1 + (j + 1) * npp],
                         wg[:, dt, :], start=(dt == 0), stop=(dt == 7))
            pgv = pg[0:npp, 64:64 + 16 * ntile].rearrange("p (a b) -> p a b", a=ntile)
            S.tt('dve', gsb[0:npp, 0:ntile, :], pgv, gateb[0:npp, :].unsqueeze(1).to_broadcast([npp, ntile, 16]),
                 ALU.add)
            spv = sp1[0:npp, 0:ntile, :]
            S.act(spv, gsb[0:npp, 0:ntile, f0:f0 + 4], AF.Exp, scale=-1.0)
            S.act(spv, spv, AF.Ln, bias=1.0, scale=1.0)
            pc = ps[3]
            S.mm(pc[0:npp, 128:128 + 4 * ntile], tri[0:npp, 0:npp], spv)
            nhb = 1 if is_mini else 2
            for hb in range(nhb):
                lh = ones_f[hb][0:npp, :] if not is_mini else ones_f[0][0:npp, :]
                S.mm(pc[:, 192 + 16 * hb:192 + 16 * hb + 4 * ntile], lh, spv)
            csv = pc[0:npp, 128:128 + 4 * ntile].rearrange("p (a b) -> p a b", a=ntile)
            S.tt('dve', tmp16[0:npp, 0:ntile, :], gsb[0:npp, 0:ntile, li0:li0 + 4], csv, ALU.add)
            S.act(w1[0:npp, 0:ntile, :], tmp16[0:npp, 0:ntile, :], AF.Exp)
            if vflag is not None:
                S.ts('dve', w1[0:npp, 0:ntile, :], w1[0:npp, 0:ntile, :], vflag[0:npp, :], None, ALU.mult)
            for hb in range(nhb):
                S.act(d1[:, hb, 0:ntile, :],
                      pc[:, 192 + 16 * hb:192 + 16 * hb + 4 * ntile].rearrange("p (a b) -> p a b", a=ntile),
                      AF.Exp, scale=-1.0)
            yield
            for et in range(8):
                pm = ps[et % 4]
                for dt in range(8):
                    S.mm(pm[:, 0:nmain], wxm[:, dt, et * 128:(et + 1) * 128], xoT[:, dt, 0:nmain],
                         start=(dt == 0), stop=(dt == 7))
                S.copy('act', xmT[:, et, 0:nmain], pm[:, 0:nmain])
            for j in range(ntile):
                for half in range(2):
                    pv_ = ps[4 + half]
                    for dt in range(8):
                        S.mm(pv_[0:npp, :], xoT[:, dt, 1 + j * npp:1 + (j + 1) * npp],
                             wxm[:, dt, 512 * half:512 * half + 512], start=(dt == 0), stop=(dt == 7))
                    S.tt('dve', vp1[0:npp, j, 2 * half:2 * half + 2, 0:256],
                         pv_[0:npp, :].rearrange("p (a b) -> p a b", a=2),
                         w1[0:npp, j, 2 * half:2 * half + 2].unsqueeze(2).to_broadcast([npp, 2, 256]), ALU.mult)
            S.copy('dve', vp1[0:npp, 0:ntile, :, 256], w1[0:npp, 0:ntile, :])
            yield
            for et in range(8):
                pcv = ps[6 + et % 2]
                for k3 in range(3):
                    S.mm(pcv[:, 0:ntok], dcw[:, k3, et, :], xmT[:, et, k3:k3 + ntok], start=(k3 == 0), stop=(k3 == 2))
                S.act(xcT[:, et, 0:ntok], pcv[:, 0:ntok], AF.Silu, bias=cb[:, et:et + 1], scale=1.0)
            yield
            for j in range(ntile):
                pk = ps[3]
                for h in range(4):
                    for e2 in range(2):
                        S.mm(pk[0:npp, h * 128:(h + 1) * 128], xcT[:, 2 * h + e2, j * npp:(j + 1) * npp],
                             wk[:, h, e2, :], start=(e2 == 0), stop=(e2 == 1))
                S.act(ktok1[0:npp, j, :], pk[0:npp, :], AF.Copy, scale=DK ** -0.5)
            yield
            order = []
            for j in range(ntile):
                for hb in range(nhb):
                    order.append((j, hb))
            if direction == 'B':
                order = order[::-1]
            for (j, hb) in order:
                p0 = 64 * hb
                cn = npp if is_mini else 64
                for h in range(4):
                    pst = ps[h]
                    S.mm(pst[:, 0:257], ktok1[p0:p0 + cn, j, h * 128:(h + 1) * 128], vp1[p0:p0 + cn, j, h, :])
                    if first:
                        S.copy('dve', U[:, h, :], pst[:, 0:257])
                    else:
                        S.stt(U[:, h, :], U[:, h, :], dprev[:, h:h + 1], pst[:, 0:257], ALU.mult, ALU.add)
                S.copy('dve', dprev, d1[:, hb, j, :])
                first = False

        gens = [seq_block('A', 0, 1, 16, 18, True, valid[:, 0:1], True),
                seq_block('B', POST0 - 1, 1, 16, 18, True, valid[:, 1:2], True), p0gen()]
        while gens:
            for g_ in list(gens):
                try:
                    next(g_)
                except StopIteration:
                    gens.remove(g_)
        dbg("xnT", xnT[:, 0, 0:NXC], [128, NXC])
        if stop_after == 'phase0':
            return finish(nc, S, y_out, dbg_outs, None)

        def big_bufs(bk):
            return p1bufs[(bk + 1) % 2]

        def X_pre(bk):
            row0 = OTH0 + 512 * bk - 1
            for j in range(4):
                S.dma('sp', xt_buf[j], xe[row0 + 1 + 128 * j:row0 + 1 + 128 * j + 128, :])
            S.dma('sp', xt_h[0:1, :], xe[row0:row0 + 1, :])
            for j in range(4):
                S.act(sq_junk, xt_buf[j], AF.Square, accum=ss_buf[:, j:j + 1])
            S.act(sq_junk[0:1, :], xt_h[0:1, :], AF.Square, accum=ss_h[0:1, 0:1])
            rs = ss_buf[:, 4:8]
            S.ts('dve', rs, ss_buf[:, 0:4], 1.0 / D, EPS, ALU.mult, ALU.add)
            S.act(rs, rs, AF.Sqrt)
            S.recip(rs, rs)
            rh = ss_h[0:1, 1:2]
            S.ts('dve', rh, ss_h[0:1, 0:1], 1.0 / D, EPS, ALU.mult, ALU.add)
            S.act(rh, rh, AF.Sqrt)
            S.recip(rh, rh)
            for j in range(4):
                S.ts('dve', xnb_buf[j], xt_buf[j], ss_buf[:, 4 + j:5 + j], None, ALU.mult)
            S.ts('dve', xnb_h[0:1, :], xt_h[0:1, :], rh, None, ALU.mult)

        def X_post(bk):
            B_ = big_bufs(bk)
            xoT, xmT = B_['xoT'], B_['xmT']
            for j in range(4):
                pb = psb[6 + (j % 2)]
                for dt in range(8):
                    S.transpose(pb[:, dt * 128:dt * 128 + 128], xnb_buf[j][:, dt * 128:(dt + 1) * 128], ident)
                S.tt('dve', xoT[:, :, 1 + 128 * j:1 + 128 * j + 128], pb[:, :].rearrange("p (a b) -> p a b", a=8),
                     g1.unsqueeze(2).to_broadcast([128, 8, 128]), ALU.mult)
            pb = psb[6]
            for dt in range(8):
                S.transpose(pb[:, dt * 128:dt * 128 + 1], xnb_h[0:1, dt * 128:(dt + 1) * 128], ident[0:1, 0:1])
            S.tt('dve', xoT[:, :, 0:1], pb[:, :].rearrange("p (a b) -> p a b", a=8)[:, :, 0:1],
                 g1.unsqueeze(2).to_broadcast([128, 8, 1]), ALU.mult)
            S.copy('dve', xmT[:, :, 512:514], carry[0])
            carry[0] = xmT[:, :, 0:2]

        def G_(bk):
            B_ = big_bufs(bk)
            xoT, gsb, sp1, w1, d1, tmp16 = B_['xoT'], B_['gsb'], B_['sp1'], B_['w1'], B_['d1'], B_['tmp16']
            pg = ps[0]
            for j in range(4):
                for dt in range(8):
                    S.mm(pg[:, 16 * j:16 * j + 16], xoT[:, dt, 1 + j * 128:1 + (j + 1) * 128], wg[:, dt, :],
                         start=(dt == 0), stop=(dt == 7))
            S.tt('dve', gsb, pg[:, 0:64].rearrange("p (a b) -> p a b", a=4),
                 gateb.unsqueeze(1).to_broadcast([128, 4, 16]), ALU.add)
            S.act(sp1, gsb[:, :, 12:16], AF.Exp, scale=-1.0)
            S.act(sp1, sp1, AF.Ln, bias=1.0, scale=1.0)
            pc = ps[1]
            S.mm(pc[:, 0:16], triB_f, sp1)
            for hb in range(2):
                S.mm(pc[:, 64 + 16 * hb:64 + 16 * hb + 16], ones_f[hb], sp1)
            S.tt('dve', tmp16, gsb[:, :, 8:12], pc[:, 0:16].rearrange("p (a b) -> p a b", a=4), ALU.add)
            S.act(w1, tmp16, AF.Exp)
            for hb in range(2):
                S.act(d1[:, hb, :, :], pc[:, 64 + 16 * hb:64 + 16 * hb + 16].rearrange("p (a b) -> p a b", a=4),
                      AF.Exp, scale=-1.0)

        def M_et(bk, et):
            B_ = big_bufs(bk)
            pm = ps[et % 4]
            for dt in range(8):
                S.mm(pm[:, :], wxm[:, dt, et * 128:(et + 1) * 128], B_['xoT'][:, dt, 0:512],
                     start=(dt == 0), stop=(dt == 7))
            S.copy('act', B_['xmT'][:, et, 0:512], pm[:, :])

        def V_(bk):
            B_ = big_bufs(bk)
            xmT, vp1, w1 = B_['xmT'], B_['vp1'], B_['w1']
            for j in range(4):
                pb = psb[4 + j % 2]
                for et in range(8):
                    S.transpose(pb[:, et * 128:(et + 1) * 128], xmT[:, et, 1 + 128 * j:1 + 128 * j + 128], ident)
                S.tt('dve', vp1[:, j, :, 0:256], pb[:, :].rearrange("p (a b) -> p a b", a=4),
                     w1[:, j, :].unsqueeze(2).to_broadcast([128, 4, 256]), ALU.mult)
            S.copy('dve', vp1[:, :, :, 256], w1)

        def C_(bk):
            B_ = big_bufs(bk)
            for et in range(8):
                pcv = ps[6 + et % 2]
                for k3 in range(3):
                    S.mm(pcv[:, :], dcw[:, k3, et, :], B_['xmT'][:, et, k3:k3 + 512], start=(k3 == 0), stop=(k3 == 2))
                S.act(B_['xcT'][:, et, :], pcv[:, :], AF.Silu, bias=cb[:, et:et + 1], scale=1.0)

        def K_(bk):
            B_ = big_bufs(bk)
            for j in range(4):
                pk = ps[4 + j % 2]
                for h in range(4):
                    for e2 in range(2):
                        S.mm(pk[:, h * 128:(h + 1) * 128], B_['xcT'][:, 2 * h + e2, j * 128:(j + 1) * 128],
                             wk[:, h, e2, :], start=(e2 == 0), stop=(e2 == 1))
                S.act(B_['ktok1'][:, j, :], pk[:, :], AF.Copy, scale=DK ** -0.5)

        def S_step(bk, i):
            B_ = big_bufs(bk)
            j, hb = [(jj, hh) for jj in range(4) for hh in range(2)][::-1][i]
            p0 = 64 * hb
            for h in range(4):
                pst = ps[4 + h]
                S.mm(pst[:, 0:257], B_['ktok1'][p0:p0 + 64, j, h * 128:(h + 1) * 128], B_['vp1'][p0:p0 + 64, j, h, :])
                S.stt(U_B[:, h, :], U_B[:, h, :], dprevB[:, h:h + 1], pst[:, 0:257], ALU.mult, ALU.add)
            S.copy('dve', dprevB, B_['d1'][:, hb, j, :])

        ss_h = AR.alloc([2], F32)
        bks = [3, 2, 1, 0]
        X_pre(3)
        X_post(3)
        G_(3)
        for bi, bk in enumerate(bks):
            prev = bks[bi - 1] if bi > 0 else None
            nxt = bks[bi + 1] if bi + 1 < 4 else None
            for et in range(8):
                M_et(bk, et)
                if prev is not None:
                    S_step(prev, et)
            V_(bk)
            if nxt is not None:
                X_pre(nxt)
            C_(bk)
            if nxt is not None:
                X_post(nxt)
            K_(bk)
            if nxt is not None:
                G_(nxt)
        for i in range(8):
            S_step(0, i)
        dbg("U_A", U_A[:, 0, :], [128, 257])
        dbg("U_B", U_B[:, 0, :], [128, 257])
        dbg("dprevA", dprevA, [128, 4])
        dbg("dprevB", dprevB, [128, 4])
        AR.pop()
        AR.pop()
        yaT = AR.alloc([8, NOWN], BF16)

        if stop_after == 'phase1':
            return finish(nc, S, y_out, dbg_outs, None)

        AR.push()
        gso = AR.alloc([NT, 16], F32)
        spo = AR.alloc([2, NT, 4], F32)
        wgt = AR.alloc([2, NT, 4], F32)
        einv = AR.alloc([2, NT, 4], F32)
        dch = AR.alloc([2, 2, NT, 4], F32)
        tmpg = AR.alloc([NT, 4], F32)
        pg = ps[2]
        for k in range(NT):
            for dt in range(8):
                S.mm(pg[:, 16 * k:16 * k + 16], xnT[:, dt, OWN0 + 128 * k:OWN0 + 128 * k + 128], wg[:, dt, :],
                     start=(dt == 0), stop=(dt == 7))
        S.tt('dve', gso, pg[:, 0:256].rearrange("p (a b) -> p a b", a=NT),
             gateb.unsqueeze(1).to_broadcast([128, NT, 16]), ALU.add)
        for di in range(2):
            f0 = 4 + 8 * di
            li0 = 8 * di
            S.act(spo[:, di, :, :], gso[:, :, f0:f0 + 4], AF.Exp, scale=-1.0)
            S.act(spo[:, di, :, :], spo[:, di, :, :], AF.Ln, bias=1.0, scale=1.0)
            pc = ps[3]
            S.mm(pc[:, 0:64], triA_f if di == 0 else triB_f, spo[:, di, :, :])
            for hb in range(2):
                S.mm(pc[:, 64 + 64 * hb:128 + 64 * hb], ones_f[hb], spo[:, di, :, :])
            csv = pc[:, 0:64].rearrange("p (a b) -> p a b", a=NT)
            S.tt('dve', tmpg, gso[:, :, li0:li0 + 4], csv, ALU.add)
            S.act(wgt[:, di, :, :], tmpg, AF.Exp)
            S.act(einv[:, di, :, :], csv, AF.Exp)
            for hb in range(2):
                S.act(dch[:, di, hb, :, :], pc[:, 64 + 64 * hb:128 + 64 * hb].rearrange("p (a b) -> p a b", a=NT),
                      AF.Exp, scale=-1.0)
        dbg("wgtA", wgt[:, 0, :, :], [128, NT, 4])
        dbg("dchA", dch[:, 0, :, :, :], [128, 2, NT, 4])

        wxmh = AR.alloc([8, 256], BF16)
        wob = [AR.alloc([8, 128], BF16) for _ in range(2)]
        xmt = [[AR.alloc([514], BF16) for _ in range(2)] for _ in range(2)]
        xcThs = [AR.alloc([2, NOWN], BF16) for _ in range(2)]
        vpA = AR.alloc([NT, 257], BF16)
        vpB = AR.alloc([NT, 257], BF16)
        qT = AR.alloc([NOWN], BF16)
        kT = AR.alloc([NOWN], BF16)
        ktok = AR.alloc([NT, 128], BF16)
        P_A = AR.alloc([NT, 128], BF16)
        P_B = AR.alloc([NT, 128], BF16)
        hrA = AR.alloc([NT, 257], BF16)
        hrB = AR.alloc([NT, 257], BF16)
        Cbf = [[AR.alloc([257], BF16) for _ in range(2)] for _ in range(2)]
        rr = AR.alloc([2, NT], F32)
        t1 = [AR.alloc([256], F32) for _ in range(2)]
        hn = [AR.alloc([256], BF16) for _ in range(4)]
        sqj = AR.alloc([256], BF16)
        st8 = AR.alloc([NT, 2], F32)
        sohs = [AR.alloc([2, NOWN], BF16) for _ in range(2)]
        wocnt = [0]

        def head_pre(h):
            xcTh = xcThs[h % 2]
            soh = sohs[h % 2]
            S.dma('pool', wxmh, d_wxm.rearrange("p (a b) -> p a b", a=8)[:, :, 256 * h:256 * h + 256])
            for tb in (3, 2, 1, 0):
                c0 = OWN0 + 512 * tb
                for e2 in range(2):
                    pm = ps[e2]
                    xb = xmt[e2][tb % 2]
                    for dt in range(8):
                        S.mm(pm[:, :], wxmh[:, dt, 128 * e2:128 * e2 + 128], xnT[:, dt, c0 - 1:c0 + 511],
                             start=(dt == 0), stop=(dt == 7))
                    if tb == 3:
                        pa = ps[2]
                        for dt in range(8):
                            S.mm(pa[:, 2 * e2:2 * e2 + 2], wxmh[:, dt, 128 * e2:128 * e2 + 128],
                                 xnT[:, dt, c0 + 511:c0 + 513], start=(dt == 0), stop=(dt == 7))
                        S.copy('act', xb[:, 512:514], pa[:, 2 * e2:2 * e2 + 2])
                    else:
                        S.copy('dve', xb[:, 512:514], xmt[e2][(tb + 1) % 2][:, 0:2])
                    S.copy('act', xb[:, 0:512], pm[:, :])
                pvb = psb[4 + (tb % 2)]
                for kk in range(4):
                    for e2 in range(2):
                        S.transpose(pvb[:, 256 * kk + 128 * e2:256 * kk + 128 * e2 + 128],
                                    xmt[e2][tb % 2][:, 1 + 128 * kk:1 + 128 * kk + 128], ident)
                for kk in range(4):
                    k = 4 * tb + kk
                    S.act(vpA[:, k, 0:256], pvb[:, 256 * kk:256 * kk + 256], AF.Copy, scale=wgt[:, 0, k, h:h + 1])
                    S.ts('dve', vpB[:, k, 0:256], pvb[:, 256 * kk:256 * kk + 256], wgt[:, 1, k, h:h + 1], None, ALU.mult)
                for e2 in range(2):
                    xb = xmt[e2][tb % 2]
                    et = 2 * h + e2
                    pcv = ps[2 + e2]
                    for k3 in range(3):
                        S.mm(pcv[:, :], dcw[:, k3, et, :], xb[:, k3:k3 + 512], start=(k3 == 0), stop=(k3 == 2))
                    S.act(xcTh[:, e2, 512 * tb:512 * tb + 512], pcv[:, :], AF.Silu, bias=cb[:, et:et + 1], scale=1.0)
                yield
            S.copy('dve', vpA[:, :, 256], wgt[:, 0, :, h])
            S.copy('dve', vpB[:, :, 256], wgt[:, 1, :, h])
            for e2 in range(2):
                S.dma('pool', wob[e2], d_wo[2 * h + e2].rearrange("p (a b) -> p a b", a=8))

            def emit_o(e2, k4):
                po = ps[4 + (k4 % 2)]
                for dt in range(8):
                    S.mm(po[:, :], wob[e2][:, dt, :], xnT[:, dt, OWN0 + 512 * k4:OWN0 + 512 * k4 + 512],
                         start=(dt == 0), stop=(dt == 7))
                S.act(soh[:, e2, 512 * k4:512 * k4 + 512], po[:, :], AF.Sigmoid)

            for tb in range(4):
                pq = ps[2 * (tb % 2)]
                pk = ps[2 * (tb % 2) + 1]
                for e2 in range(2):
                    S.mm(pq[:, :], wq[:, h, e2, :], xcTh[:, e2, 512 * tb:512 * tb + 512], start=(e2 == 0),
                         stop=(e2 == 1))
                for e2 in range(2):
                    S.mm(pk[:, :], wk[:, h, e2, :], xcTh[:, e2, 512 * tb:512 * tb + 512], start=(e2 == 0),
                         stop=(e2 == 1))
                S.copy('dve', qT[:, 512 * tb:512 * tb + 512], pq[:, :])
                S.ts('dve', kT[:, 512 * tb:512 * tb + 512], pk[:, :], DK ** -0.5, None, ALU.mult)
                emit_o(0, tb)
                yield
            for k4 in range(4):
                pk = ps[6 + (k4 % 2)]
                for kk in range(4):
                    k = 4 * k4 + kk
                    for e2 in range(2):
                        S.mm(pk[:, 128 * kk:128 * kk + 128], xcTh[:, e2, 128 * k:128 * k + 128], wk[:, h, e2, :],
                             start=(e2 == 0), stop=(e2 == 1))
                S.ts('dve', ktok[:, 4 * k4:4 * k4 + 4, :], pk[:, :].rearrange("p (a b) -> p a b", a=4), DK ** -0.5,
                     None, ALU.mult)
                emit_o(1, k4)
                yield
            for k4 in range(4):
                psc = ps[k4 % 2]
                for kk in range(4):
                    k = 4 * k4 + kk
                    S.mm(psc[:, 128 * kk:128 * kk + 128], kT[:, 128 * k:128 * k + 128], qT[:, 128 * k:128 * k + 128])
                pv3 = psc[:, :].rearrange("p (a b) -> p a b", a=4)
                S.tt('dve', P_A[:, 4 * k4:4 * k4 + 4, :], pv3, maskA.unsqueeze(1).to_broadcast([128, 4, 128]), ALU.mult)
                S.tt('dve', P_B[:, 4 * k4:4 * k4 + 4, :], pv3, maskB.unsqueeze(1).to_broadcast([128, 4, 128]), ALU.mult)
                yield
            if h == 0:
                dbg("xcTh", xcTh[:, 0, :], [128, NOWN])
                dbg("qT", qT, [128, NOWN])
                dbg("vpA", vpA, [128, NT, 257])
                dbg("P_A", P_A, [128, NT, 128])
                dbg("ktok", ktok, [128, NT, 128])
            for e2 in range(2):
                et = 2 * h + e2
                S.ts('dve', xcTh[:, e2, :], xcTh[:, e2, :], skp[:, et:et + 1], None, ALU.mult)

        def head_chain(h):
            S.act(Cbf[0][0], U_A[:, h, :], AF.Copy, scale=dprevA[:, h:h + 1])
            S.act(Cbf[1][0], U_B[:, h, :], AF.Copy, scale=dprevB[:, h:h + 1])

            def chunk_of(step, di):
                c = step if di == 0 else 31 - step
                return c // 2, c % 2

            def emit_st(step):
                for di in range(2):
                    k, hb = chunk_of(step, di)
                    p0 = 64 * hb
                    vp = vpA if di == 0 else vpB
                    S.mm(ps[2 * di + step % 2][:, 0:257], ktok[p0:p0 + 64, k, :], vp[p0:p0 + 64, k, :])

            emit_st(0)
            for step in range(32):
                if step + 1 < 32:
                    emit_st(step + 1)
                for di in range(2):
                    k, hb = chunk_of(step, di)
                    p0 = 64 * hb
                    vp = vpA if di == 0 else vpB
                    Pm = P_A if di == 0 else P_B
                    pout = ps[4 + 2 * di + step % 2]
                    S.mm(pout[:, 0:257], Pm[p0:p0 + 64, k, :], vp[p0:p0 + 64, k, :], start=True, stop=False)
                    S.mm(pout[:, 0:257], qT[:, 128 * k:128 * k + 128], Cbf[di][step % 2], start=False, stop=True)
                for di in range(2):
                    k, hb = chunk_of(step, di)
                    p0 = 64 * hb
                    U = U_A if di == 0 else U_B
                    hr = hrA if di == 0 else hrB
                    if step == 0:
                        dpv = (dprevA if di == 0 else dprevB)[:, h:h + 1]
                    else:
                        kp, hbp = chunk_of(step - 1, di)
                        dpv = dch[:, di, hbp, kp, h:h + 1]
                    S.stt(U[:, h, :], U[:, h, :], dpv, ps[2 * di + step % 2][:, 0:257], ALU.mult, ALU.add)
                    if step + 1 < 32:
                        S.act(Cbf[di][(step + 1) % 2], U[:, h, :], AF.Copy, scale=dch[:, di, hb, k, h:h + 1])
                lag = [step - 1] if step >= 1 else []
                if step == 31:
                    lag.append(31)
                for s2 in lag:
                    for di in range(2):
                        k2, hb2 = chunk_of(s2, di)
                        hr2 = hrA if di == 0 else hrB
                        S.copy('act' if di == 0 else 'dve', hr2[64 * hb2:64 * hb2 + 64, k2, :],
                               ps[4 + 2 * di + s2 % 2][64 * hb2:64 * hb2 + 64, 0:257])
            if h == 0:
                dbg("hrA", hrA, [128, NT, 257])
                dbg("hrB", hrB, [128, NT, 257])

        def head_post(h):
            uuh = xcThs[h % 2]
            soh = sohs[h % 2]
            for di in range(2):
                hr = hrA if di == 0 else hrB
                S.act(rr[:, di, :], hr[:, :, 256], AF.Abs)
                S.tt('dve', rr[:, di, :], rr[:, di, :], einv[:, di, :, h], ALU.max)
                S.recip(rr[:, di, :], rr[:, di, :])
            for k in range(NT):
                b = k % 2
                S.ts('dve', t1[b], hrA[:, k, 0:256], rr[:, 0, k:k + 1], None, ALU.mult)
                S.stt(hrA[:, k, 0:256], hrB[:, k, 0:256], rr[:, 1, k:k + 1], t1[b], ALU.mult, ALU.add)
                S.act(sqj, hrA[:, k, 0:256], AF.Square, accum=st8[:, k, 0:1])
                if k % 2 == 1:
                    yield
            S.ts('dve', st8[:, :, 1], st8[:, :, 0], 1.0 / DV, EPS, ALU.mult, ALU.add)
            S.act(st8[:, :, 1], st8[:, :, 1], AF.Sqrt)
            S.recip(st8[:, :, 1], st8[:, :, 1])
            for k4 in range(4):
                pT = psb[(k4 % 2)]
                for kk in range(4):
                    k = 4 * k4 + kk
                    b = k % 4
                    S.act(hn[b], hrA[:, k, 0:256], AF.Copy, scale=st8[:, k, 1:2])
                    for e2 in range(2):
                        S.transpose(pT[:, 512 * e2 + 128 * kk:512 * e2 + 128 * kk + 128],
                                    hn[b][:, 128 * e2:128 * e2 + 128], ident)
                for e2 in range(2):
                    et = 2 * h + e2
                    w = (2 * k4 + e2) % 2
                    yv = yaT[:, et, 512 * k4:512 * k4 + 512]
                    S.stt(yv, pT[:, 512 * e2:512 * e2 + 512], mng[:, et:et + 1], uuh[:, e2, 512 * k4:512 * k4 + 512],
                          ALU.mult, ALU.add)
                    S.tt('dve', yv, yv, soh[:, e2, 512 * k4:512 * k4 + 512], ALU.mult)
                yield

        def drive(gens):
            gens = list(gens)
            while gens:
                for g_ in list(gens):
                    try:
                        next(g_)
                    except StopIteration:
                        gens.remove(g_)

        drive([head_pre(0)])
        for h in range(H_M):
            head_chain(h)
            drive([head_post(h)] + ([head_pre(h + 1)] if h + 1 < H_M else []))
        dbg("yaT", yaT, [128, 8, NOWN])
        AR.pop()
        if stop_after == 'mstage':
            return finish(nc, S, y_out, dbg_outs, None)
        ybT = AR.alloc([4, NOWN], BF16)

        AR.push()
        vext2 = AR.alloc([19, 8, 128], BF16)
        AR.push()
        wnv = AR.alloc([8, 512], BF16)
        S.dma('pool', wnv, d_wnv.rearrange("p (a b) -> p a b", a=8))
        AR.pop()
        wqp = AR.alloc([8, 128], BF16)
        wkp = AR.alloc([8, 128], BF16)
        btp = AR.alloc([2, 13, 128], BF16)
        qnT = AR.alloc([NOWN], BF16)
        knT = AR.alloc([2304 + 32], BF16)
        sqT = [AR.alloc([512], BF16) for _ in range(4)]
        rsT4 = AR.alloc([4, 512], F32)
        rsT = [rsT4[:, i, :] for i in range(4)]
        PT = [AR.alloc([768], BF16) for _ in range(2)]
        PTm = AR.alloc([NOWN], BF16)
        lnb = [AR.alloc([512], F32) for _ in range(2)]
        recq = [AR.alloc([512], F32) for _ in range(2)]
        outsb = [rsT4, rsT4]
        zer = AR.alloc([128], BF16)
        qz = [AR.alloc([NOWN], BF16) for _ in range(2)]
        epsq = AR.alloc([2], F32)
        S.memset('dve', zer, 0.0)
        S.memset('dve', epsq[:, 0:1], DH * EPS)
        S.memset('dve', epsq[:, 1:2], EPS)
        S.memset('dve', vext2[32:64, 18, :, :], 0.0)
        S.memset('dve', vext2[64:128, 18, :, :], 0.0)
        S.memset('dve', PTm[32:64, :], 0.0)
        S.memset('dve', PTm[64:128, :], 0.0)
        vv = vext2[:, :, :, :].rearrange("p j (a b) c -> p j a b c", b=2)
        S.memset('dve', vv[:, :, :, 0, 64:128], 1.0)
        S.memset('dve', vv[:, :, :, 1, 0:64], 1.0)
        for j in range(19):
            npk = 128 if j < 18 else 32
            pv_ = ps[j % 2]
            for dt in range(8):
                lh = xnT[:, dt, OWN0 + 128 * j:OWN0 + 128 * j + 128] if j < 18 else xnTm[:, dt, :]
                S.mm(pv_[0:npk, :], lh, wnv[:, dt, :], start=(dt == 0), stop=(dt == 7))
            src = pv_[0:npk, :].rearrange("p (a b c) -> p a b c", a=4, b=2)
            dst = vext2[0:npk, j, :, :].rearrange("p (a b) c -> p a b c", b=2)
            S.copy('act', dst[:, :, 0, 0:64], src[:, :, 0, :])
            S.copy('dve', dst[:, :, 1, 64:128], src[:, :, 1, :])

        def qk_norm(dst, wts, col0, ncols, which, cnt):
            b = cnt % 4
            pq = ps[b]
            pss = ps[4 + b]
            for dt in range(8):
                S.mm(pq[:, 0:ncols], wts[:, dt, :], col0[dt], start=(dt == 0), stop=(dt == 7))
            S.act(sqT[b][:, 0:ncols], pq[:, 0:ncols], AF.Square)
            S.mm(pss[:, 0:ncols], blk64, sqT[b][:, 0:ncols])
            if which == 0:
                S.act(rsT[b][:, 0:ncols], pss[:, 0:ncols], AF.Ln, bias=epsq[:, 0:1], scale=1.0)
            else:
                S.act(rsT[b][:, 0:ncols], pss[:, 0:ncols], AF.Ln, bias=epsq[:, 1:2], scale=1.0 / DH)
            S.act(rsT[b][:, 0:ncols], rsT[b][:, 0:ncols], AF.Exp, scale=-0.5)
            S.stt(dst, pq[:, 0:ncols], qkg[:, which:which + 1], rsT[b][:, 0:ncols], ALU.mult, ALU.mult)

        def bias_pos(i, j):
            if i == 0:
                return j
            if i == 1:
                return 4 + j
            return 10 - (j - i)

        Iof = {j: [i for i in range(NT) if j in [jj for jj, _ in key_tiles(i)]] for j in range(18)}
        ncnt = 0
        sc = 0
        for pr in range(4):
            S.dma('pool', wqp, d_wnq[pr].rearrange("p (a b) -> p a b", a=8))
            S.dma('pool', wkp, d_wnk[pr].rearrange("p (a b) -> p a b", a=8))
            S.dma('pool', btp, d_bt[pr].rearrange("p (a b c) -> p a b c", a=2, b=13))
            for tb in range(4):
                c0 = OWN0 + 512 * tb
                qk_norm(qnT[:, 512 * tb:512 * tb + 512], wqp, [xnT[:, dt, c0:c0 + 512] for dt in range(8)], 512, 0, ncnt)
                ncnt += 1
            for tb in range(5):
                c0 = OWN0 + 512 * tb
                n = 512 if tb < 4 else 256
                qk_norm(knT[:, 512 * tb:512 * tb + n], wkp, [xnT[:, dt, c0:c0 + n] for dt in range(8)], n, 1, ncnt)
                ncnt += 1
            qk_norm(knT[:, 2304:2336], wkp, [xnTm[:, dt, :] for dt in range(8)], 32, 1, ncnt)
            ncnt += 1
            if pr == 0:
                dbg("qnT", qnT, [128, NOWN])
                dbg("knT", knT, [128, 2336])
            for hh in range(2):
                S.memset('dve', qz[hh][64 - 64 * hh:128 - 64 * hh, :], 0.0)
                S.copy('dve', qz[hh][64 * hh:64 * hh + 64, :], qnT[64 * hh:64 * hh + 64, :])
            for hh in range(2):
                h = 2 * pr + hh
                bp = 64 * hh
                for b in range(4):
                    S.mm(ps[b][:, :], zer, qnT[:, 512 * b:512 * b + 512], start=True, stop=True)
                for b in range(4):
                    sA = ps[4 + 2 * (sc % 2)]
                    sc += 1
                    S.mm(sA[0:32, :], knT[:, 2304:2336], qz[hh][:, 512 * b:512 * b + 512])
                    S.act(PTm[0:32, 512 * b:512 * b + 512], sA[0:32, :], AF.Exp, bias=mbc[0:32, h:h + 1], scale=1.0)
                    S.mm(ps[b][:, :], vext2[:, 18, h, :], PTm[:, 512 * b:512 * b + 512], start=False, stop=False,
                         sgc=True)
                def emit_scores(j, slot):
                    I = Iof[j]
                    i0, n = I[0], len(I)
                    nq = 128 * n
                    sA = ps[4 + 2 * slot]
                    sB = ps[5 + 2 * slot]
                    pt = PT[slot]
                    segs = [(sA, 0, min(nq, 512))] + ([(sB, 512, nq)] if nq > 512 else [])
                    for (bank, lo, hi) in segs:
                        S.mm(bank[:, 0:hi - lo], knT[:, 128 * j:128 * j + 128],
                             qz[hh][:, 128 * i0 + lo:128 * i0 + hi], start=True, stop=False)
                        idxs = [ix for ix in range(n) if lo <= 128 * ix < hi]
                        runs = []
                        for ix in idxs:
                            p_ = bias_pos(I[ix], j)
                            if runs and runs[-1][1] + runs[-1][2] == p_ and runs[-1][0] + runs[-1][2] == ix:
                                runs[-1][2] += 1
                            else:
                                runs.append([ix, p_, 1])
                        for ri, (ix, p_, r) in enumerate(runs):
                            S.mm(bank[:, 128 * ix - lo:128 * (ix + r) - lo], ident, btp[:, hh, p_:p_ + r, :],
                                 start=False, stop=(ri == len(runs) - 1))
                        S.act(pt[:, lo:hi], bank[:, 0:hi - lo], AF.Exp)

                def emit_pv(j, slot):
                    I = Iof[j]
                    i0, n = I[0], len(I)
                    pt = PT[slot]
                    for b in range(4):
                        ilo, ihi = max(i0, 4 * b), min(i0 + n, 4 * b + 4)
                        if ilo >= ihi:
                            continue
                        S.mm(ps[b][:, 128 * (ilo - 4 * b):128 * (ihi - 4 * b)], vext2[:, j, h, :],
                             pt[:, 128 * (ilo - i0):128 * (ihi - i0)], start=False, stop=False, sgc=True)

                emit_scores(0, 0)
                for j in range(18):
                    if j + 1 < 18:
                        emit_scores(j + 1, (j + 1) % 2)
                    emit_pv(j, j % 2)
                osb = outsb[h % 2]
                for b in range(4):
                    S.copy('act' if b % 2 == 0 else 'dve', osb[:, b, :], ps[b][:, :])
                for b in range(4):
                    bb = b % 2
                    num0, den0 = (0, 64) if hh == 0 else (64, 0)
                    S.act(lnb[bb][num0:num0 + 64, :], osb[den0:den0 + 64, b, :], AF.Ln)
                    S.act(recq[bb][num0:num0 + 64, :], lnb[bb][num0:num0 + 64, :], AF.Exp, scale=-1.0)
                    S.tt('dve', ybT[num0:num0 + 64, pr, 512 * b:512 * b + 512], osb[num0:num0 + 64, b, :],
                         recq[bb][num0:num0 + 64, :], ALU.mult)
        dbg("ybT", ybT, [128, 4, NOWN])
        AR.pop()
        if stop_after == 'na':
            return finish(nc, S, y_out, dbg_outs, None)

        AR.push()
        assert AR.off < MIX_OFF
        mixT = arena_t[:, MIX_OFF // 4:ARENA_BYTES // 4].bitcast(BF16).rearrange("p (a b) -> p a b", a=8)
        wgab = [AR.alloc([8, 128], BF16) for _ in range(2)]
        wgbb = [AR.alloc([8, 128], BF16) for _ in range(2)]
        wab = [AR.alloc([8, 128], BF16) for _ in range(2)]
        wbb = [AR.alloc([4, 128], BF16) for _ in range(2)]
        sga = [AR.alloc([512], F32) for _ in range(2)]
        sgb = [AR.alloc([512], F32) for _ in range(2)]
        tg1 = [AR.alloc([512], F32) for _ in range(2)]
        tg2 = [AR.alloc([512], F32) for _ in range(2)]
        WOUT_OFF = MIX_OFF - 8 * 1024 * 2
        assert AR.off <= WOUT_OFF, (AR.off, WOUT_OFF)
        wout = arena_t[:, WOUT_OFF // 4:MIX_OFF // 4].bitcast(BF16).rearrange("p (a b) -> p a b", a=8)
        for q4 in range(4):
            S.dma('pool', wout[:, 2 * q4:2 * q4 + 2, :],
                  d_wout.rearrange("p (a b) -> p a b", a=8)[:, 2 * q4:2 * q4 + 2, :])
        gc = 0
        for blk in range(8):
            w = blk % 2
            S.dma('pool', wgab[w], d_wga[blk].rearrange("p (a b) -> p a b", a=8))
            S.dma('pool', wgbb[w], d_wgb[blk].rearrange("p (a b) -> p a b", a=8))
            S.dma('pool', wab[w], d_wa[blk].rearrange("p (a b) -> p a b", a=8))
            S.dma('pool', wbb[w], d_wb[blk].rearrange("p (a b) -> p a b", a=4))
            for tb in range(4):
                b = gc % 2
                gc += 1
                c0 = OWN0 + 512 * tb
                pga, pgb, pa_, pb_ = ps[4 * b], ps[4 * b + 1], ps[4 * b + 2], ps[4 * b + 3]
                for dt in range(8):
                    S.mm(pga[:, :], wgab[w][:, dt, :], xnT[:, dt, c0:c0 + 512], start=(dt == 0), stop=(dt == 7))
                for dt in range(8):
                    S.mm(pgb[:, :], wgbb[w][:, dt, :], xnT[:, dt, c0:c0 + 512], start=(dt == 0), stop=(dt == 7))
                for et in range(8):
                    S.mm(pa_[:, :], wab[w][:, et, :], yaT[:, et, 512 * tb:512 * tb + 512], start=(et == 0), stop=(et == 7))
                for c4 in range(4):
                    S.mm(pb_[:, :], wbb[w][:, c4, :], ybT[:, c4, 512 * tb:512 * tb + 512], start=(c4 == 0), stop=(c4 == 3))
                S.act(sga[b], pga[:, :], AF.Sigmoid)
                S.act(sgb[b], pgb[:, :], AF.Sigmoid)
                S.tt('dve', tg1[b], sga[b], pa_[:, :], ALU.mult)
                S.tt('dve', tg2[b], sgb[b], pb_[:, :], ALU.mult)
                S.tt('dve', mixT[:, blk, 512 * tb:512 * tb + 512], tg1[b], tg2[b], ALU.add)
        dbg("mixT", mixT, [128, 8, NOWN])
        AR.pop()
        if stop_after == 'g':
            return finish(nc, S, y_out, dbg_outs, None)

        AR.off = OFF_XNT
        h1 = AR.alloc([NT, 1024], F32)
        xn2T = AR.alloc([8, NOWN], BF16)
        OFF_OTMP = AR.off
        xq = [AR.alloc([1024], F32) for _ in range(4)]
        xn2b = [AR.alloc([1024], BF16) for _ in range(4)]
        sqj2 = AR.alloc([1024], BF16)
        ss2 = AR.alloc([16], F32)
        assert AR.off <= MIX_OFF - 16 * 1024, AR.off
        OFF_FFN = AR.off
        tcnt = [0]

        def O_proj(t0):
            for i in range(4):
                t = t0 + i
                S.dma('sp', xq[i], xe[OWN0 + 128 * t:OWN0 + 128 * t + 128, :])
            for i in range(4):
                t = t0 + i
                for half in range(2):
                    po = ps[(2 * i + half) % 6]
                    for dt in range(8):
                        S.mm(po[:, :], mixT[:, dt, 128 * t:128 * t + 128], wout[:, dt, 512 * half:512 * half + 512],
                             start=(dt == 0), stop=(dt == 7))
                    S.tt('dve', h1[:, t, 512 * half:512 * half + 512], po[:, :], xq[i][:, 512 * half:512 * half + 512],
                         ALU.add)
                S.act(sqj2, h1[:, t, :], AF.Square, accum=ss2[:, 8 * ((t0 // 4) % 2) + i:8 * ((t0 // 4) % 2) + i + 1])

        def O_norm(t0):
            o8 = 8 * ((t0 // 4) % 2)
            rsv = ss2[:, o8 + 4:o8 + 8]
            S.ts('dve', rsv, ss2[:, o8:o8 + 4], 1.0 / D, EPS, ALU.mult, ALU.add)
            S.act(rsv, rsv, AF.Sqrt)
            S.recip(rsv, rsv)
            for i in range(4):
                t = t0 + i
                S.ts('dve', xn2b[i], h1[:, t, :], ss2[:, o8 + 4 + i:o8 + 5 + i], None, ALU.mult)

        def O_tr(t0):
            for i in range(4):
                t = t0 + i
                pb = psb[6 + (tcnt[0] % 2)]
                tcnt[0] += 1
                for dt in range(8):
                    S.transpose(pb[:, dt * 128:dt * 128 + 128], xn2b[i][:, dt * 128:(dt + 1) * 128], ident)
                S.tt('dve', xn2T[:, :, 128 * t:128 * t + 128], pb[:, :].rearrange("p (a b) -> p a b", a=8),
                     g2.unsqueeze(2).to_broadcast([128, 8, 128]), ALU.mult)

        O_proj(0)
        for t0 in range(0, NT, 4):
            O_norm(t0)
            if t0 + 4 < NT:
                O_proj(t0 + 4)
            O_tr(t0)
        dbg("h1", h1, [128, NT, 1024])
        AR.off = WOUT_OFF
        wf2r = [AR.alloc([4, 1024], BF16) for _ in range(2)]
        AR.off = OFF_OTMP
        wf1r = [AR.alloc([4, 8, 128], BF16) for _ in range(2)]
        AR.off = MIX_OFF
        zTg = [AR.alloc([4, NOWN], BF16) for _ in range(2)]
        AR.off = OFF_FFN
        rl = [AR.alloc([512], BF16) for _ in range(2)]
        assert AR.off <= WOUT_OFF

        def ffn_load(G):
            S.dma('pool', wf1r[G % 2], d_wf1[4 * G:4 * G + 4].rearrange("a p (b c) -> p a b c", b=8))
            S.dma('pool', wf2r[G % 2], d_wf2[4 * G:4 * G + 4].rearrange("a p c -> p a c"))

        ffn_load(0)
        zc = 0
        bc = 0
        for G in range(8):
            if G + 1 < 8:
                ffn_load(G + 1)
            g2_ = G % 2
            for fbi in range(4):
                for tb in range(4):
                    b = zc % 2
                    pz = ps[zc % 8]
                    zc += 1
                    for dt in range(8):
                        S.mm(pz[:, :], wf1r[g2_][:, fbi, dt, :], xn2T[:, dt, 512 * tb:512 * tb + 512],
                             start=(dt == 0), stop=(dt == 7))
                    S.act(rl[b], pz[:, :], AF.Relu)
                    S.tt('dve', zTg[g2_][:, fbi, 512 * tb:512 * tb + 512], rl[b], rl[b], ALU.mult)
            for ts_ in range(4):
                for half in range(2):
                    bs = 4 * (bc % 2)
                    bc += 1
                    for fbi in range(4):
                        for k4 in range(4):
                            t = 4 * ts_ + k4
                            S.mm(ps[bs + k4][:, :], zTg[g2_][:, fbi, 128 * t:128 * t + 128],
                                 wf2r[g2_][:, fbi, 512 * half:512 * half + 512], start=(fbi == 0), stop=(fbi == 3))
                    for k4 in range(4):
                        t = 4 * ts_ + k4
                        hv = h1[:, t, 512 * half:512 * half + 512]
                        S.tt('dve', hv, ps[bs + k4][:, :], hv, ALU.add)
        out_toks = []
        for t in range(NT):
            out_toks.append(S.dma('sp', y_out[128 * t:128 * t + 128, :], h1[:, t, :]))
        return finish(nc, S, y_out, dbg_outs, out_toks)


def finish(nc, S, y_out, dbg_outs, out_toks):
    toks = list(dbg_outs.values())
    if out_toks:
        toks += out_toks
    S.wait_tokens('sp', toks)
    S.emit()
    return nc


def _colvec(v):
    return np.ascontiguousarray(np.asarray(v, np.float32).reshape(8, 128).T)


def _rows_ptc(w):
    T = w.shape[0] // 128
    return np.ascontiguousarray(w.reshape(T, 128, w.shape[1]).transpose(1, 0, 2).reshape(128, -1))


def _col_blocks(w, c0, nblk, bw=128):
    return np.ascontiguousarray(np.stack([_rows_ptc(w[:, c0 + bw * i:c0 + bw * (i + 1)]) for i in range(nblk)]))


def _na_tables(hf, rpb, meta_bias):
    def tile(i, j):
        out = np.full((NH, 128, 128), NEG, np.float32)
        cl = np.arange(64)
        for kr in range(2):
            for qr in range(2):
                krow_l, qrow_l = 2 * j + kr, 2 * i + qr
                if hf == 0:
                    krow, qrow, kcol, qcol = krow_l, qrow_l, cl, cl
                else:
                    krow, qrow, kcol, qcol = 63 - krow_l, 63 - qrow_l, 63 - cl, 63 - cl
                r0 = min(max(qrow - 4, 0), 56)
                if not (r0 <= krow < r0 + 8):
                    continue
                win0 = np.clip(qcol - 8, 0, 48)
                ok = (kcol[:, None] >= win0[None, :]) & (kcol[:, None] < win0[None, :] + 16)
                dr = krow - qrow + 7
                dc = np.clip(kcol[:, None] - qcol[None, :], -15, 15) + 15
                vals = rpb[:, dr, dc]
                out[:, kr * 64:(kr + 1) * 64, qr * 64:(qr + 1) * 64] = np.where(ok[None], vals, NEG)
        return out
    kinds = [tile(0, j) for j in range(4)] + [tile(1, j) for j in range(4)] + [tile(8, 8 + d) for d in (2, 1, 0, -1, -2)]
    BT = np.stack(kinds)
    bt = BT.reshape(13, 4, 2, 128, 128).transpose(1, 3, 2, 0, 4).reshape(4, 128, 13 * 2 * 128)
    MB = np.full((NH, 32), NEG, np.float32)
    if hf == 0:
        MB[:, 0:16] = meta_bias
    else:
        MB[:, 16:32] = meta_bias[:, ::-1]
    return np.ascontiguousarray(bt), np.ascontiguousarray(MB.T)


def _const_masks():
    j = np.arange(128)[:, None]
    t = np.arange(128)[None, :]
    same = (j // 64) == (t // 64)
    cm = np.zeros((128, 6, 128), np.float32)
    cm[:, 0, :] = same & (j <= t)
    cm[:, 1, :] = same & (j >= t)
    cm[:, 2, :] = (j < 64) & (t >= 0)
    cm[:, 3, :] = (j >= 64) & (t >= 0)
    cm[:, 4, :] = (j == t)
    cm[:, 5, :] = same
    return cm


def prep_inputs(inp):
    f = lambda a: np.ascontiguousarray(np.asarray(a, np.float32))
    w_in = f(inp['w_in'])
    shared = {
        'wxm': _rows_ptc(w_in[:, 0:1024]),
        'wo': _col_blocks(w_in, 1024, 8),
        'wqm': np.ascontiguousarray(f(inp['mlstm_wq']).reshape(4, 2, 128, 128).transpose(2, 0, 1, 3).reshape(128, -1)),
        'wkm': np.ascontiguousarray(f(inp['mlstm_wk']).reshape(4, 2, 128, 128).transpose(2, 0, 1, 3).reshape(128, -1)),
        'wnq': _col_blocks(w_in, 2064, 4),
        'wnk': _col_blocks(w_in, 2576, 4),
        'wnv': _rows_ptc(w_in[:, 3088:3600]),
        'wga': _col_blocks(w_in, 3600, 8),
        'wgb': _col_blocks(w_in, 4624, 8),
        'wa': _col_blocks(f(inp['w_branch_a']), 0, 8),
        'wb': _col_blocks(f(inp['w_branch_b']), 0, 8),
        'wout': _rows_ptc(f(inp['w_out'])),
        'wf1': _col_blocks(f(inp['w_ff1']), 0, 32),
        'wf2': np.ascontiguousarray(f(inp['w_ff2']).reshape(32, 128, 1024)),
        'cmask': _const_masks(),
        'qkg': np.ascontiguousarray(np.stack([np.tile(f(inp['na_q_norm_g']), 2), np.tile(f(inp['na_k_norm_g']), 2)], axis=1)),
    }
    x = f(inp['x'])
    meta = f(inp['meta_tokens'])
    cwfull = f(inp['mlstm_conv_w'])[:, 0, :]
    gb = f(inp['mlstm_gate_b']).reshape(16)
    gcols = w_in[:, 2048:2064]
    zero1 = np.zeros((1, D), np.float32)
    z16 = np.zeros((16, D), np.float32)
    maps = []
    tabs = {}
    for core in range(8):
        b, hf = core // 2, core % 2
        m = dict(shared)
        if hf == 0:
            xe = np.concatenate([zero1, meta, x[b], z16, zero1])
            vl = (1.0, 0.0)
            cwl, gc, gbl = cwfull, gcols, gb
        else:
            xe = np.concatenate([zero1, z16, x[b][::-1], meta[::-1], zero1])
            vl = (0.0, 1.0)
            cwl = cwfull[::-1]
            gc = np.concatenate([gcols[:, 8:16], gcols[:, 0:8]], axis=1)
            gbl = np.concatenate([gb[8:16], gb[0:8]])
        m['xe'] = np.ascontiguousarray(xe)
        m['valid'] = np.ascontiguousarray(np.tile(np.array(vl, np.float32)[None, :], (128, 1)))
        m['gate_b'] = np.ascontiguousarray(np.tile(gbl[None, :], (128, 1)))
        m['wg'] = _rows_ptc(np.ascontiguousarray(gc))
        m['vecs'] = np.ascontiguousarray(np.concatenate(
            [_colvec(inp['norm1_g']), _colvec(cwl[0]), _colvec(cwl[1]), _colvec(cwl[2]), _colvec(inp['mlstm_conv_b']),
             _colvec(f(inp['mlstm_norm_g']).reshape(-1)), _colvec(inp['mlstm_skip']), _colvec(inp['norm2_g'])], axis=1))
        if hf not in tabs:
            tabs[hf] = _na_tables(hf, f(inp['na_rpb']), f(inp['na_meta_bias']))
        m['bt'], m['mb'] = tabs[hf]
        maps.append(m)
    return maps


_NC_CACHE = {}


def kernel(**inputs):
    maps = prep_inputs(inputs)
    if 'nc' not in _NC_CACHE:
        _NC_CACHE['nc'] = build_nc()
    nc = _NC_CACHE['nc']
    res = run_bass_kernel_spmd(nc, maps, core_ids=list(range(8)))
    out = np.zeros((4, 4096, D), np.float32)
    for core in range(8):
        b, hf = core // 2, core % 2
        y = np.asarray(res.results[core]["y"], np.float32)
        if hf == 0:
            out[b, 0:NOWN] = y
        else:
            out[b, NOWN:] = y[::-1]
    return out
```

```python
import contextlib
import numpy as np
import concourse.bass as bass
import concourse.mybir as mybir
from concourse.bass_utils import run_bass_kernel_spmd

F32 = mybir.dt.float32
BF16 = mybir.dt.bfloat16
AF = mybir.ActivationFunctionType
ALU = mybir.AluOpType
DSZ = {F32: 4, BF16: 2}

D = 1024
NOWN = 2048
NT = 16
OWN0 = 17
OTH0 = 17 + 2048
POST0 = 17 + 4096
XE_ROWS = 4130
NXC = 17 + 2048 + 256
H_M, DV, DK = 4, 256, 128
NH, DH = 8, 64
EPS = 1e-6
NEG = -30000.0
N_DMA_SEMS = 12
SAME_ENGINE_SYNC = True


class Sched:
    def __init__(self, nc):
        self.nc = nc
        self.engs = ('pe', 'act', 'dve', 'pool', 'sp')
        self.prog = {k: [] for k in self.engs}
        self.count = {k: 0 for k in self.engs}
        self.waited = {k: {} for k in self.engs}
        self.recs = {}
        self.dma_q = ('sp', 'pool', 'act')
        self.dma_uses = {q: [0] * N_DMA_SEMS for q in self.dma_q}
        self.dma_rr = {q: 0 for q in self.dma_q}
        self.n_inst = 0
        self.dram = set()

    def _box(self, ap):
        name = ap.tensor.name
        if name in self.dram:
            return None
        if name.startswith('ps'):
            return name, 0, 128, 0, 2048
        dims = ap.ap
        sz = DSZ[ap.dtype]
        shp = ap.tensor.shape
        row = 1
        for s in list(shp)[1:]:
            row *= int(s)
        off = int(ap.offset)
        p0 = off // row
        b0 = (off % row) * sz
        pc = int(dims[0][1])
        ext = 0
        for st, cn in dims[1:]:
            ext += (int(cn) - 1) * abs(int(st))
        b1 = b0 + (ext + 1) * sz
        return name, p0, p0 + pc, b0, b1

    def _deps(self, eng, ins, outs):
        deps = {}

        def add(sk, v):
            if deps.get(sk, 0) < v:
                deps[sk] = v
        rb = [b for b in (self._box(a) for a in ins) if b is not None]
        wb = [b for b in (self._box(a) for a in outs) if b is not None]
        wb = wb + [b for b in rb if b[0].startswith('ps') and b not in wb]
        rb = [b for b in rb if not b[0].startswith('ps')]
        for (name, p0, p1, b0, b1) in rb:
            for r in self.recs.get(name, ()):
                if r[0] < p1 and p0 < r[1] and r[2] < b1 and b0 < r[3]:
                    if r[4] is not None:
                        add(*r[4])
        for (name, p0, p1, b0, b1) in wb:
            for r in self.recs.get(name, ()):
                if r[0] < p1 and p0 < r[1] and r[2] < b1 and b0 < r[3]:
                    if r[4] is not None:
                        add(*r[4])
                    for sk, v in r[5].items():
                        add(sk, v)
        waits = []
        for sk, v in deps.items():
            if sk == eng and (eng == 'pe' or not SAME_ENGINE_SYNC):
                continue
            if self.waited[eng].get(sk, 0) >= v:
                continue
            self.waited[eng][sk] = v
            waits.append((sk, v))
        return waits, rb, wb

    def _commit(self, tok, rb, wb):
        for (name, p0, p1, b0, b1) in wb:
            lst = self.recs.setdefault(name, [])
            keep = []
            for r in lst:
                if r[0] >= p0 and r[1] <= p1 and r[2] >= b0 and r[3] <= b1:
                    continue
                keep.append(r)
            keep.append([p0, p1, b0, b1, tok, {}])
            self.recs[name] = keep
        for (name, p0, p1, b0, b1) in rb:
            lst = self.recs.setdefault(name, [])
            best = None
            for r in lst:
                if r[0] <= p0 and r[1] >= p1 and r[2] <= b0 and r[3] >= b1 and r[4] != tok:
                    sz = (r[1] - r[0]) * (r[3] - r[2])
                    if best is None or sz < best[0]:
                        best = (sz, r)
            if best is not None:
                r = best[1]
                if r[5].get(tok[0], 0) < tok[1]:
                    r[5][tok[0]] = tok[1]
                continue
            for r in lst:
                if r[0] < p1 and p0 < r[1] and r[2] < b1 and b0 < r[3]:
                    if r[4] == tok:
                        continue
                    if r[5].get(tok[0], 0) < tok[1]:
                        r[5][tok[0]] = tok[1]
            lst.append([p0, p1, b0, b1, None, {tok[0]: tok[1]}])

    def op(self, eng, fn, ins, outs):
        waits, rb, wb = self._deps(eng, ins, outs)
        self.count[eng] += 1
        tok = (eng, self.count[eng])
        self.prog[eng].append((waits, fn, tok))
        self._commit(tok, rb, wb)
        self.n_inst += 1
        return tok

    def dma(self, eng, out, in_):
        waits, rb, wb = self._deps(eng, [in_], [out])
        i = self.dma_rr[eng]
        self.dma_rr[eng] = (i + 1) % N_DMA_SEMS
        self.dma_uses[eng][i] += 1
        sk = ('dma', eng, i)
        prev = 16 * (self.dma_uses[eng][i] - 1)
        if prev > 0 and self.waited[eng].get(sk, 0) < prev:
            self.waited[eng][sk] = prev
            waits.append((sk, prev))
        tok = (sk, 16 * self.dma_uses[eng][i])
        self.prog[eng].append((waits, (lambda e, o=out, a=in_: e.dma_start(out=o, in_=a)), tok))
        self._commit(tok, rb, wb)
        self.n_inst += 1
        return tok

    def wait_tokens(self, eng, toks):
        waits = []
        for sk, v in toks:
            if self.waited[eng].get(sk, 0) >= v:
                continue
            self.waited[eng][sk] = v
            waits.append((sk, v))
        if waits:
            self.prog[eng].append((waits, None, None))

    def mm(self, out, lhsT, rhs, start=True, stop=True, sgc=False):
        if sgc:
            return self.op('pe', lambda e: e.matmul(out, lhsT=lhsT, rhs=rhs, start=start, stop=stop,
                                                    skip_group_check=True), [lhsT, rhs], [out])
        return self.op('pe', lambda e: e.matmul(out, lhsT=lhsT, rhs=rhs, start=start, stop=stop),
                       [lhsT, rhs], [out])

    def transpose(self, out, in_, ident):
        return self.op('pe', lambda e: e.transpose(out=out, in_=in_, identity=ident), [in_, ident], [out])

    def act(self, out, in_, func, bias=None, scale=None, accum=None):
        kw = {}
        ins = [in_]
        outs = [out]
        if bias is not None:
            kw['bias'] = bias
            if not isinstance(bias, (int, float)):
                ins.append(bias)
        if scale is not None:
            kw['scale'] = scale
            if not isinstance(scale, (int, float)):
                ins.append(scale)
        if accum is not None:
            kw['accum_out'] = accum
            outs.append(accum)
        return self.op('act', lambda e: e.activation(out=out, in_=in_, func=func, **kw), ins, outs)

    def ts(self, eng, out, in0, s1, s2, op0, op1=None):
        ins = [in0] + [s for s in (s1, s2) if s is not None and not isinstance(s, (int, float))]
        if op1 is None:
            fn = lambda e: e.tensor_scalar(out=out, in0=in0, scalar1=s1, scalar2=None, op0=op0)
        else:
            fn = lambda e: e.tensor_scalar(out=out, in0=in0, scalar1=s1, scalar2=s2, op0=op0, op1=op1)
        return self.op(eng, fn, ins, [out])

    def tt(self, eng, out, in0, in1, op):
        return self.op(eng, lambda e: e.tensor_tensor(out=out, in0=in0, in1=in1, op=op), [in0, in1], [out])

    def stt(self, out, in0, scalar, in1, op0, op1):
        ins = [in0, in1] + ([] if isinstance(scalar, (int, float)) else [scalar])
        return self.op('dve', lambda e: e.scalar_tensor_tensor(out=out, in0=in0, scalar=scalar, in1=in1,
                                                               op0=op0, op1=op1), ins, [out])

    def copy(self, eng, out, in_):
        if eng == 'act':
            return self.op('act', lambda e: e.copy(out=out, in_=in_), [in_], [out])
        return self.op(eng, lambda e: e.tensor_copy(out=out, in_=in_), [in_], [out])

    def recip(self, out, in_):
        return self.op('dve', lambda e: e.reciprocal(out=out, in_=in_), [in_], [out])

    def memset(self, eng, ap, val):
        return self.op(eng, lambda e: e.memset(ap, val), [], [ap])

    def emit(self):
        nc = self.nc
        engmap = {'pe': nc.tensor, 'act': nc.scalar, 'dve': nc.vector, 'pool': nc.gpsimd, 'sp': nc.sync}
        with contextlib.ExitStack() as st:
            sems = {}
            for e in self.engs:
                sems[e] = st.enter_context(nc.semaphore("sem_" + e))
            for q in self.dma_q:
                for i in range(N_DMA_SEMS):
                    sems[('dma', q, i)] = st.enter_context(nc.semaphore("sem_dma_%s%d" % (q, i)))
            block = st.enter_context(nc.Block())

            def replay(ename, e):
                for waits, fn, tok in self.prog[ename]:
                    for sk, v in waits:
                        e.wait_ge(sems[sk], v)
                    if fn is None:
                        continue
                    inst = fn(e)
                    if isinstance(tok[0], tuple):
                        inst.then_inc(sems[tok[0]], 16)
                    else:
                        inst.then_inc(sems[tok[0]], 1)

            block.tensor(lambda e: replay('pe', e))
            block.scalar(lambda e: replay('act', e))
            block.vector(lambda e: replay('dve', e))
            block.gpsimd(lambda e: replay('pool', e))
            block.sync(lambda e: replay('sp', e))


class Arena:
    def __init__(self, t, nbytes):
        self.t = t
        self.nbytes = nbytes
        self.off = 0
        self.stack = []

    def push(self):
        self.stack.append(self.off)

    def pop(self):
        self.off = self.stack.pop()

    def alloc(self, shape, dtype):
        n = 1
        for s in shape:
            n *= s
        nb = n * DSZ[dtype]
        nb = (nb + 63) // 64 * 64
        assert self.off + nb <= self.nbytes, ("arena overflow", self.off, nb, self.nbytes)
        a = self.t[:, self.off // 4:(self.off + nb) // 4]
        self.off += nb
        if dtype != F32:
            a = a.bitcast(dtype)
        a = a[:, 0:n]
        if len(shape) == 2:
            a = a.rearrange("p (a b) -> p a b", a=shape[0])
        elif len(shape) == 3:
            a = a.rearrange("p (a b c) -> p a b c", a=shape[0], b=shape[1])
        elif len(shape) == 4:
            a = a.rearrange("p (a b c d) -> p a b c d", a=shape[0], b=shape[1], c=shape[2])
        return a


def key_tiles(i):
    if i == 0:
        return [(j, j) for j in range(4)]
    if i == 1:
        return [(j, 4 + j) for j in range(4)]
    return [(i + d, 8 + d + 2) for d in range(-2, 3)]


def build_nc(debug=None, stop_after=None):
    nc = bass.Bass("TRN2", target_bir_lowering=False)
    S = Sched(nc)
    dbg_outs = {}

    def din(name, shape):
        t = nc.dram_tensor(name, list(shape), F32, kind="ExternalInput")
        S.dram.add(name)
        return t.ap()

    xe = din("xe", [XE_ROWS, D])
    d_vec = din("vecs", [128, 64])
    d_valid = din("valid", [128, 2])
    d_gateb = din("gate_b", [128, 16])
    d_cmask = din("cmask", [128, 6, 128])
    d_mb = din("mb", [32, 8])
    d_bt = din("bt", [4, 128, 13 * 2 * 128])
    d_wxm = din("wxm", [128, 8 * 1024])
    d_wo = din("wo", [8, 128, 8 * 128])
    d_wg = din("wg", [128, 8 * 16])
    d_wq = din("wqm", [128, 4 * 2 * 128])
    d_wk = din("wkm", [128, 4 * 2 * 128])
    d_wnq = din("wnq", [4, 128, 8 * 128])
    d_wnk = din("wnk", [4, 128, 8 * 128])
    d_wnv = din("wnv", [128, 8 * 512])
    d_wga = din("wga", [8, 128, 8 * 128])
    d_wgb = din("wgb", [8, 128, 8 * 128])
    d_wa = din("wa", [8, 128, 8 * 128])
    d_wb = din("wb", [8, 128, 4 * 128])
    d_wout = din("wout", [128, 8 * 1024])
    d_wf1 = din("wf1", [32, 128, 8 * 128])
    d_wf2 = din("wf2", [32, 128, 1024])
    y_out = nc.dram_tensor("y", [NOWN, D], F32, kind="ExternalOutput")
    S.dram.add("y")
    y_out = y_out.ap()

    with contextlib.ExitStack() as st:
        ARENA_BYTES = 207 * 1024
        arena_t = st.enter_context(nc.sbuf_tensor("arena", [128, ARENA_BYTES // 4], F32))
        AR = Arena(arena_t, ARENA_BYTES)
        ps = [st.enter_context(nc.psum_tensor("ps%d" % i, [128, 512], F32)) for i in range(8)]
        psb = [p[:].bitcast(BF16) for p in ps]

        def dbg(name, ap, shape):
            if debug is None or name not in debug:
                return
            t = nc.dram_tensor("dbg_" + name, list(shape), F32, kind="ExternalOutput")
            S.dram.add("dbg_" + name)
            dbg_outs[name] = S.dma('pool', t.ap(), ap)

        vec = AR.alloc([64], F32)
        valid = AR.alloc([2], F32)
        gateb = AR.alloc([16], F32)
        cmf = AR.alloc([4, 128], F32)
        cmb = AR.alloc([4, 128], BF16)
        mbc = AR.alloc([8], F32)
        S.dma('sp', vec, d_vec)
        S.dma('sp', valid, d_valid)
        S.dma('sp', gateb, d_gateb)
        S.dma('sp', cmf, d_cmask[:, 0:4, :])
        S.dma('pool', cmb[:, 0:2, :], d_cmask[:, 0:2, :])
        S.dma('pool', cmb[:, 2:4, :], d_cmask[:, 4:6, :])
        S.dma('sp', mbc[0:32, :], d_mb)
        triA_f, triB_f = cmf[:, 0, :], cmf[:, 1, :]
        ones_f = [cmf[:, 2, :], cmf[:, 3, :]]
        maskA, maskB, ident, blk64 = cmb[:, 0, :], cmb[:, 1, :], cmb[:, 2, :], cmb[:, 3, :]
        g1 = vec[:, 0:8]
        cw = [vec[:, 8:16], vec[:, 16:24], vec[:, 24:32]]
        cb = vec[:, 32:40]
        mng = vec[:, 40:48]
        skp = vec[:, 48:56]
        g2 = vec[:, 56:64]
        qkg = AR.alloc([2], F32)
        d_qkg = din("qkg", [128, 2])
        S.dma('sp', qkg, d_qkg)

        dcw = AR.alloc([3, 8, 128], BF16)
        for k3 in range(3):
            for et in range(8):
                S.ts('dve', dcw[:, k3, et, :], ident, cw[k3][:, et:et + 1], None, ALU.mult)
        U_A = AR.alloc([4, 257], F32)
        U_B = AR.alloc([4, 257], F32)
        dprevA = AR.alloc([4], F32)
        dprevB = AR.alloc([4], F32)
        OFF_XNT = AR.off
        xnT = AR.alloc([8, NXC + 1], BF16)
        S.memset('dve', xnT[:, :, 0:1], 0.0)
        xnTm = AR.alloc([8, 32], BF16)
        MIX_OFF = ARENA_BYTES - 8 * NOWN * 2

        wg = AR.alloc([8, 16], BF16)
        wq = AR.alloc([4, 2, 128], BF16)
        wk = AR.alloc([4, 2, 128], BF16)
        S.dma('pool', wg, d_wg.rearrange("p (a b) -> p a b", a=8))
        S.dma('pool', wq, d_wq.rearrange("p (a b c) -> p a b c", a=4, b=2))
        S.dma('pool', wk, d_wk.rearrange("p (a b c) -> p a b c", a=4, b=2))
        AR.push()
        xt_buf = [AR.alloc([D], F32) for _ in range(4)]
        xnb_buf = [AR.alloc([D], BF16) for _ in range(4)]
        xt_buf2 = [AR.alloc([D], F32) for _ in range(4)]
        xnb_buf2 = [AR.alloc([D], BF16) for _ in range(4)]
        ss_buf2 = AR.alloc([8], F32)
        sq_junk = AR.alloc([D], BF16)
        ss_buf = AR.alloc([8], F32)
        xcnt = [0]

        def emit_xnT_batch(items, gvec, xts=None, xnbs=None, ssb=None):
            nb = len(items)
            assert nb <= 4
            xts = xts or xt_buf
            xnbs = xnbs or xnb_buf
            ssb = ssb if ssb is not None else ss_buf
            nmax = max(n for _, n, _ in items)
            for i, (row0, n, dst) in enumerate(items):
                S.dma('sp', xts[i][0:n, :], xe[row0:row0 + n, :])
            for i, (row0, n, dst) in enumerate(items):
                S.act(sq_junk[0:n, :], xts[i][0:n, :], AF.Square, accum=ssb[0:n, i:i + 1])
            rs = ssb[0:nmax, 4:4 + nb]
            S.ts('dve', rs, ssb[0:nmax, 0:nb], 1.0 / D, EPS, ALU.mult, ALU.add)
            S.act(rs, rs, AF.Sqrt)
            S.recip(rs, rs)
            for i, (row0, n, dst) in enumerate(items):
                S.ts('dve', xnbs[i][0:n, :], xts[i][0:n, :], ssb[0:n, 4 + i:5 + i], None, ALU.mult)
            for i, (row0, n, dst) in enumerate(items):
                pb = psb[6 + (xcnt[0] % 2)]
                xcnt[0] += 1
                for dt in range(8):
                    S.transpose(pb[:, dt * 128:dt * 128 + n], xnbs[i][0:n, dt * 128:(dt + 1) * 128],
                                ident[0:n, 0:n])
                pv = pb[:, :].rearrange("p (a b) -> p a b", a=8)[:, :, 0:n]
                S.tt('dve', dst, pv, gvec.unsqueeze(2).to_broadcast([128, 8, n]), ALU.mult)

        xt_h = AR.alloc([D], F32)
        xnb_h = AR.alloc([D], BF16)
        ss_m = AR.alloc([8], F32)

        def emit_xnT(row0, n, dst, gvec):
            emit_xnT_batch([(row0, n, dst)], gvec, xts=[xt_h], xnbs=[xnb_h], ssb=ss_m)

        def emit_h1_xn2T(tile_rows, h1, dst):
            pass

        if stop_after == 'consts':
            dbg("vec", vec, [128, 64])
            return finish(nc, S, y_out, dbg_outs, None)
        if stop_after == 'x16':
            emit_xnT(1, 16, xnT[:, :, 1:17], g1)
            dbg("xnT", xnT[:, 0, 0:NXC], [128, NXC])
            return finish(nc, S, y_out, dbg_outs, None)
        if stop_after and stop_after.startswith('xn'):
            for k in range(int(stop_after[2:])):
                emit_xnT(OWN0 + 128 * k, 128, xnT[:, :, OWN0 + 128 * k:OWN0 + 128 * k + 128], g1)
            dbg("xnT", xnT[:, 0, 0:NXC], [128, NXC])
            return finish(nc, S, y_out, dbg_outs, None)
        if stop_after == 'x1':
            emit_xnT(OWN0, 128, xnT[:, :, OWN0:OWN0 + 128], g1)
            dbg("xnT", xnT[:, 0, 0:NXC], [128, NXC])
            return finish(nc, S, y_out, dbg_outs, None)
        emit_xnT_batch([(1, 16, xnT[:, :, 1:17]), (1, 16, xnTm[:, :, 0:16]), (POST0, 16, xnTm[:, :, 16:32])], g1)
        def p0gen():
            for k0 in range(0, NT + 2, 4):
                alt = (k0 // 4) % 2 == 1
                emit_xnT_batch([(OWN0 + 128 * k, 128, xnT[:, :, OWN0 + 128 * k:OWN0 + 128 * k + 128])
                                for k in range(k0, min(k0 + 4, NT + 2))], g1,
                               xts=xt_buf2 if alt else None, xnbs=xnb_buf2 if alt else None,
                               ssb=ss_buf2 if alt else None)
                yield

        AR.push()
        wxm = AR.alloc([8, 1024], BF16)
        for q4 in range(4):
            S.dma('pool', wxm[:, 2 * q4:2 * q4 + 2, :],
                  d_wxm.rearrange("p (a b) -> p a b", a=8)[:, 2 * q4:2 * q4 + 2, :])
        p1bufs = []
        for _ in range(2):
            p1bufs.append(dict(
                xoT=AR.alloc([8, 514], BF16), xmT=AR.alloc([8, 514], BF16), xcT=AR.alloc([8, 512], BF16),
                ktok1=AR.alloc([4, 512], BF16), vp1=AR.alloc([4, 4, 257], BF16),
                gsb=AR.alloc([4, 16], F32), sp1=AR.alloc([4, 4], F32), w1=AR.alloc([4, 4], F32),
                d1=AR.alloc([2, 4, 4], F32), tmp16=AR.alloc([4, 4], F32)))
        p1cnt = [0]
        carry = [None]

        def seq_block(direction, row0, ntile, npp, ncol, is_mini, vflag, first):
            ntok = ntile * npp
            B_ = p1bufs[p1cnt[0] % 2]
            p1cnt[0] += 1
            xoT, xmT, xcT, ktok1, vp1 = B_['xoT'], B_['xmT'], B_['xcT'], B_['ktok1'], B_['vp1']
            gsb, sp1, w1, d1, tmp16 = B_['gsb'], B_['sp1'], B_['w1'], B_['d1'], B_['tmp16']
            if direction == 'A':
                U, dprev, tri, li0, f0 = U_A, dprevA, triA_f, 0, 4
            else:
                U, dprev, tri, li0, f0 = U_B, dprevB, triB_f, 8, 12
            if is_mini:
                emit_xnT(row0, ncol, xoT[:, :, 0:ncol], g1)
            else:
                emit_xnT_batch([(row0 + 1 + 128 * j, 128, xoT[:, :, 1 + 128 * j:1 + 128 * j + 128])
                                for j in range(ntile)], g1)
                emit_xnT(row0, 1, xoT[:, :, 0:1], g1)
                S.copy('dve', xmT[:, :, 512:514], carry[0])
            carry[0] = xmT[:, :, 0:2]
            nmain = min(512, ncol)
            yield
            pg = ps[2]
            for j in range(ntile):
                for dt in range(8):
                    S.mm(pg[0:npp, 64 + 16 * j:64 + 16 * j + 16], xoT[:, dt, 1 + j * npp:1 + (j + 1) * npp],
                         wg[:, dt, :], start=(dt == 0), stop=(dt == 7))
            pgv = pg[0:npp, 64:64 + 16 * ntile].rearrange("p (a b) -> p a b", a=ntile)
            S.tt('dve', gsb[0:npp, 0:ntile, :], pgv, gateb[0:npp, :].unsqueeze(1).to_broadcast([npp, ntile, 16]),
                 ALU.add)
            spv = sp1[0:npp, 0:ntile, :]
            S.act(spv, gsb[0:npp, 0:ntile, f0:f0 + 4], AF.Exp, scale=-1.0)
            S.act(spv, spv, AF.Ln, bias=1.0, scale=1.0)
            pc = ps[3]
            S.mm(pc[0:npp, 128:128 + 4 * ntile], tri[0:npp, 0:npp], spv)
            nhb = 1 if is_mini else 2
            for hb in range(nhb):
                lh = ones_f[hb][0:npp, :] if not is_mini else ones_f[0][0:npp, :]
                S.mm(pc[:, 192 + 16 * hb:192 + 16 * hb + 4 * ntile], lh, spv)
            csv = pc[0:npp, 128:128 + 4 * ntile].rearrange("p (a b) -> p a b", a=ntile)
            S.tt('dve', tmp16[0:npp, 0:ntile, :], gsb[0:npp, 0:ntile, li0:li0 + 4], csv, ALU.add)
            S.act(w1[0:npp, 0:ntile, :], tmp16[0:npp, 0:ntile, :], AF.Exp)
            if vflag is not None:
                S.ts('dve', w1[0:npp, 0:ntile, :], w1[0:npp, 0:ntile, :], vflag[0:npp, :], None, ALU.mult)
            for hb in range(nhb):
                S.act(d1[:, hb, 0:ntile, :],
                      pc[:, 192 + 16 * hb:192 + 16 * hb + 4 * ntile].rearrange("p (a b) -> p a b", a=ntile),
                      AF.Exp, scale=-1.0)
            yield
            for et in range(8):
                pm = ps[et % 4]
                for dt in range(8):
                    S.mm(pm[:, 0:nmain], wxm[:, dt, et * 128:(et + 1) * 128], xoT[:, dt, 0:nmain],
                         start=(dt == 0), stop=(dt == 7))
                S.copy('act', xmT[:, et, 0:nmain], pm[:, 0:nmain])
            for j in range(ntile):
                for half in range(2):
                    pv_ = ps[4 + half]
                    for dt in range(8):
                        S.mm(pv_[0:npp, :], xoT[:, dt, 1 + j * npp:1 + (j + 1) * npp],
                             wxm[:, dt, 512 * half:512 * half + 512], start=(dt == 0), stop=(dt == 7))
                    S.tt('dve', vp1[0:npp, j, 2 * half:2 * half + 2, 0:256],
                         pv_[0:npp, :].rearrange("p (a b) -> p a b", a=2),
                         w1[0:npp, j, 2 * half:2 * half + 2].unsqueeze(2).to_broadcast([npp, 2, 256]), ALU.mult)
            S.copy('dve', vp1[0:npp, 0:ntile, :, 256], w1[0:npp, 0:ntile, :])
            yield
            for et in range(8):
                pcv = ps[6 + et % 2]
                for k3 in range(3):
                    S.mm(pcv[:, 0:ntok], dcw[:, k3, et, :], xmT[:, et, k3:k3 + ntok], start=(k3 == 0), stop=(k3 == 2))
                S.act(xcT[:, et, 0:ntok], pcv[:, 0:ntok], AF.Silu, bias=cb[:, et:et + 1], scale=1.0)
            yield
            for j in range(ntile):
                pk = ps[3]
                for h in range(4):
                    for e2 in range(2):
                        S.mm(pk[0:npp, h * 128:(h + 1) * 128], xcT[:, 2 * h + e2, j * npp:(j + 1) * npp],
                             wk[:, h, e2, :], start=(e2 == 0), stop=(e2 == 1))
                S.act(ktok1[0:npp, j, :], pk[0:npp, :], AF.Copy, scale=DK ** -0.5)
            yield
            order = []
            for j in range(ntile):
                for hb in range(nhb):
                    order.append((j, hb))
            if direction == 'B':
                order = order[::-1]
            for (j, hb) in order:
                p0 = 64 * hb
                cn = npp if is_mini else 64
                for h in range(4):
                    pst = ps[h]
                    S.mm(pst[:, 0:257], ktok1[p0:p0 + cn, j, h * 128:(h + 1) * 128], vp1[p0:p0 + cn, j, h, :])
                    if first:
                        S.copy('dve', U[:, h, :], pst[:, 0:257])
                    else:
                        S.stt(U[:, h, :], U[:, h, :], dprev[:, h:h + 1], pst[:, 0:257], ALU.mult, ALU.add)
                S.copy('dve', dprev, d1[:, hb, j, :])
                first = False

        gens = [seq_block('A', 0, 1, 16, 18, True, valid[:, 0:1], True),
                seq_block('B', POST0 - 1, 1, 16, 18, True, valid[:, 1:2], True), p0gen()]
        while gens:
            for g_ in list(gens):
                try:
                    next(g_)
                except StopIteration:
                    gens.remove(g_)
        dbg("xnT", xnT[:, 0, 0:NXC], [128, NXC])
        if stop_after == 'phase0':
            return finish(nc, S, y_out, dbg_outs, None)

        def big_bufs(bk):
            return p1bufs[(bk + 1) % 2]

        def X_pre(bk):
            row0 = OTH0 + 512 * bk - 1
            for j in range(4):
                S.dma('sp', xt_buf[j], xe[row0 + 1 + 128 * j:row0 + 1 + 128 * j + 128, :])
            S.dma('sp', xt_h[0:1, :], xe[row0:row0 + 1, :])
            for j in range(4):
                S.act(sq_junk, xt_buf[j], AF.Square, accum=ss_buf[:, j:j + 1])
            S.act(sq_junk[0:1, :], xt_h[0:1, :], AF.Square, accum=ss_h[0:1, 0:1])
            rs = ss_buf[:, 4:8]
            S.ts('dve', rs, ss_buf[:, 0:4], 1.0 / D, EPS, ALU.mult, ALU.add)
            S.act(rs, rs, AF.Sqrt)
            S.recip(rs, rs)
            rh = ss_h[0:1, 1:2]
            S.ts('dve', rh, ss_h[0:1, 0:1], 1.0 / D, EPS, ALU.mult, ALU.add)
            S.act(rh, rh, AF.Sqrt)
            S.recip(rh, rh)
            for j in range(4):
                S.ts('dve', xnb_buf[j], xt_buf[j], ss_buf[:, 4 + j:5 + j], None, ALU.mult)
            S.ts('dve', xnb_h[0:1, :], xt_h[0:1, :], rh, None, ALU.mult)

        def X_post(bk):
            B_ = big_bufs(bk)
            xoT, xmT = B_['xoT'], B_['xmT']
            for j in range(4):
                pb = psb[6 + (j % 2)]
                for dt in range(8):
                    S.transpose(pb[:, dt * 128:dt * 128 + 128], xnb_buf[j][:, dt * 128:(dt + 1) * 128], ident)
                S.tt('dve', xoT[:, :, 1 + 128 * j:1 + 128 * j + 128], pb[:, :].rearrange("p (a b) -> p a b", a=8),
                     g1.unsqueeze(2).to_broadcast([128, 8, 128]), ALU.mult)
            pb = psb[6]
            for dt in range(8):
                S.transpose(pb[:, dt * 128:dt * 128 + 1], xnb_h[0:1, dt * 128:(dt + 1) * 128], ident[0:1, 0:1])
            S.tt('dve', xoT[:, :, 0:1], pb[:, :].rearrange("p (a b) -> p a b", a=8)[:, :, 0:1],
                 g1.unsqueeze(2).to_broadcast([128, 8, 1]), ALU.mult)
            S.copy('dve', xmT[:, :, 512:514], carry[0])
            carry[0] = xmT[:, :, 0:2]

        def G_(bk):
            B_ = big_bufs(bk)
            xoT, gsb, sp1, w1, d1, tmp16 = B_['xoT'], B_['gsb'], B_['sp1'], B_['w1'], B_['d1'], B_['tmp16']
            pg = ps[0]
            for j in range(4):
                for dt in range(8):
                    S.mm(pg[:, 16 * j:16 * j + 16], xoT[:, dt, 1 + j * 128:1 + (j + 1) * 128], wg[:, dt, :],
                         start=(dt == 0), stop=(dt == 7))
            S.tt('dve', gsb, pg[:, 0:64].rearrange("p (a b) -> p a b", a=4),
                 gateb.unsqueeze(1).to_broadcast([128, 4, 16]), ALU.add)
            S.act(sp1, gsb[:, :, 12:16], AF.Exp, scale=-1.0)
            S.act(sp1, sp1, AF.Ln, bias=1.0, scale=1.0)
            pc = ps[1]
            S.mm(pc[:, 0:16], triB_f, sp1)
            for hb in range(2):
                S.mm(pc[:, 64 + 16 * hb:64 + 16 * hb + 16], ones_f[hb], sp1)
            S.tt('dve', tmp16, gsb[:, :, 8:12], pc[:, 0:16].rearrange("p (a b) -> p a b", a=4), ALU.add)
            S.act(w1, tmp16, AF.Exp)
            for hb in range(2):
                S.act(d1[:, hb, :, :], pc[:, 64 + 16 * hb:64 + 16 * hb + 16].rearrange("p (a b) -> p a b", a=4),
                      AF.Exp, scale=-1.0)

        def M_et(bk, et):
            B_ = big_bufs(bk)
            pm = ps[et % 4]
            for dt in range(8):
                S.mm(pm[:, :], wxm[:, dt, et * 128:(et + 1) * 128], B_['xoT'][:, dt, 0:512],
                     start=(dt == 0), stop=(dt == 7))
            S.copy('act', B_['xmT'][:, et, 0:512], pm[:, :])

        def V_(bk):
            B_ = big_bufs(bk)
            xmT, vp1, w1 = B_['xmT'], B_['vp1'], B_['w1']
            for j in range(4):
                pb = psb[4 + j % 2]
                for et in range(8):
                    S.transpose(pb[:, et * 128:(et + 1) * 128], xmT[:, et, 1 + 128 * j:1 + 128 * j + 128], ident)
                S.tt('dve', vp1[:, j, :, 0:256], pb[:, :].rearrange("p (a b) -> p a b", a=4),
                     w1[:, j, :].unsqueeze(2).to_broadcast([128, 4, 256]), ALU.mult)
            S.copy('dve', vp1[:, :, :, 256], w1)

        def C_(bk):
            B_ = big_bufs(bk)
            for et in range(8):
                pcv = ps[6 + et % 2]
                for k3 in range(3):
                    S.mm(pcv[:, :], dcw[:, k3, et, :], B_['xmT'][:, et, k3:k3 + 512], start=(k3 == 0), stop=(k3 == 2))
                S.act(B_['xcT'][:, et, :], pcv[:, :], AF.Silu, bias=cb[:, et:et + 1], scale=1.0)

        def K_(bk):
            B_ = big_bufs(bk)
            for j in range(4):
                pk = ps[4 + j % 2]
                for h in range(4):
                    for e2 in range(2):
                        S.mm(pk[:, h * 128:(h + 1) * 128], B_['xcT'][:, 2 * h + e2, j * 128:(j + 1) * 128],
                             wk[:, h, e2, :], start=(e2 == 0), stop=(e2 == 1))
                S.act(B_['ktok1'][:, j, :], pk[:, :], AF.Copy, scale=DK ** -0.5)

        def S_step(bk, i):
            B_ = big_bufs(bk)
            j, hb = [(jj, hh) for jj in range(4) for hh in range(2)][::-1][i]
            p0 = 64 * hb
            for h in range(4):
                pst = ps[4 + h]
                S.mm(pst[:, 0:257], B_['ktok1'][p0:p0 + 64, j, h * 128:(h + 1) * 128], B_['vp1'][p0:p0 + 64, j, h, :])
                S.stt(U_B[:, h, :], U_B[:, h, :], dprevB[:, h:h + 1], pst[:, 0:257], ALU.mult, ALU.add)
            S.copy('dve', dprevB, B_['d1'][:, hb, j, :])

        ss_h = AR.alloc([2], F32)
        bks = [3, 2, 1, 0]
        X_pre(3)
        X_post(3)
        G_(3)
        for bi, bk in enumerate(bks):
            prev = bks[bi - 1] if bi > 0 else None
            nxt = bks[bi + 1] if bi + 1 < 4 else None
            for et in range(8):
                M_et(bk, et)
                if prev is not None:
                    S_step(prev, et)
            V_(bk)
            if nxt is not None:
                X_pre(nxt)
            C_(bk)
            if nxt is not None:
                X_post(nxt)
            K_(bk)
            if nxt is not None:
                G_(nxt)
        for i in range(8):
            S_step(0, i)
        dbg("U_A", U_A[:, 0, :], [128, 257])
        dbg("U_B", U_B[:, 0, :], [128, 257])
        dbg("dprevA", dprevA, [128, 4])
        dbg("dprevB", dprevB, [128, 4])
        AR.pop()
        AR.pop()
        yaT = AR.alloc([8, NOWN], BF16)

        if stop_after == 'phase1':
            return finish(nc, S, y_out, dbg_outs, None)

        AR.push()
        gso = AR.alloc([NT, 16], F32)
        spo = AR.alloc([2, NT, 4], F32)
        wgt = AR.alloc([2, NT, 4], F32)
        einv = AR.alloc([2, NT, 4], F32)
        dch = AR.alloc([2, 2, NT, 4], F32)
        tmpg = AR.alloc([NT, 4], F32)
        pg = ps[2]
        for k in range(NT):
            for dt in range(8):
                S.mm(pg[:, 16 * k:16 * k + 16], xnT[:, dt, OWN0 + 128 * k:OWN0 + 128 * k + 128], wg[:, dt, :],
                     start=(dt == 0), stop=(dt == 7))
        S.tt('dve', gso, pg[:, 0:256].rearrange("p (a b) -> p a b", a=NT),
             gateb.unsqueeze(1).to_broadcast([128, NT, 16]), ALU.add)
        for di in range(2):
            f0 = 4 + 8 * di
            li0 = 8 * di
            S.act(spo[:, di, :, :], gso[:, :, f0:f0 + 4], AF.Exp, scale=-1.0)
            S.act(spo[:, di, :, :], spo[:, di, :, :], AF.Ln, bias=1.0, scale=1.0)
            pc = ps[3]
            S.mm(pc[:, 0:64], triA_f if di == 0 else triB_f, spo[:, di, :, :])
            for hb in range(2):
                S.mm(pc[:, 64 + 64 * hb:128 + 64 * hb], ones_f[hb], spo[:, di, :, :])
            csv = pc[:, 0:64].rearrange("p (a b) -> p a b", a=NT)
            S.tt('dve', tmpg, gso[:, :, li0:li0 + 4], csv, ALU.add)
            S.act(wgt[:, di, :, :], tmpg, AF.Exp)
            S.act(einv[:, di, :, :], csv, AF.Exp)
            for hb in range(2):
                S.act(dch[:, di, hb, :, :], pc[:, 64 + 64 * hb:128 + 64 * hb].rearrange("p (a b) -> p a b", a=NT),
                      AF.Exp, scale=-1.0)
        dbg("wgtA", wgt[:, 0, :, :], [128, NT, 4])
        dbg("dchA", dch[:, 0, :, :, :], [128, 2, NT, 4])

        wxmh = AR.alloc([8, 256], BF16)
        wob = [AR.alloc([8, 128], BF16) for _ in range(2)]
        xmt = [[AR.alloc([514], BF16) for _ in range(2)] for _ in range(2)]
        xcThs = [AR.alloc([2, NOWN], BF16) for _ in range(2)]
        vpA = AR.alloc([NT, 257], BF16)
        vpB = AR.alloc([NT, 257], BF16)
        qT = AR.alloc([NOWN], BF16)
        kT = AR.alloc([NOWN], BF16)
        ktok = AR.alloc([NT, 128], BF16)
        P_A = AR.alloc([NT, 128], BF16)
        P_B = AR.alloc([NT, 128], BF16)
        hrA = AR.alloc([NT, 257], BF16)
        hrB = AR.alloc([NT, 257], BF16)
        Cbf = [[AR.alloc([257], BF16) for _ in range(2)] for _ in range(2)]
        rr = AR.alloc([2, NT], F32)
        t1 = [AR.alloc([256], F32) for _ in range(2)]
        hn = [AR.alloc([256], BF16) for _ in range(4)]
        sqj = AR.alloc([256], BF16)
        st8 = AR.alloc([NT, 2], F32)
        sohs = [AR.alloc([2, NOWN], BF16) for _ in range(2)]
        wocnt = [0]

        def head_pre(h):
            xcTh = xcThs[h % 2]
            soh = sohs[h % 2]
            S.dma('pool', wxmh, d_wxm.rearrange("p (a b) -> p a b", a=8)[:, :, 256 * h:256 * h + 256])
            for tb in (3, 2, 1, 0):
                c0 = OWN0 + 512 * tb
                for e2 in range(2):
                    pm = ps[e2]
                    xb = xmt[e2][tb % 2]
                    for dt in range(8):
                        S.mm(pm[:, :], wxmh[:, dt, 128 * e2:128 * e2 + 128], xnT[:, dt, c0 - 1:c0 + 511],
                             start=(dt == 0), stop=(dt == 7))
                    if tb == 3:
                        pa = ps[2]
                        for dt in range(8):
                            S.mm(pa[:, 2 * e2:2 * e2 + 2], wxmh[:, dt, 128 * e2:128 * e2 + 128],
                                 xnT[:, dt, c0 + 511:c0 + 513], start=(dt == 0), stop=(dt == 7))
                        S.copy('act', xb[:, 512:514], pa[:, 2 * e2:2 * e2 + 2])
                    else:
                        S.copy('dve', xb[:, 512:514], xmt[e2][(tb + 1) % 2][:, 0:2])
                    S.copy('act', xb[:, 0:512], pm[:, :])
                pvb = psb[4 + (tb % 2)]
                for kk in range(4):
                    for e2 in range(2):
                        S.transpose(pvb[:, 256 * kk + 128 * e2:256 * kk + 128 * e2 + 128],
                                    xmt[e2][tb % 2][:, 1 + 128 * kk:1 + 128 * kk + 128], ident)
                for kk in range(4):
                    k = 4 * tb + kk
                    S.act(vpA[:, k, 0:256], pvb[:, 256 * kk:256 * kk + 256], AF.Copy, scale=wgt[:, 0, k, h:h + 1])
                    S.ts('dve', vpB[:, k, 0:256], pvb[:, 256 * kk:256 * kk + 256], wgt[:, 1, k, h:h + 1], None, ALU.mult)
                for e2 in range(2):
                    xb = xmt[e2][tb % 2]
                    et = 2 * h + e2
                    pcv = ps[2 + e2]
                    for k3 in range(3):
                        S.mm(pcv[:, :], dcw[:, k3, et, :], xb[:, k3:k3 + 512], start=(k3 == 0), stop=(k3 == 2))
                    S.act(xcTh[:, e2, 512 * tb:512 * tb + 512], pcv[:, :], AF.Silu, bias=cb[:, et:et + 1], scale=1.0)
                yield
            S.copy('dve', vpA[:, :, 256], wgt[:, 0, :, h])
            S.copy('dve', vpB[:, :, 256], wgt[:, 1, :, h])
            for e2 in range(2):
                S.dma('pool', wob[e2], d_wo[2 * h + e2].rearrange("p (a b) -> p a b", a=8))

            def emit_o(e2, k4):
                po = ps[4 + (k4 % 2)]
                for dt in range(8):
                    S.mm(po[:, :], wob[e2][:, dt, :], xnT[:, dt, OWN0 + 512 * k4:OWN0 + 512 * k4 + 512],
                         start=(dt == 0), stop=(dt == 7))
                S.act(soh[:, e2, 512 * k4:512 * k4 + 512], po[:, :], AF.Sigmoid)

            for tb in range(4):
                pq = ps[2 * (tb % 2)]
                pk = ps[2 * (tb % 2) + 1]
                for e2 in range(2):
                    S.mm(pq[:, :], wq[:, h, e2, :], xcTh[:, e2, 512 * tb:512 * tb + 512], start=(e2 == 0),
                         stop=(e2 == 1))
                for e2 in range(2):
                    S.mm(pk[:, :], wk[:, h, e2, :], xcTh[:, e2, 512 * tb:512 * tb + 512], start=(e2 == 0),
                         stop=(e2 == 1))
                S.copy('dve', qT[:, 512 * tb:512 * tb + 512], pq[:, :])
                S.ts('dve', kT[:, 512 * tb:512 * tb + 512], pk[:, :], DK ** -0.5, None, ALU.mult)
                emit_o(0, tb)
                yield
            for k4 in range(4):
                pk = ps[6 + (k4 % 2)]
                for kk in range(4):
                    k = 4 * k4 + kk
                    for e2 in range(2):
                        S.mm(pk[:, 128 * kk:128 * kk + 128], xcTh[:, e2, 128 * k:128 * k + 128], wk[:, h, e2, :],
                             start=(e2 == 0), stop=(e2 == 1))
                S.ts('dve', ktok[:, 4 * k4:4 * k4 + 4, :], pk[:, :].rearrange("p (a b) -> p a b", a=4), DK ** -0.5,
                     None, ALU.mult)
                emit_o(1, k4)
                yield
            for k4 in range(4):
                psc = ps[k4 % 2]
                for kk in range(4):
                    k = 4 * k4 + kk
                    S.mm(psc[:, 128 * kk:128 * kk + 128], kT[:, 128 * k:128 * k + 128], qT[:, 128 * k:128 * k + 128])
                pv3 = psc[:, :].rearrange("p (a b) -> p a b", a=4)
                S.tt('dve', P_A[:, 4 * k4:4 * k4 + 4, :], pv3, maskA.unsqueeze(1).to_broadcast([128, 4, 128]), ALU.mult)
                S.tt('dve', P_B[:, 4 * k4:4 * k4 + 4, :], pv3, maskB.unsqueeze(1).to_broadcast([128, 4, 128]), ALU.mult)
                yield
            if h == 0:
                dbg("xcTh", xcTh[:, 0, :], [128, NOWN])
                dbg("qT", qT, [128, NOWN])
                dbg("vpA", vpA, [128, NT, 257])
                dbg("P_A", P_A, [128, NT, 128])
                dbg("ktok", ktok, [128, NT, 128])
            for e2 in range(2):
                et = 2 * h + e2
                S.ts('dve', xcTh[:, e2, :], xcTh[:, e2, :], skp[:, et:et + 1], None, ALU.mult)

        def head_chain(h):
            S.act(Cbf[0][0], U_A[:, h, :], AF.Copy, scale=dprevA[:, h:h + 1])
            S.act(Cbf[1][0], U_B[:, h, :], AF.Copy, scale=dprevB[:, h:h + 1])

            def chunk_of(step, di):
                c = step if di == 0 else 31 - step
                return c // 2, c % 2

            def emit_st(step):
                for di in range(2):
                    k, hb = chunk_of(step, di)
                    p0 = 64 * hb
                    vp = vpA if di == 0 else vpB
                    S.mm(ps[2 * di + step % 2][:, 0:257], ktok[p0:p0 + 64, k, :], vp[p0:p0 + 64, k, :])

            emit_st(0)
            for step in range(32):
                if step + 1 < 32:
                    emit_st(step + 1)
                for di in range(2):
                    k, hb = chunk_of(step, di)
                    p0 = 64 * hb
                    vp = vpA if di == 0 else vpB
                    Pm = P_A if di == 0 else P_B
                    pout = ps[4 + 2 * di + step % 2]
                    S.mm(pout[:, 0:257], Pm[p0:p0 + 64, k, :], vp[p0:p0 + 64, k, :], start=True, stop=False)
                    S.mm(pout[:, 0:257], qT[:, 128 * k:128 * k + 128], Cbf[di][step % 2], start=False, stop=True)
                for di in range(2):
                    k, hb = chunk_of(step, di)
                    p0 = 64 * hb
                    U = U_A if di == 0 else U_B
                    hr = hrA if di == 0 else hrB
                    if step == 0:
                        dpv = (dprevA if di == 0 else dprevB)[:, h:h + 1]
                    else:
                        kp, hbp = chunk_of(step - 1, di)
                        dpv = dch[:, di, hbp, kp, h:h + 1]
                    S.stt(U[:, h, :], U[:, h, :], dpv, ps[2 * di + step % 2][:, 0:257], ALU.mult, ALU.add)
                    if step + 1 < 32:
                        S.act(Cbf[di][(step + 1) % 2], U[:, h, :], AF.Copy, scale=dch[:, di, hb, k, h:h + 1])
                lag = [step - 1] if step >= 1 else []
                if step == 31:
                    lag.append(31)
                for s2 in lag:
                    for di in range(2):
                        k2, hb2 = chunk_of(s2, di)
                        hr2 = hrA if di == 0 else hrB
                        S.copy('act' if di == 0 else 'dve', hr2[64 * hb2:64 * hb2 + 64, k2, :],
                               ps[4 + 2 * di + s2 % 2][64 * hb2:64 * hb2 + 64, 0:257])
            if h == 0:
                dbg("hrA", hrA, [128, NT, 257])
                dbg("hrB", hrB, [128, NT, 257])

        def head_post(h):
            uuh = xcThs[h % 2]
            soh = sohs[h % 2]
            for di in range(2):
                hr = hrA if di == 0 else hrB
                S.act(rr[:, di, :], hr[:, :, 256], AF.Abs)
                S.tt('dve', rr[:, di, :], rr[:, di, :], einv[:, di, :, h], ALU.max)
                S.recip(rr[:, di, :], rr[:, di, :])
            for k in range(NT):
                b = k % 2
                S.ts('dve', t1[b], hrA[:, k, 0:256], rr[:, 0, k:k + 1], None, ALU.mult)
                S.stt(hrA[:, k, 0:256], hrB[:, k, 0:256], rr[:, 1, k:k + 1], t1[b], ALU.mult, ALU.add)
                S.act(sqj, hrA[:, k, 0:256], AF.Square, accum=st8[:, k, 0:1])
                if k % 2 == 1:
                    yield
            S.ts('dve', st8[:, :, 1], st8[:, :, 0], 1.0 / DV, EPS, ALU.mult, ALU.add)
            S.act(st8[:, :, 1], st8[:, :, 1], AF.Sqrt)
            S.recip(st8[:, :, 1], st8[:, :, 1])
            for k4 in range(4):
                pT = psb[(k4 % 2)]
                for kk in range(4):
                    k = 4 * k4 + kk
                    b = k % 4
                    S.act(hn[b], hrA[:, k, 0:256], AF.Copy, scale=st8[:, k, 1:2])
                    for e2 in range(2):
                        S.transpose(pT[:, 512 * e2 + 128 * kk:512 * e2 + 128 * kk + 128],
                                    hn[b][:, 128 * e2:128 * e2 + 128], ident)
                for e2 in range(2):
                    et = 2 * h + e2
                    w = (2 * k4 + e2) % 2
                    yv = yaT[:, et, 512 * k4:512 * k4 + 512]
                    S.stt(yv, pT[:, 512 * e2:512 * e2 + 512], mng[:, et:et + 1], uuh[:, e2, 512 * k4:512 * k4 + 512],
                          ALU.mult, ALU.add)
                    S.tt('dve', yv, yv, soh[:, e2, 512 * k4:512 * k4 + 512], ALU.mult)
                yield

        def drive(gens):
            gens = list(gens)
            while gens:
                for g_ in list(gens):
                    try:
                        next(g_)
                    except StopIteration:
                        gens.remove(g_)

        drive([head_pre(0)])
        for h in range(H_M):
            head_chain(h)
            drive([head_post(h)] + ([head_pre(h + 1)] if h + 1 < H_M else []))
        dbg("yaT", yaT, [128, 8, NOWN])
        AR.pop()
        if stop_after == 'mstage':
            return finish(nc, S, y_out, dbg_outs, None)
        ybT = AR.alloc([4, NOWN], BF16)

        AR.push()
        vext2 = AR.alloc([19, 8, 128], BF16)
        AR.push()
        wnv = AR.alloc([8, 512], BF16)
        S.dma('pool', wnv, d_wnv.rearrange("p (a b) -> p a b", a=8))
        AR.pop()
        wqp = AR.alloc([8, 128], BF16)
        wkp = AR.alloc([8, 128], BF16)
        btp = AR.alloc([2, 13, 128], BF16)
        qnT = AR.alloc([NOWN], BF16)
        knT = AR.alloc([2304 + 32], BF16)
        sqT = [AR.alloc([512], BF16) for _ in range(4)]
        rsT4 = AR.alloc([4, 512], F32)
        rsT = [rsT4[:, i, :] for i in range(4)]
        PT = [AR.alloc([768], BF16) for _ in range(2)]
        PTm = AR.alloc([NOWN], BF16)
        lnb = [AR.alloc([512], F32) for _ in range(2)]
        recq = [AR.alloc([512], F32) for _ in range(2)]
        outsb = [rsT4, rsT4]
        zer = AR.alloc([128], BF16)
        qz = [AR.alloc([NOWN], BF16) for _ in range(2)]
        epsq = AR.alloc([2], F32)
        S.memset('dve', zer, 0.0)
        S.memset('dve', epsq[:, 0:1], DH * EPS)
        S.memset('dve', epsq[:, 1:2], EPS)
        vv = vext2[:, :, :, :].rearrange("p j (a b) c -> p j a b c", b=2)
        S.memset('dve', vv[:, :, :, 0, 64:128], 1.0)
        S.memset('dve', vv[:, :, :, 1, 0:64], 1.0)
        for j in range(19):
            npk = 128 if j < 18 else 32
            pv_ = ps[j % 2]
            for dt in range(8):
                lh = xnT[:, dt, OWN0 + 128 * j:OWN0 + 128 * j + 128] if j < 18 else xnTm[:, dt, :]
                S.mm(pv_[0:npk, :], lh, wnv[:, dt, :], start=(dt == 0), stop=(dt == 7))
            src = pv_[0:npk, :].rearrange("p (a b c) -> p a b c", a=4, b=2)
            dst = vext2[0:npk, j, :, :].rearrange("p (a b) c -> p a b c", b=2)
            S.copy('act', dst[:, :, 0, 0:64], src[:, :, 0, :])
            S.copy('dve', dst[:, :, 1, 64:128], src[:, :, 1, :])

        def qk_norm(dst, wts, col0, ncols, which, cnt):
            b = cnt % 4
            pq = ps[b]
            pss = ps[4 + b]
            for dt in range(8):
                S.mm(pq[:, 0:ncols], wts[:, dt, :], col0[dt], start=(dt == 0), stop=(dt == 7))
            S.act(sqT[b][:, 0:ncols], pq[:, 0:ncols], AF.Square)
            S.mm(pss[:, 0:ncols], blk64, sqT[b][:, 0:ncols])
            if which == 0:
                S.act(rsT[b][:, 0:ncols], pss[:, 0:ncols], AF.Ln, bias=epsq[:, 0:1], scale=1.0)
            else:
                S.act(rsT[b][:, 0:ncols], pss[:, 0:ncols], AF.Ln, bias=epsq[:, 1:2], scale=1.0 / DH)
            S.act(rsT[b][:, 0:ncols], rsT[b][:, 0:ncols], AF.Exp, scale=-0.5)
            S.stt(dst, pq[:, 0:ncols], qkg[:, which:which + 1], rsT[b][:, 0:ncols], ALU.mult, ALU.mult)

        def bias_pos(i, j):
            if i == 0:
                return j
            if i == 1:
                return 4 + j
            return 10 - (j - i)

        Iof = {j: [i for i in range(NT) if j in [jj for jj, _ in key_tiles(i)]] for j in range(18)}
        ncnt = 0
        sc = 0
        for pr in range(4):
            S.dma('pool', wqp, d_wnq[pr].rearrange("p (a b) -> p a b", a=8))
            S.dma('pool', wkp, d_wnk[pr].rearrange("p (a b) -> p a b", a=8))
            S.dma('pool', btp, d_bt[pr].rearrange("p (a b c) -> p a b c", a=2, b=13))
            for tb in range(4):
                c0 = OWN0 + 512 * tb
                qk_norm(qnT[:, 512 * tb:512 * tb + 512], wqp, [xnT[:, dt, c0:c0 + 512] for dt in range(8)], 512, 0, ncnt)
                ncnt += 1
            for tb in range(5):
                c0 = OWN0 + 512 * tb
                n = 512 if tb < 4 else 256
                qk_norm(knT[:, 512 * tb:512 * tb + n], wkp, [xnT[:, dt, c0:c0 + n] for dt in range(8)], n, 1, ncnt)
                ncnt += 1
            qk_norm(knT[:, 2304:2336], wkp, [xnTm[:, dt, :] for dt in range(8)], 32, 1, ncnt)
            ncnt += 1
            if pr == 0:
                dbg("qnT", qnT, [128, NOWN])
                dbg("knT", knT, [128, 2336])
            for hh in range(2):
                S.memset('dve', qz[hh][64 - 64 * hh:128 - 64 * hh, :], 0.0)
                S.copy('dve', qz[hh][64 * hh:64 * hh + 64, :], qnT[64 * hh:64 * hh + 64, :])
            for hh in range(2):
                h = 2 * pr + hh
                bp = 64 * hh
                for b in range(4):
                    S.mm(ps[b][:, :], zer, qnT[:, 512 * b:512 * b + 512], start=True, stop=True)
                for b in range(4):
                    sA = ps[4 + 2 * (sc % 2)]
                    sc += 1
                    S.mm(sA[0:32, :], knT[bp:bp + 64, 2304:2336], qnT[bp:bp + 64, 512 * b:512 * b + 512])
                    S.act(PTm[0:32, 512 * b:512 * b + 512], sA[0:32, :], AF.Exp, bias=mbc[0:32, h:h + 1], scale=1.0)
                    S.mm(ps[b][:, :], vext2[0:32, 18, h, :], PTm[0:32, 512 * b:512 * b + 512], start=False, stop=False,
                         sgc=True)
                def emit_scores(j, slot):
                    I = Iof[j]
                    i0, n = I[0], len(I)
                    nq = 128 * n
                    sA = ps[4 + 2 * slot]
                    sB = ps[5 + 2 * slot]
                    pt = PT[slot]
                    segs = [(sA, 0, min(nq, 512))] + ([(sB, 512, nq)] if nq > 512 else [])
                    for (bank, lo, hi) in segs:
                        S.mm(bank[:, 0:hi - lo], knT[:, 128 * j:128 * j + 128],
                             qz[hh][:, 128 * i0 + lo:128 * i0 + hi], start=True, stop=False)
                        idxs = [ix for ix in range(n) if lo <= 128 * ix < hi]
                        runs = []
                        for ix in idxs:
                            p_ = bias_pos(I[ix], j)
                            if runs and runs[-1][1] + runs[-1][2] == p_ and runs[-1][0] + runs[-1][2] == ix:
                                runs[-1][2] += 1
                            else:
                                runs.append([ix, p_, 1])
                        for ri, (ix, p_, r) in enumerate(runs):
                            S.mm(bank[:, 128 * ix - lo:128 * (ix + r) - lo], ident, btp[:, hh, p_:p_ + r, :],
                                 start=False, stop=(ri == len(runs) - 1))
                        S.act(pt[:, lo:hi], bank[:, 0:hi - lo], AF.Exp)

                def emit_pv(j, slot):
                    I = Iof[j]
                    i0, n = I[0], len(I)
                    pt = PT[slot]
                    for b in range(4):
                        ilo, ihi = max(i0, 4 * b), min(i0 + n, 4 * b + 4)
                        if ilo >= ihi:
                            continue
                        S.mm(ps[b][:, 128 * (ilo - 4 * b):128 * (ihi - 4 * b)], vext2[:, j, h, :],
                             pt[:, 128 * (ilo - i0):128 * (ihi - i0)], start=False, stop=False, sgc=True)

                emit_scores(0, 0)
                for j in range(18):
                    if j + 1 < 18:
                        emit_scores(j + 1, (j + 1) % 2)
                    emit_pv(j, j % 2)
                osb = outsb[h % 2]
                for b in range(4):
                    S.copy('act' if b % 2 == 0 else 'dve', osb[:, b, :], ps[b][:, :])
                for b in range(4):
                    bb = b % 2
                    num0, den0 = (0, 64) if hh == 0 else (64, 0)
                    S.act(lnb[bb][num0:num0 + 64, :], osb[den0:den0 + 64, b, :], AF.Ln)
                    S.act(recq[bb][num0:num0 + 64, :], lnb[bb][num0:num0 + 64, :], AF.Exp, scale=-1.0)
                    S.tt('dve', ybT[num0:num0 + 64, pr, 512 * b:512 * b + 512], osb[num0:num0 + 64, b, :],
                         recq[bb][num0:num0 + 64, :], ALU.mult)
        dbg("ybT", ybT, [128, 4, NOWN])
        AR.pop()
        if stop_after == 'na':
            return finish(nc, S, y_out, dbg_outs, None)

        AR.push()
        assert AR.off < MIX_OFF
        mixT = arena_t[:, MIX_OFF // 4:ARENA_BYTES // 4].bitcast(BF16).rearrange("p (a b) -> p a b", a=8)
        wgab = [AR.alloc([8, 128], BF16) for _ in range(2)]
        wgbb = [AR.alloc([8, 128], BF16) for _ in range(2)]
        wab = [AR.alloc([8, 128], BF16) for _ in range(2)]
        wbb = [AR.alloc([4, 128], BF16) for _ in range(2)]
        sga = [AR.alloc([512], F32) for _ in range(2)]
        sgb = [AR.alloc([512], F32) for _ in range(2)]
        tg1 = [AR.alloc([512], F32) for _ in range(2)]
        tg2 = [AR.alloc([512], F32) for _ in range(2)]
        WOUT_OFF = MIX_OFF - 8 * 1024 * 2
        assert AR.off <= WOUT_OFF, (AR.off, WOUT_OFF)
        wout = arena_t[:, WOUT_OFF // 4:MIX_OFF // 4].bitcast(BF16).rearrange("p (a b) -> p a b", a=8)
        for q4 in range(4):
            S.dma('pool', wout[:, 2 * q4:2 * q4 + 2, :],
                  d_wout.rearrange("p (a b) -> p a b", a=8)[:, 2 * q4:2 * q4 + 2, :])
        gc = 0
        for blk in range(8):
            w = blk % 2
            S.dma('pool', wgab[w], d_wga[blk].rearrange("p (a b) -> p a b", a=8))
            S.dma('pool', wgbb[w], d_wgb[blk].rearrange("p (a b) -> p a b", a=8))
            S.dma('pool', wab[w], d_wa[blk].rearrange("p (a b) -> p a b", a=8))
            S.dma('pool', wbb[w], d_wb[blk].rearrange("p (a b) -> p a b", a=4))
            for tb in range(4):
                b = gc % 2
                gc += 1
                c0 = OWN0 + 512 * tb
                pga, pgb, pa_, pb_ = ps[4 * b], ps[4 * b + 1], ps[4 * b + 2], ps[4 * b + 3]
                for dt in range(8):
                    S.mm(pga[:, :], wgab[w][:, dt, :], xnT[:, dt, c0:c0 + 512], start=(dt == 0), stop=(dt == 7))
                for dt in range(8):
                    S.mm(pgb[:, :], wgbb[w][:, dt, :], xnT[:, dt, c0:c0 + 512], start=(dt == 0), stop=(dt == 7))
                for et in range(8):
                    S.mm(pa_[:, :], wab[w][:, et, :], yaT[:, et, 512 * tb:512 * tb + 512], start=(et == 0), stop=(et == 7))
                for c4 in range(4):
                    S.mm(pb_[:, :], wbb[w][:, c4, :], ybT[:, c4, 512 * tb:512 * tb + 512], start=(c4 == 0), stop=(c4 == 3))
                S.act(sga[b], pga[:, :], AF.Sigmoid)
                S.act(sgb[b], pgb[:, :], AF.Sigmoid)
                S.tt('dve', tg1[b], sga[b], pa_[:, :], ALU.mult)
                S.tt('dve', tg2[b], sgb[b], pb_[:, :], ALU.mult)
                S.tt('dve', mixT[:, blk, 512 * tb:512 * tb + 512], tg1[b], tg2[b], ALU.add)
        dbg("mixT", mixT, [128, 8, NOWN])
        AR.pop()
        if stop_after == 'g':
            return finish(nc, S, y_out, dbg_outs, None)

        AR.off = OFF_XNT
        h1 = AR.alloc([NT, 1024], F32)
        xn2T = AR.alloc([8, NOWN], BF16)
        OFF_OTMP = AR.off
        xq = [AR.alloc([1024], F32) for _ in range(4)]
        xn2b = [AR.alloc([1024], BF16) for _ in range(4)]
        sqj2 = AR.alloc([1024], BF16)
        ss2 = AR.alloc([16], F32)
        assert AR.off <= MIX_OFF - 16 * 1024, AR.off
        OFF_FFN = AR.off
        tcnt = [0]

        def O_proj(t0):
            for i in range(4):
                t = t0 + i
                S.dma('sp', xq[i], xe[OWN0 + 128 * t:OWN0 + 128 * t + 128, :])
            for i in range(4):
                t = t0 + i
                for half in range(2):
                    po = ps[(2 * i + half) % 6]
                    for dt in range(8):
                        S.mm(po[:, :], mixT[:, dt, 128 * t:128 * t + 128], wout[:, dt, 512 * half:512 * half + 512],
                             start=(dt == 0), stop=(dt == 7))
                    S.tt('dve', h1[:, t, 512 * half:512 * half + 512], po[:, :], xq[i][:, 512 * half:512 * half + 512],
                         ALU.add)
                S.act(sqj2, h1[:, t, :], AF.Square, accum=ss2[:, 8 * ((t0 // 4) % 2) + i:8 * ((t0 // 4) % 2) + i + 1])

        def O_norm(t0):
            o8 = 8 * ((t0 // 4) % 2)
            rsv = ss2[:, o8 + 4:o8 + 8]
            S.ts('dve', rsv, ss2[:, o8:o8 + 4], 1.0 / D, EPS, ALU.mult, ALU.add)
            S.act(rsv, rsv, AF.Sqrt)
            S.recip(rsv, rsv)
            for i in range(4):
                t = t0 + i
                S.ts('dve', xn2b[i], h1[:, t, :], ss2[:, o8 + 4 + i:o8 + 5 + i], None, ALU.mult)

        def O_tr(t0):
            for i in range(4):
                t = t0 + i
                pb = psb[6 + (tcnt[0] % 2)]
                tcnt[0] += 1
                for dt in range(8):
                    S.transpose(pb[:, dt * 128:dt * 128 + 128], xn2b[i][:, dt * 128:(dt + 1) * 128], ident)
                S.tt('dve', xn2T[:, :, 128 * t:128 * t + 128], pb[:, :].rearrange("p (a b) -> p a b", a=8),
                     g2.unsqueeze(2).to_broadcast([128, 8, 128]), ALU.mult)

        O_proj(0)
        for t0 in range(0, NT, 4):
            O_norm(t0)
            if t0 + 4 < NT:
                O_proj(t0 + 4)
            O_tr(t0)
        dbg("h1", h1, [128, NT, 1024])
        AR.off = WOUT_OFF
        wf2r = [AR.alloc([4, 1024], BF16) for _ in range(2)]
        AR.off = OFF_OTMP
        wf1r = [AR.alloc([4, 8, 128], BF16) for _ in range(2)]
        AR.off = MIX_OFF
        zTg = [AR.alloc([4, NOWN], BF16) for _ in range(2)]
        AR.off = OFF_FFN
        rl = [AR.alloc([512], BF16) for _ in range(2)]
        assert AR.off <= WOUT_OFF

        def ffn_load(G):
            S.dma('pool', wf1r[G % 2], d_wf1[4 * G:4 * G + 4].rearrange("a p (b c) -> p a b c", b=8))
            S.dma('pool', wf2r[G % 2], d_wf2[4 * G:4 * G + 4].rearrange("a p c -> p a c"))

        ffn_load(0)
        zc = 0
        bc = 0
        for G in range(8):
            if G + 1 < 8:
                ffn_load(G + 1)
            g2_ = G % 2
            for fbi in range(4):
                for tb in range(4):
                    b = zc % 2
                    pz = ps[zc % 8]
                    zc += 1
                    for dt in range(8):
                        S.mm(pz[:, :], wf1r[g2_][:, fbi, dt, :], xn2T[:, dt, 512 * tb:512 * tb + 512],
                             start=(dt == 0), stop=(dt == 7))
                    S.act(rl[b], pz[:, :], AF.Relu)
                    S.tt('dve', zTg[g2_][:, fbi, 512 * tb:512 * tb + 512], rl[b], rl[b], ALU.mult)
            for ts_ in range(4):
                for half in range(2):
                    bs = 4 * (bc % 2)
                    bc += 1
                    for fbi in range(4):
                        for k4 in range(4):
                            t = 4 * ts_ + k4
                            S.mm(ps[bs + k4][:, :], zTg[g2_][:, fbi, 128 * t:128 * t + 128],
                                 wf2r[g2_][:, fbi, 512 * half:512 * half + 512], start=(fbi == 0), stop=(fbi == 3))
                    for k4 in range(4):
                        t = 4 * ts_ + k4
                        hv = h1[:, t, 512 * half:512 * half + 512]
                        S.tt('dve', hv, ps[bs + k4][:, :], hv, ALU.add)
        out_toks = []
        for t in range(NT):
            out_toks.append(S.dma('sp' if t % 2 == 0 else 'act', y_out[128 * t:128 * t + 128, :], h1[:, t, :]))
        return finish(nc, S, y_out, dbg_outs, out_toks)


def finish(nc, S, y_out, dbg_outs, out_toks):
    toks = list(dbg_outs.values())
    if out_toks:
        toks += out_toks
    S.wait_tokens('sp', toks)
    S.emit()
    return nc


def _colvec(v):
    return np.ascontiguousarray(np.asarray(v, np.float32).reshape(8, 128).T)


def _rows_ptc(w):
    T = w.shape[0] // 128
    return np.ascontiguousarray(w.reshape(T, 128, w.shape[1]).transpose(1, 0, 2).reshape(128, -1))


def _col_blocks(w, c0, nblk, bw=128):
    return np.ascontiguousarray(np.stack([_rows_ptc(w[:, c0 + bw * i:c0 + bw * (i + 1)]) for i in range(nblk)]))


def _na_tables(hf, rpb, meta_bias):
    def tile(i, j):
        out = np.full((NH, 128, 128), NEG, np.float32)
        cl = np.arange(64)
        for kr in range(2):
            for qr in range(2):
                krow_l, qrow_l = 2 * j + kr, 2 * i + qr
                if hf == 0:
                    krow, qrow, kcol, qcol = krow_l, qrow_l, cl, cl
                else:
                    krow, qrow, kcol, qcol = 63 - krow_l, 63 - qrow_l, 63 - cl, 63 - cl
                r0 = min(max(qrow - 4, 0), 56)
                if not (r0 <= krow < r0 + 8):
                    continue
                win0 = np.clip(qcol - 8, 0, 48)
                ok = (kcol[:, None] >= win0[None, :]) & (kcol[:, None] < win0[None, :] + 16)
                dr = krow - qrow + 7
                dc = np.clip(kcol[:, None] - qcol[None, :], -15, 15) + 15
                vals = rpb[:, dr, dc]
                out[:, kr * 64:(kr + 1) * 64, qr * 64:(qr + 1) * 64] = np.where(ok[None], vals, NEG)
        return out
    kinds = [tile(0, j) for j in range(4)] + [tile(1, j) for j in range(4)] + [tile(8, 8 + d) for d in (2, 1, 0, -1, -2)]
    BT = np.stack(kinds)
    bt = BT.reshape(13, 4, 2, 128, 128).transpose(1, 3, 2, 0, 4).reshape(4, 128, 13 * 2 * 128)
    MB = np.full((NH, 32), NEG, np.float32)
    if hf == 0:
        MB[:, 0:16] = meta_bias
    else:
        MB[:, 16:32] = meta_bias[:, ::-1]
    return np.ascontiguousarray(bt), np.ascontiguousarray(MB.T)


def _const_masks():
    j = np.arange(128)[:, None]
    t = np.arange(128)[None, :]
    same = (j // 64) == (t // 64)
    cm = np.zeros((128, 6, 128), np.float32)
    cm[:, 0, :] = same & (j <= t)
    cm[:, 1, :] = same & (j >= t)
    cm[:, 2, :] = (j < 64) & (t >= 0)
    cm[:, 3, :] = (j >= 64) & (t >= 0)
    cm[:, 4, :] = (j == t)
    cm[:, 5, :] = same
    return cm


def prep_inputs(inp):
    f = lambda a: np.ascontiguousarray(np.asarray(a, np.float32))
    w_in = f(inp['w_in'])
    shared = {
        'wxm': _rows_ptc(w_in[:, 0:1024]),
        'wo': _col_blocks(w_in, 1024, 8),
        'wqm': np.ascontiguousarray(f(inp['mlstm_wq']).reshape(4, 2, 128, 128).transpose(2, 0, 1, 3).reshape(128, -1)),
        'wkm': np.ascontiguousarray(f(inp['mlstm_wk']).reshape(4, 2, 128, 128).transpose(2, 0, 1, 3).reshape(128, -1)),
        'wnq': _col_blocks(w_in, 2064, 4),
        'wnk': _col_blocks(w_in, 2576, 4),
        'wnv': _rows_ptc(w_in[:, 3088:3600]),
        'wga': _col_blocks(w_in, 3600, 8),
        'wgb': _col_blocks(w_in, 4624, 8),
        'wa': _col_blocks(f(inp['w_branch_a']), 0, 8),
        'wb': _col_blocks(f(inp['w_branch_b']), 0, 8),
        'wout': _rows_ptc(f(inp['w_out'])),
        'wf1': _col_blocks(f(inp['w_ff1']), 0, 32),
        'wf2': np.ascontiguousarray(f(inp['w_ff2']).reshape(32, 128, 1024)),
        'cmask': _const_masks(),
        'qkg': np.ascontiguousarray(np.stack([np.tile(f(inp['na_q_norm_g']), 2), np.tile(f(inp['na_k_norm_g']), 2)], axis=1)),
    }
    x = f(inp['x'])
    meta = f(inp['meta_tokens'])
    cwfull = f(inp['mlstm_conv_w'])[:, 0, :]
    gb = f(inp['mlstm_gate_b']).reshape(16)
    gcols = w_in[:, 2048:2064]
    zero1 = np.zeros((1, D), np.float32)
    z16 = np.zeros((16, D), np.float32)
    maps = []
    tabs = {}
    for core in range(8):
        b, hf = core // 2, core % 2
        m = dict(shared)
        if hf == 0:
            xe = np.concatenate([zero1, meta, x[b], z16, zero1])
            vl = (1.0, 0.0)
            cwl, gc, gbl = cwfull, gcols, gb
        else:
            xe = np.concatenate([zero1, z16, x[b][::-1], meta[::-1], zero1])
            vl = (0.0, 1.0)
            cwl = cwfull[::-1]
            gc = np.concatenate([gcols[:, 8:16], gcols[:, 0:8]], axis=1)
            gbl = np.concatenate([gb[8:16], gb[0:8]])
        m['xe'] = np.ascontiguousarray(xe)
        m['valid'] = np.ascontiguousarray(np.tile(np.array(vl, np.float32)[None, :], (128, 1)))
        m['gate_b'] = np.ascontiguousarray(np.tile(gbl[None, :], (128, 1)))
        m['wg'] = _rows_ptc(np.ascontiguousarray(gc))
        m['vecs'] = np.ascontiguousarray(np.concatenate(
            [_colvec(inp['norm1_g']), _colvec(cwl[0]), _colvec(cwl[1]), _colvec(cwl[2]), _colvec(inp['mlstm_conv_b']),
             _colvec(f(inp['mlstm_norm_g']).reshape(-1)), _colvec(inp['mlstm_skip']), _colvec(inp['norm2_g'])], axis=1))
        if hf not in tabs:
            tabs[hf] = _na_tables(hf, f(inp['na_rpb']), f(inp['na_meta_bias']))
        m['bt'], m['mb'] = tabs[hf]
        maps.append(m)
    return maps


_NC_CACHE = {}


def kernel(**inputs):
    maps = prep_inputs(inputs)
    if 'nc' not in _NC_CACHE:
        _NC_CACHE['nc'] = build_nc()
    nc = _NC_CACHE['nc']
    res = run_bass_kernel_spmd(nc, maps, core_ids=list(range(8)))
    out = np.zeros((4, 4096, D), np.float32)
    for core in range(8):
        b, hf = core // 2, core % 2
        y = np.asarray(res.results[core]["y"], np.float32)
        if hf == 0:
            out[b, 0:NOWN] = y
        else:
            out[b, NOWN:] = y[::-1]
    return out
```

```python
import contextlib
import numpy as np
import concourse.bass as bass
import concourse.mybir as mybir
from concourse.bass_utils import run_bass_kernel_spmd

F32 = mybir.dt.float32
BF16 = mybir.dt.bfloat16
AF = mybir.ActivationFunctionType
ALU = mybir.AluOpType
DSZ = {F32: 4, BF16: 2}

D = 1024
NOWN = 2048
NT = 16
OWN0 = 17
OTH0 = 17 + 2048
POST0 = 17 + 4096
XE_ROWS = 4130
NXC = 17 + 2048 + 256
H_M, DV, DK = 4, 256, 128
NH, DH = 8, 64
EPS = 1e-6
NEG = -30000.0
N_DMA_SEMS = 12
SAME_ENGINE_SYNC = True


class Sched:
    def __init__(self, nc):
        self.nc = nc
        self.engs = ('pe', 'act', 'dve', 'pool', 'sp')
        self.prog = {k: [] for k in self.engs}
        self.count = {k: 0 for k in self.engs}
        self.waited = {k: {} for k in self.engs}
        self.recs = {}
        self.dma_q = ('sp', 'pool', 'act')
        self.dma_uses = {q: [0] * N_DMA_SEMS for q in self.dma_q}
        self.dma_rr = {q: 0 for q in self.dma_q}
        self.n_inst = 0
        self.dram = set()

    def _box(self, ap):
        name = ap.tensor.name
        if name in self.dram:
            return None
        if name.startswith('ps'):
            return name, 0, 128, 0, 2048
        dims = ap.ap
        sz = DSZ[ap.dtype]
        shp = ap.tensor.shape
        row = 1
        for s in list(shp)[1:]:
            row *= int(s)
        off = int(ap.offset)
        p0 = off // row
        b0 = (off % row) * sz
        pc = int(dims[0][1])
        ext = 0
        for st, cn in dims[1:]:
            ext += (int(cn) - 1) * abs(int(st))
        b1 = b0 + (ext + 1) * sz
        return name, p0, p0 + pc, b0, b1

    def _deps(self, eng, ins, outs):
        deps = {}

        def add(sk, v):
            if deps.get(sk, 0) < v:
                deps[sk] = v
        rb = [b for b in (self._box(a) for a in ins) if b is not None]
        wb = [b for b in (self._box(a) for a in outs) if b is not None]
        wb = wb + [b for b in rb if b[0].startswith('ps') and b not in wb]
        rb = [b for b in rb if not b[0].startswith('ps')]
        for (name, p0, p1, b0, b1) in rb:
            for r in self.recs.get(name, ()):
                if r[0] < p1 and p0 < r[1] and r[2] < b1 and b0 < r[3]:
                    if r[4] is not None:
                        add(*r[4])
        for (name, p0, p1, b0, b1) in wb:
            for r in self.recs.get(name, ()):
                if r[0] < p1 and p0 < r[1] and r[2] < b1 and b0 < r[3]:
                    if r[4] is not None:
                        add(*r[4])
                    for sk, v in r[5].items():
                        add(sk, v)
        waits = []
        for sk, v in deps.items():
            if sk == eng and (eng == 'pe' or not SAME_ENGINE_SYNC):
                continue
            if self.waited[eng].get(sk, 0) >= v:
                continue
            self.waited[eng][sk] = v
            waits.append((sk, v))
        return waits, rb, wb

    def _commit(self, tok, rb, wb):
        for (name, p0, p1, b0, b1) in wb:
            lst = self.recs.setdefault(name, [])
            keep = []
            for r in lst:
                if r[0] >= p0 and r[1] <= p1 and r[2] >= b0 and r[3] <= b1:
                    continue
                keep.append(r)
            keep.append([p0, p1, b0, b1, tok, {}])
            self.recs[name] = keep
        for (name, p0, p1, b0, b1) in rb:
            lst = self.recs.setdefault(name, [])
            best = None
            for r in lst:
                if r[0] <= p0 and r[1] >= p1 and r[2] <= b0 and r[3] >= b1 and r[4] != tok:
                    sz = (r[1] - r[0]) * (r[3] - r[2])
                    if best is None or sz < best[0]:
                        best = (sz, r)
            if best is not None:
                r = best[1]
                if r[5].get(tok[0], 0) < tok[1]:
                    r[5][tok[0]] = tok[1]
                continue
            for r in lst:
                if r[0] < p1 and p0 < r[1] and r[2] < b1 and b0 < r[3]:
                    if r[4] == tok:
                        continue
                    if r[5].get(tok[0], 0) < tok[1]:
                        r[5][tok[0]] = tok[1]
            lst.append([p0, p1, b0, b1, None, {tok[0]: tok[1]}])

    def op(self, eng, fn, ins, outs):
        waits, rb, wb = self._deps(eng, ins, outs)
        self.count[eng] += 1
        tok = (eng, self.count[eng])
        self.prog[eng].append((waits, fn, tok))
        self._commit(tok, rb, wb)
        self.n_inst += 1
        return tok

    def dma(self, eng, out, in_):
        waits, rb, wb = self._deps(eng, [in_], [out])
        i = self.dma_rr[eng]
        self.dma_rr[eng] = (i + 1) % N_DMA_SEMS
        self.dma_uses[eng][i] += 1
        sk = ('dma', eng, i)
        prev = 16 * (self.dma_uses[eng][i] - 1)
        if prev > 0 and self.waited[eng].get(sk, 0) < prev:
            self.waited[eng][sk] = prev
            waits.append((sk, prev))
        tok = (sk, 16 * self.dma_uses[eng][i])
        self.prog[eng].append((waits, (lambda e, o=out, a=in_: e.dma_start(out=o, in_=a)), tok))
        self._commit(tok, rb, wb)
        self.n_inst += 1
        return tok

    def wait_tokens(self, eng, toks):
        waits = []
        for sk, v in toks:
            if self.waited[eng].get(sk, 0) >= v:
                continue
            self.waited[eng][sk] = v
            waits.append((sk, v))
        if waits:
            self.prog[eng].append((waits, None, None))

    def mm(self, out, lhsT, rhs, start=True, stop=True, sgc=False):
        if sgc:
            return self.op('pe', lambda e: e.matmul(out, lhsT=lhsT, rhs=rhs, start=start, stop=stop,
                                                    skip_group_check=True), [lhsT, rhs], [out])
        return self.op('pe', lambda e: e.matmul(out, lhsT=lhsT, rhs=rhs, start=start, stop=stop),
                       [lhsT, rhs], [out])

    def transpose(self, out, in_, ident):
        return self.op('pe', lambda e: e.transpose(out=out, in_=in_, identity=ident), [in_, ident], [out])

    def act(self, out, in_, func, bias=None, scale=None, accum=None):
        kw = {}
        ins = [in_]
        outs = [out]
        if bias is not None:
            kw['bias'] = bias
            if not isinstance(bias, (int, float)):
                ins.append(bias)
        if scale is not None:
            kw['scale'] = scale
            if not isinstance(scale, (int, float)):
                ins.append(scale)
        if accum is not None:
            kw['accum_out'] = accum
            outs.append(accum)
        return self.op('act', lambda e: e.activation(out=out, in_=in_, func=func, **kw), ins, outs)

    def ts(self, eng, out, in0, s1, s2, op0, op1=None):
        ins = [in0] + [s for s in (s1, s2) if s is not None and not isinstance(s, (int, float))]
        if op1 is None:
            fn = lambda e: e.tensor_scalar(out=out, in0=in0, scalar1=s1, scalar2=None, op0=op0)
        else:
            fn = lambda e: e.tensor_scalar(out=out, in0=in0, scalar1=s1, scalar2=s2, op0=op0, op1=op1)
        return self.op(eng, fn, ins, [out])

    def tt(self, eng, out, in0, in1, op):
        return self.op(eng, lambda e: e.tensor_tensor(out=out, in0=in0, in1=in1, op=op), [in0, in1], [out])

    def stt(self, out, in0, scalar, in1, op0, op1):
        ins = [in0, in1] + ([] if isinstance(scalar, (int, float)) else [scalar])
        return self.op('dve', lambda e: e.scalar_tensor_tensor(out=out, in0=in0, scalar=scalar, in1=in1,
                                                               op0=op0, op1=op1), ins, [out])

    def copy(self, eng, out, in_):
        if eng == 'act':
            return self.op('act', lambda e: e.copy(out=out, in_=in_), [in_], [out])
        return self.op(eng, lambda e: e.tensor_copy(out=out, in_=in_), [in_], [out])

    def recip(self, out, in_):
        return self.op('dve', lambda e: e.reciprocal(out=out, in_=in_), [in_], [out])

    def memset(self, eng, ap, val):
        return self.op(eng, lambda e: e.memset(ap, val), [], [ap])

    def emit(self):
        nc = self.nc
        engmap = {'pe': nc.tensor, 'act': nc.scalar, 'dve': nc.vector, 'pool': nc.gpsimd, 'sp': nc.sync}
        with contextlib.ExitStack() as st:
            sems = {}
            for e in self.engs:
                sems[e] = st.enter_context(nc.semaphore("sem_" + e))
            for q in self.dma_q:
                for i in range(N_DMA_SEMS):
                    sems[('dma', q, i)] = st.enter_context(nc.semaphore("sem_dma_%s%d" % (q, i)))
            block = st.enter_context(nc.Block())

            def replay(ename, e):
                for waits, fn, tok in self.prog[ename]:
                    for sk, v in waits:
                        e.wait_ge(sems[sk], v)
                    if fn is None:
                        continue
                    inst = fn(e)
                    if isinstance(tok[0], tuple):
                        inst.then_inc(sems[tok[0]], 16)
                    else:
                        inst.then_inc(sems[tok[0]], 1)

            block.tensor(lambda e: replay('pe', e))
            block.scalar(lambda e: replay('act', e))
            block.vector(lambda e: replay('dve', e))
            block.gpsimd(lambda e: replay('pool', e))
            block.sync(lambda e: replay('sp', e))


class Arena:
    def __init__(self, t, nbytes):
        self.t = t
        self.nbytes = nbytes
        self.off = 0
        self.stack = []

    def push(self):
        self.stack.append(self.off)

    def pop(self):
        self.off = self.stack.pop()

    def alloc(self, shape, dtype):
        n = 1
        for s in shape:
            n *= s
        nb = n * DSZ[dtype]
        nb = (nb + 63) // 64 * 64
        assert self.off + nb <= self.nbytes, ("arena overflow", self.off, nb, self.nbytes)
        a = self.t[:, self.off // 4:(self.off + nb) // 4]
        self.off += nb
        if dtype != F32:
            a = a.bitcast(dtype)
        a = a[:, 0:n]
        if len(shape) == 2:
            a = a.rearrange("p (a b) -> p a b", a=shape[0])
        elif len(shape) == 3:
            a = a.rearrange("p (a b c) -> p a b c", a=shape[0], b=shape[1])
        elif len(shape) == 4:
            a = a.rearrange("p (a b c d) -> p a b c d", a=shape[0], b=shape[1], c=shape[2])
        return a


def key_tiles(i):
    if i == 0:
        return [(j, j) for j in range(4)]
    if i == 1:
        return [(j, 4 + j) for j in range(4)]
    return [(i + d, 8 + d + 2) for d in range(-2, 3)]


def build_nc(debug=None, stop_after=None):
    nc = bass.Bass("TRN2", target_bir_lowering=False)
    S = Sched(nc)
    dbg_outs = {}

    def din(name, shape):
        t = nc.dram_tensor(name, list(shape), F32, kind="ExternalInput")
        S.dram.add(name)
        return t.ap()

    xe = din("xe", [XE_ROWS, D])
    d_vec = din("vecs", [128, 64])
    d_valid = din("valid", [128, 2])
    d_gateb = din("gate_b", [128, 16])
    d_cmask = din("cmask", [128, 6, 128])
    d_mb = din("mb", [32, 8])
    d_bt = din("bt", [4, 128, 13 * 2 * 128])
    d_wxm = din("wxm", [128, 8 * 1024])
    d_wo = din("wo", [8, 128, 8 * 128])
    d_wg = din("wg", [128, 8 * 16])
    d_wq = din("wqm", [128, 4 * 2 * 128])
    d_wk = din("wkm", [128, 4 * 2 * 128])
    d_wnq = din("wnq", [4, 128, 8 * 128])
    d_wnk = din("wnk", [4, 128, 8 * 128])
    d_wnv = din("wnv", [128, 8 * 512])
    d_wga = din("wga", [8, 128, 8 * 128])
    d_wgb = din("wgb", [8, 128, 8 * 128])
    d_wa = din("wa", [8, 128, 8 * 128])
    d_wb = din("wb", [8, 128, 4 * 128])
    d_wout = din("wout", [128, 8 * 1024])
    d_wf1 = din("wf1", [32, 128, 8 * 128])
    d_wf2 = din("wf2", [32, 128, 1024])
    y_out = nc.dram_tensor("y", [NOWN, D], F32, kind="ExternalOutput")
    S.dram.add("y")
    y_out = y_out.ap()

    with contextlib.ExitStack() as st:
        ARENA_BYTES = 207 * 1024
        arena_t = st.enter_context(nc.sbuf_tensor("arena", [128, ARENA_BYTES // 4], F32))
        AR = Arena(arena_t, ARENA_BYTES)
        ps = [st.enter_context(nc.psum_tensor("ps%d" % i, [128, 512], F32)) for i in range(8)]
        psb = [p[:].bitcast(BF16) for p in ps]

        def dbg(name, ap, shape):
            if debug is None or name not in debug:
                return
            t = nc.dram_tensor("dbg_" + name, list(shape), F32, kind="ExternalOutput")
            S.dram.add("dbg_" + name)
            dbg_outs[name] = S.dma('pool', t.ap(), ap)

        vec = AR.alloc([64], F32)
        valid = AR.alloc([2], F32)
        gateb = AR.alloc([16], F32)
        cmf = AR.alloc([4, 128], F32)
        cmb = AR.alloc([4, 128], BF16)
        mbc = AR.alloc([8], F32)
        S.dma('sp', vec, d_vec)
        S.dma('sp', valid, d_valid)
        S.dma('sp', gateb, d_gateb)
        S.dma('sp', cmf, d_cmask[:, 0:4, :])
        S.dma('pool', cmb[:, 0:2, :], d_cmask[:, 0:2, :])
        S.dma('pool', cmb[:, 2:4, :], d_cmask[:, 4:6, :])
        S.dma('sp', mbc[0:32, :], d_mb)
        triA_f, triB_f = cmf[:, 0, :], cmf[:, 1, :]
        ones_f = [cmf[:, 2, :], cmf[:, 3, :]]
        maskA, maskB, ident, blk64 = cmb[:, 0, :], cmb[:, 1, :], cmb[:, 2, :], cmb[:, 3, :]
        g1 = vec[:, 0:8]
        cw = [vec[:, 8:16], vec[:, 16:24], vec[:, 24:32]]
        cb = vec[:, 32:40]
        mng = vec[:, 40:48]
        skp = vec[:, 48:56]
        g2 = vec[:, 56:64]
        qkg = AR.alloc([2], F32)
        d_qkg = din("qkg", [128, 2])
        S.dma('sp', qkg, d_qkg)

        dcw = AR.alloc([3, 8, 128], BF16)
        for k3 in range(3):
            for et in range(8):
                S.ts('dve', dcw[:, k3, et, :], ident, cw[k3][:, et:et + 1], None, ALU.mult)
        U_A = AR.alloc([4, 257], F32)
        U_B = AR.alloc([4, 257], F32)
        dprevA = AR.alloc([4], F32)
        dprevB = AR.alloc([4], F32)
        OFF_XNT = AR.off
        xnT = AR.alloc([8, NXC + 1], BF16)
        S.memset('dve', xnT[:, :, 0:1], 0.0)
        xnTm = AR.alloc([8, 32], BF16)
        MIX_OFF = ARENA_BYTES - 8 * NOWN * 2

        wg = AR.alloc([8, 16], BF16)
        wq = AR.alloc([4, 2, 128], BF16)
        wk = AR.alloc([4, 2, 128], BF16)
        S.dma('pool', wg, d_wg.rearrange("p (a b) -> p a b", a=8))
        S.dma('pool', wq, d_wq.rearrange("p (a b c) -> p a b c", a=4, b=2))
        S.dma('pool', wk, d_wk.rearrange("p (a b c) -> p a b c", a=4, b=2))
        AR.push()
        xt_buf = [AR.alloc([D], F32) for _ in range(4)]
        xnb_buf = [AR.alloc([D], BF16) for _ in range(4)]
        xt_buf2 = [AR.alloc([D], F32) for _ in range(4)]
        xnb_buf2 = [AR.alloc([D], BF16) for _ in range(4)]
        ss_buf2 = AR.alloc([8], F32)
        sq_junk = AR.alloc([D], BF16)
        ss_buf = AR.alloc([8], F32)
        xcnt = [0]

        def emit_xnT_batch(items, gvec, xts=None, xnbs=None, ssb=None):
            nb = len(items)
            assert nb <= 4
            xts = xts or xt_buf
            xnbs = xnbs or xnb_buf
            ssb = ssb if ssb is not None else ss_buf
            nmax = max(n for _, n, _ in items)
            for i, (row0, n, dst) in enumerate(items):
                S.dma('sp', xts[i][0:n, :], xe[row0:row0 + n, :])
            for i, (row0, n, dst) in enumerate(items):
                S.act(sq_junk[0:n, :], xts[i][0:n, :], AF.Square, accum=ssb[0:n, i:i + 1])
            rs = ssb[0:nmax, 4:4 + nb]
            S.ts('dve', rs, ssb[0:nmax, 0:nb], 1.0 / D, EPS, ALU.mult, ALU.add)
            S.act(rs, rs, AF.Sqrt)
            S.recip(rs, rs)
            for i, (row0, n, dst) in enumerate(items):
                S.ts('dve', xnbs[i][0:n, :], xts[i][0:n, :], ssb[0:n, 4 + i:5 + i], None, ALU.mult)
            for i, (row0, n, dst) in enumerate(items):
                pb = psb[6 + (xcnt[0] % 2)]
                xcnt[0] += 1
                for dt in range(8):
                    S.transpose(pb[:, dt * 128:dt * 128 + n], xnbs[i][0:n, dt * 128:(dt + 1) * 128],
                                ident[0:n, 0:n])
                pv = pb[:, :].rearrange("p (a b) -> p a b", a=8)[:, :, 0:n]
                S.tt('dve', dst, pv, gvec.unsqueeze(2).to_broadcast([128, 8, n]), ALU.mult)

        xt_h = AR.alloc([D], F32)
        xnb_h = AR.alloc([D], BF16)
        ss_m = AR.alloc([8], F32)

        def emit_xnT(row0, n, dst, gvec):
            emit_xnT_batch([(row0, n, dst)], gvec, xts=[xt_h], xnbs=[xnb_h], ssb=ss_m)

        def emit_h1_xn2T(tile_rows, h1, dst):
            pass

        if stop_after == 'consts':
            dbg("vec", vec, [128, 64])
            return finish(nc, S, y_out, dbg_outs, None)
        if stop_after == 'x16':
            emit_xnT(1, 16, xnT[:, :, 1:17], g1)
            dbg("xnT", xnT[:, 0, 0:NXC], [128, NXC])
            return finish(nc, S, y_out, dbg_outs, None)
        if stop_after and stop_after.startswith('xn'):
            for k in range(int(stop_after[2:])):
                emit_xnT(OWN0 + 128 * k, 128, xnT[:, :, OWN0 + 128 * k:OWN0 + 128 * k + 128], g1)
            dbg("xnT", xnT[:, 0, 0:NXC], [128, NXC])
            return finish(nc, S, y_out, dbg_outs, None)
        if stop_after == 'x1':
            emit_xnT(OWN0, 128, xnT[:, :, OWN0:OWN0 + 128], g1)
            dbg("xnT", xnT[:, 0, 0:NXC], [128, NXC])
            return finish(nc, S, y_out, dbg_outs, None)
        emit_xnT_batch([(1, 16, xnT[:, :, 1:17]), (1, 16, xnTm[:, :, 0:16]), (POST0, 16, xnTm[:, :, 16:32])], g1)
        def p0gen():
            for k0 in range(0, NT + 2, 4):
                alt = (k0 // 4) % 2 == 1
                emit_xnT_batch([(OWN0 + 128 * k, 128, xnT[:, :, OWN0 + 128 * k:OWN0 + 128 * k + 128])
                                for k in range(k0, min(k0 + 4, NT + 2))], g1,
                               xts=xt_buf2 if alt else None, xnbs=xnb_buf2 if alt else None,
                               ssb=ss_buf2 if alt else None)
                yield

        AR.push()
        wxm = AR.alloc([8, 1024], BF16)
        for q4 in range(4):
            S.dma('pool', wxm[:, 2 * q4:2 * q4 + 2, :],
                  d_wxm.rearrange("p (a b) -> p a b", a=8)[:, 2 * q4:2 * q4 + 2, :])
        p1bufs = []
        for _ in range(2):
            p1bufs.append(dict(
                xoT=AR.alloc([8, 514], BF16), xmT=AR.alloc([8, 514], BF16), xcT=AR.alloc([8, 512], BF16),
                ktok1=AR.alloc([4, 512], BF16), vp1=AR.alloc([4, 4, 257], BF16),
                gsb=AR.alloc([4, 16], F32), sp1=AR.alloc([4, 4], F32), w1=AR.alloc([4, 4], F32),
                d1=AR.alloc([2, 4, 4], F32), tmp16=AR.alloc([4, 4], F32)))
        p1cnt = [0]
        carry = [None]

        def seq_block(direction, row0, ntile, npp, ncol, is_mini, vflag, first):
            ntok = ntile * npp
            B_ = p1bufs[p1cnt[0] % 2]
            p1cnt[0] += 1
            xoT, xmT, xcT, ktok1, vp1 = B_['xoT'], B_['xmT'], B_['xcT'], B_['ktok1'], B_['vp1']
            gsb, sp1, w1, d1, tmp16 = B_['gsb'], B_['sp1'], B_['w1'], B_['d1'], B_['tmp16']
            if direction == 'A':
                U, dprev, tri, li0, f0 = U_A, dprevA, triA_f, 0, 4
            else:
                U, dprev, tri, li0, f0 = U_B, dprevB, triB_f, 8, 12
            if is_mini:
                emit_xnT(row0, ncol, xoT[:, :, 0:ncol], g1)
            else:
                emit_xnT_batch([(row0 + 1 + 128 * j, 128, xoT[:, :, 1 + 128 * j:1 + 128 * j + 128])
                                for j in range(ntile)], g1)
                emit_xnT(row0, 1, xoT[:, :, 0:1], g1)
                S.copy('dve', xmT[:, :, 512:514], carry[0])
            carry[0] = xmT[:, :, 0:2]
            nmain = min(512, ncol)
            yield
            pg = ps[2]
            for j in range(ntile):
                for dt in range(8):
                    S.mm(pg[0:npp, 64 + 16 * j:64 + 16 * j + 16], xoT[:, dt, 1 + j * npp:1 + (j + 1) * npp],
                         wg[:, dt, :], start=(dt == 0), stop=(dt == 7))
            pgv = pg[0:npp, 64:64 + 16 * ntile].rearrange("p (a b) -> p a b", a=ntile)
            S.tt('dve', gsb[0:npp, 0:ntile, :], pgv, gateb[0:npp, :].unsqueeze(1).to_broadcast([npp, ntile, 16]),
                 ALU.add)
            spv = sp1[0:npp, 0:ntile, :]
            S.act(spv, gsb[0:npp, 0:ntile, f0:f0 + 4], AF.Exp, scale=-1.0)
            S.act(spv, spv, AF.Ln, bias=1.0, scale=1.0)
            pc = ps[3]
            S.mm(pc[0:npp, 128:128 + 4 * ntile], tri[0:npp, 0:npp], spv)
            nhb = 1 if is_mini else 2
            for hb in range(nhb):
                lh = ones_f[hb][0:npp, :] if not is_mini else ones_f[0][0:npp, :]
                S.mm(pc[:, 192 + 16 * hb:192 + 16 * hb + 4 * ntile], lh, spv)
            csv = pc[0:npp, 128:128 + 4 * ntile].rearrange("p (a b) -> p a b", a=ntile)
            S.tt('dve', tmp16[0:npp, 0:ntile, :], gsb[0:npp, 0:ntile, li0:li0 + 4], csv, ALU.add)
            S.act(w1[0:npp, 0:ntile, :], tmp16[0:npp, 0:ntile, :], AF.Exp)
            if vflag is not None:
                S.ts('dve', w1[0:npp, 0:ntile, :], w1[0:npp, 0:ntile, :], vflag[0:npp, :], None, ALU.mult)
            for hb in range(nhb):
                S.act(d1[:, hb, 0:ntile, :],
                      pc[:, 192 + 16 * hb:192 + 16 * hb + 4 * ntile].rearrange("p (a b) -> p a b", a=ntile),
                      AF.Exp, scale=-1.0)
            yield
            for et in range(8):
                pm = ps[et % 4]
                for dt in range(8):
                    S.mm(pm[:, 0:nmain], wxm[:, dt, et * 128:(et + 1) * 128], xoT[:, dt, 0:nmain],
                         start=(dt == 0), stop=(dt == 7))
                S.copy('act', xmT[:, et, 0:nmain], pm[:, 0:nmain])
            for j in range(ntile):
                for half in range(2):
                    pv_ = ps[4 + half]
                    for dt in range(8):
                        S.mm(pv_[0:npp, :], xoT[:, dt, 1 + j * npp:1 + (j + 1) * npp],
                             wxm[:, dt, 512 * half:512 * half + 512], start=(dt == 0), stop=(dt == 7))
                    S.tt('dve', vp1[0:npp, j, 2 * half:2 * half + 2, 0:256],
                         pv_[0:npp, :].rearrange("p (a b) -> p a b", a=2),
                         w1[0:npp, j, 2 * half:2 * half + 2].unsqueeze(2).to_broadcast([npp, 2, 256]), ALU.mult)
            S.copy('dve', vp1[0:npp, 0:ntile, :, 256], w1[0:npp, 0:ntile, :])
            yield
            for et in range(8):
                pcv = ps[6 + et % 2]
                for k3 in range(3):
                    S.mm(pcv[:, 0:ntok], dcw[:, k3, et, :], xmT[:, et, k3:k3 + ntok], start=(k3 == 0), stop=(k3 == 2))
                S.act(xcT[:, et, 0:ntok], pcv[:, 0:ntok], AF.Silu, bias=cb[:, et:et + 1], scale=1.0)
            yield
            for j in range(ntile):
                pk = ps[3]
                for h in range(4):
                    for e2 in range(2):
                        S.mm(pk[0:npp, h * 128:(h + 1) * 128], xcT[:, 2 * h + e2, j * npp:(j + 1) * npp],
                             wk[:, h, e2, :], start=(e2 == 0), stop=(e2 == 1))
                S.act(ktok1[0:npp, j, :], pk[0:npp, :], AF.Copy, scale=DK ** -0.5)
            yield
            order = []
            for j in range(ntile):
                for hb in range(nhb):
                    order.append((j, hb))
            if direction == 'B':
                order = order[::-1]
            for (j, hb) in order:
                p0 = 64 * hb
                cn = npp if is_mini else 64
                for h in range(4):
                    pst = ps[h]
                    S.mm(pst[:, 0:257], ktok1[p0:p0 + cn, j, h * 128:(h + 1) * 128], vp1[p0:p0 + cn, j, h, :])
                    if first:
                        S.copy('dve', U[:, h, :], pst[:, 0:257])
                    else:
                        S.stt(U[:, h, :], U[:, h, :], dprev[:, h:h + 1], pst[:, 0:257], ALU.mult, ALU.add)
                S.copy('dve', dprev, d1[:, hb, j, :])
                first = False

        gens = [seq_block('A', 0, 1, 16, 18, True, valid[:, 0:1], True),
                seq_block('B', POST0 - 1, 1, 16, 18, True, valid[:, 1:2], True), p0gen()]
        while gens:
            for g_ in list(gens):
                try:
                    next(g_)
                except StopIteration:
                    gens.remove(g_)
        dbg("xnT", xnT[:, 0, 0:NXC], [128, NXC])
        if stop_after == 'phase0':
            return finish(nc, S, y_out, dbg_outs, None)

        def big_bufs(bk):
            return p1bufs[(bk + 1) % 2]

        def X_pre(bk):
            row0 = OTH0 + 512 * bk - 1
            for j in range(4):
                S.dma('sp', xt_buf[j], xe[row0 + 1 + 128 * j:row0 + 1 + 128 * j + 128, :])
            S.dma('sp', xt_h[0:1, :], xe[row0:row0 + 1, :])
            for j in range(4):
                S.act(sq_junk, xt_buf[j], AF.Square, accum=ss_buf[:, j:j + 1])
            S.act(sq_junk[0:1, :], xt_h[0:1, :], AF.Square, accum=ss_h[0:1, 0:1])
            rs = ss_buf[:, 4:8]
            S.ts('dve', rs, ss_buf[:, 0:4], 1.0 / D, EPS, ALU.mult, ALU.add)
            S.act(rs, rs, AF.Sqrt)
            S.recip(rs, rs)
            rh = ss_h[0:1, 1:2]
            S.ts('dve', rh, ss_h[0:1, 0:1], 1.0 / D, EPS, ALU.mult, ALU.add)
            S.act(rh, rh, AF.Sqrt)
            S.recip(rh, rh)
            for j in range(4):
                S.ts('dve', xnb_buf[j], xt_buf[j], ss_buf[:, 4 + j:5 + j], None, ALU.mult)
            S.ts('dve', xnb_h[0:1, :], xt_h[0:1, :], rh, None, ALU.mult)

        def X_post(bk):
            B_ = big_bufs(bk)
            xoT, xmT = B_['xoT'], B_['xmT']
            for j in range(4):
                pb = psb[6 + (j % 2)]
                for dt in range(8):
                    S.transpose(pb[:, dt * 128:dt * 128 + 128], xnb_buf[j][:, dt * 128:(dt + 1) * 128], ident)
                S.tt('dve', xoT[:, :, 1 + 128 * j:1 + 128 * j + 128], pb[:, :].rearrange("p (a b) -> p a b", a=8),
                     g1.unsqueeze(2).to_broadcast([128, 8, 128]), ALU.mult)
            pb = psb[6]
            for dt in range(8):
                S.transpose(pb[:, dt * 128:dt * 128 + 1], xnb_h[0:1, dt * 128:(dt + 1) * 128], ident[0:1, 0:1])
            S.tt('dve', xoT[:, :, 0:1], pb[:, :].rearrange("p (a b) -> p a b", a=8)[:, :, 0:1],
                 g1.unsqueeze(2).to_broadcast([128, 8, 1]), ALU.mult)
            S.copy('dve', xmT[:, :, 512:514], carry[0])
            carry[0] = xmT[:, :, 0:2]

        def G_(bk):
            B_ = big_bufs(bk)
            xoT, gsb, sp1, w1, d1, tmp16 = B_['xoT'], B_['gsb'], B_['sp1'], B_['w1'], B_['d1'], B_['tmp16']
            pg = ps[0]
            for j in range(4):
                for dt in range(8):
                    S.mm(pg[:, 16 * j:16 * j + 16], xoT[:, dt, 1 + j * 128:1 + (j + 1) * 128], wg[:, dt, :],
                         start=(dt == 0), stop=(dt == 7))
            S.tt('dve', gsb, pg[:, 0:64].rearrange("p (a b) -> p a b", a=4),
                 gateb.unsqueeze(1).to_broadcast([128, 4, 16]), ALU.add)
            S.act(sp1, gsb[:, :, 12:16], AF.Exp, scale=-1.0)
            S.act(sp1, sp1, AF.Ln, bias=1.0, scale=1.0)
            pc = ps[1]
            S.mm(pc[:, 0:16], triB_f, sp1)
            for hb in range(2):
                S.mm(pc[:, 64 + 16 * hb:64 + 16 * hb + 16], ones_f[hb], sp1)
            S.tt('dve', tmp16, gsb[:, :, 8:12], pc[:, 0:16].rearrange("p (a b) -> p a b", a=4), ALU.add)
            S.act(w1, tmp16, AF.Exp)
            for hb in range(2):
                S.act(d1[:, hb, :, :], pc[:, 64 + 16 * hb:64 + 16 * hb + 16].rearrange("p (a b) -> p a b", a=4),
                      AF.Exp, scale=-1.0)

        def M_et(bk, et):
            B_ = big_bufs(bk)
            pm = ps[et % 4]
            for dt in range(8):
                S.mm(pm[:, :], wxm[:, dt, et * 128:(et + 1) * 128], B_['xoT'][:, dt, 0:512],
                     start=(dt == 0), stop=(dt == 7))
            S.copy('act', B_['xmT'][:, et, 0:512], pm[:, :])

        def V_(bk):
            B_ = big_bufs(bk)
            xmT, vp1, w1 = B_['xmT'], B_['vp1'], B_['w1']
            for j in range(4):
                pb = psb[4 + j % 2]
                for et in range(8):
                    S.transpose(pb[:, et * 128:(et + 1) * 128], xmT[:, et, 1 + 128 * j:1 + 128 * j + 128], ident)
                S.tt('dve', vp1[:, j, :, 0:256], pb[:, :].rearrange("p (a b) -> p a b", a=4),
                     w1[:, j, :].unsqueeze(2).to_broadcast([128, 4, 256]), ALU.mult)
            S.copy('dve', vp1[:, :, :, 256], w1)

        def C_(bk):
            B_ = big_bufs(bk)
            for et in range(8):
                pcv = ps[6 + et % 2]
                for k3 in range(3):
                    S.mm(pcv[:, :], dcw[:, k3, et, :], B_['xmT'][:, et, k3:k3 + 512], start=(k3 == 0), stop=(k3 == 2))
                S.act(B_['xcT'][:, et, :], pcv[:, :], AF.Silu, bias=cb[:, et:et + 1], scale=1.0)

        def K_(bk):
            B_ = big_bufs(bk)
            for j in range(4):
                pk = ps[4 + j % 2]
                for h in range(4):
                    for e2 in range(2):
                        S.mm(pk[:, h * 128:(h + 1) * 128], B_['xcT'][:, 2 * h + e2, j * 128:(j + 1) * 128],
                             wk[:, h, e2, :], start=(e2 == 0), stop=(e2 == 1))
                S.act(B_['ktok1'][:, j, :], pk[:, :], AF.Copy, scale=DK ** -0.5)

        def S_step(bk, i):
            B_ = big_bufs(bk)
            j, hb = [(jj, hh) for jj in range(4) for hh in range(2)][::-1][i]
            p0 = 64 * hb
            for h in range(4):
                pst = ps[4 + h]
                S.mm(pst[:, 0:257], B_['ktok1'][p0:p0 + 64, j, h * 128:(h + 1) * 128], B_['vp1'][p0:p0 + 64, j, h, :])
                S.stt(U_B[:, h, :], U_B[:, h, :], dprevB[:, h:h + 1], pst[:, 0:257], ALU.mult, ALU.add)
            S.copy('dve', dprevB, B_['d1'][:, hb, j, :])

        ss_h = AR.alloc([2], F32)
        bks = [3, 2, 1, 0]
        X_pre(3)
        X_post(3)
        G_(3)
        for bi, bk in enumerate(bks):
            prev = bks[bi - 1] if bi > 0 else None
            nxt = bks[bi + 1] if bi + 1 < 4 else None
            for et in range(8):
                M_et(bk, et)
                if prev is not None:
                    S_step(prev, et)
            V_(bk)
            if nxt is not None:
                X_pre(nxt)
            C_(bk)
            if nxt is not None:
                X_post(nxt)
            K_(bk)
            if nxt is not None:
                G_(nxt)
        for i in range(8):
            S_step(0, i)
        dbg("U_A", U_A[:, 0, :], [128, 257])
        dbg("U_B", U_B[:, 0, :], [128, 257])
        dbg("dprevA", dprevA, [128, 4])
        dbg("dprevB", dprevB, [128, 4])
        AR.pop()
        AR.pop()
        yaT = AR.alloc([8, NOWN], BF16)

        if stop_after == 'phase1':
            return finish(nc, S, y_out, dbg_outs, None)

        AR.push()
        gso = AR.alloc([NT, 16], F32)
        spo = AR.alloc([2, NT, 4], F32)
        wgt = AR.alloc([2, NT, 4], F32)
        einv = AR.alloc([2, NT, 4], F32)
        dch = AR.alloc([2, 2, NT, 4], F32)
        tmpg = AR.alloc([NT, 4], F32)
        pg = ps[2]
        for k in range(NT):
            for dt in range(8):
                S.mm(pg[:, 16 * k:16 * k + 16], xnT[:, dt, OWN0 + 128 * k:OWN0 + 128 * k + 128], wg[:, dt, :],
                     start=(dt == 0), stop=(dt == 7))
        S.tt('dve', gso, pg[:, 0:256].rearrange("p (a b) -> p a b", a=NT),
             gateb.unsqueeze(1).to_broadcast([128, NT, 16]), ALU.add)
        for di in range(2):
            f0 = 4 + 8 * di
            li0 = 8 * di
            S.act(spo[:, di, :, :], gso[:, :, f0:f0 + 4], AF.Exp, scale=-1.0)
            S.act(spo[:, di, :, :], spo[:, di, :, :], AF.Ln, bias=1.0, scale=1.0)
            pc = ps[3]
            S.mm(pc[:, 0:64], triA_f if di == 0 else triB_f, spo[:, di, :, :])
            for hb in range(2):
                S.mm(pc[:, 64 + 64 * hb:128 + 64 * hb], ones_f[hb], spo[:, di, :, :])
            csv = pc[:, 0:64].rearrange("p (a b) -> p a b", a=NT)
            S.tt('dve', tmpg, gso[:, :, li0:li0 + 4], csv, ALU.add)
            S.act(wgt[:, di, :, :], tmpg, AF.Exp)
            S.act(einv[:, di, :, :], csv, AF.Exp)
            for hb in range(2):
                S.act(dch[:, di, hb, :, :], pc[:, 64 + 64 * hb:128 + 64 * hb].rearrange("p (a b) -> p a b", a=NT),
                      AF.Exp, scale=-1.0)
        dbg("wgtA", wgt[:, 0, :, :], [128, NT, 4])
        dbg("dchA", dch[:, 0, :, :, :], [128, 2, NT, 4])

        wxmh = AR.alloc([8, 256], BF16)
        wob = [AR.alloc([8, 128], BF16) for _ in range(2)]
        xmt = [[AR.alloc([514], BF16) for _ in range(2)] for _ in range(2)]
        xcThs = [AR.alloc([2, NOWN], BF16) for _ in range(2)]
        vpA = AR.alloc([NT, 257], BF16)
        vpB = AR.alloc([NT, 257], BF16)
        qT = AR.alloc([NOWN], BF16)
        kT = AR.alloc([NOWN], BF16)
        ktok = AR.alloc([NT, 128], BF16)
        P_A = AR.alloc([NT, 128], BF16)
        P_B = AR.alloc([NT, 128], BF16)
        hrA = AR.alloc([NT, 257], BF16)
        hrB = AR.alloc([NT, 257], BF16)
        Cbf = [[AR.alloc([257], BF16) for _ in range(2)] for _ in range(2)]
        rr = AR.alloc([2, NT], F32)
        t1 = [AR.alloc([256], F32) for _ in range(2)]
        hn = [AR.alloc([256], BF16) for _ in range(4)]
        sqj = AR.alloc([256], BF16)
        st8 = AR.alloc([NT, 2], F32)
        sohs = [AR.alloc([2, NOWN], BF16) for _ in range(2)]
        wocnt = [0]

        def head_pre(h):
            xcTh = xcThs[h % 2]
            soh = sohs[h % 2]
            S.dma('pool', wxmh, d_wxm.rearrange("p (a b) -> p a b", a=8)[:, :, 256 * h:256 * h + 256])
            for tb in (3, 2, 1, 0):
                c0 = OWN0 + 512 * tb
                for e2 in range(2):
                    pm = ps[e2]
                    xb = xmt[e2][tb % 2]
                    for dt in range(8):
                        S.mm(pm[:, :], wxmh[:, dt, 128 * e2:128 * e2 + 128], xnT[:, dt, c0 - 1:c0 + 511],
                             start=(dt == 0), stop=(dt == 7))
                    if tb == 3:
                        pa = ps[2]
                        for dt in range(8):
                            S.mm(pa[:, 2 * e2:2 * e2 + 2], wxmh[:, dt, 128 * e2:128 * e2 + 128],
                                 xnT[:, dt, c0 + 511:c0 + 513], start=(dt == 0), stop=(dt == 7))
                        S.copy('act', xb[:, 512:514], pa[:, 2 * e2:2 * e2 + 2])
                    else:
                        S.copy('dve', xb[:, 512:514], xmt[e2][(tb + 1) % 2][:, 0:2])
                    S.copy('act', xb[:, 0:512], pm[:, :])
                pvb = psb[4 + (tb % 2)]
                for kk in range(4):
                    for e2 in range(2):
                        S.transpose(pvb[:, 256 * kk + 128 * e2:256 * kk + 128 * e2 + 128],
                                    xmt[e2][tb % 2][:, 1 + 128 * kk:1 + 128 * kk + 128], ident)
                for kk in range(4):
                    k = 4 * tb + kk
                    S.act(vpA[:, k, 0:256], pvb[:, 256 * kk:256 * kk + 256], AF.Copy, scale=wgt[:, 0, k, h:h + 1])
                    S.ts('dve', vpB[:, k, 0:256], pvb[:, 256 * kk:256 * kk + 256], wgt[:, 1, k, h:h + 1], None, ALU.mult)
                for e2 in range(2):
                    xb = xmt[e2][tb % 2]
                    et = 2 * h + e2
                    pcv = ps[2 + e2]
                    for k3 in range(3):
                        S.mm(pcv[:, :], dcw[:, k3, et, :], xb[:, k3:k3 + 512], start=(k3 == 0), stop=(k3 == 2))
                    S.act(xcTh[:, e2, 512 * tb:512 * tb + 512], pcv[:, :], AF.Silu, bias=cb[:, et:et + 1], scale=1.0)
                yield
            S.copy('dve', vpA[:, :, 256], wgt[:, 0, :, h])
            S.copy('dve', vpB[:, :, 256], wgt[:, 1, :, h])
            for e2 in range(2):
                S.dma('pool', wob[e2], d_wo[2 * h + e2].rearrange("p (a b) -> p a b", a=8))

            def emit_o(e2, k4):
                po = ps[4 + (k4 % 2)]
                for dt in range(8):
                    S.mm(po[:, :], wob[e2][:, dt, :], xnT[:, dt, OWN0 + 512 * k4:OWN0 + 512 * k4 + 512],
                         start=(dt == 0), stop=(dt == 7))
                S.act(soh[:, e2, 512 * k4:512 * k4 + 512], po[:, :], AF.Sigmoid)

            for tb in range(4):
                pq = ps[2 * (tb % 2)]
                pk = ps[2 * (tb % 2) + 1]
                for e2 in range(2):
                    S.mm(pq[:, :], wq[:, h, e2, :], xcTh[:, e2, 512 * tb:512 * tb + 512], start=(e2 == 0),
                         stop=(e2 == 1))
                for e2 in range(2):
                    S.mm(pk[:, :], wk[:, h, e2, :], xcTh[:, e2, 512 * tb:512 * tb + 512], start=(e2 == 0),
                         stop=(e2 == 1))
                S.copy('dve', qT[:, 512 * tb:512 * tb + 512], pq[:, :])
                S.ts('dve', kT[:, 512 * tb:512 * tb + 512], pk[:, :], DK ** -0.5, None, ALU.mult)
                emit_o(0, tb)
                yield
            for k4 in range(4):
                pk = ps[6 + (k4 % 2)]
                for kk in range(4):
                    k = 4 * k4 + kk
                    for e2 in range(2):
                        S.mm(pk[:, 128 * kk:128 * kk + 128], xcTh[:, e2, 128 * k:128 * k + 128], wk[:, h, e2, :],
                             start=(e2 == 0), stop=(e2 == 1))
                S.ts('dve', ktok[:, 4 * k4:4 * k4 + 4, :], pk[:, :].rearrange("p (a b) -> p a b", a=4), DK ** -0.5,
                     None, ALU.mult)
                emit_o(1, k4)
                yield
            for k4 in range(4):
                psc = ps[k4 % 2]
                for kk in range(4):
                    k = 4 * k4 + kk
                    S.mm(psc[:, 128 * kk:128 * kk + 128], kT[:, 128 * k:128 * k + 128], qT[:, 128 * k:128 * k + 128])
                pv3 = psc[:, :].rearrange("p (a b) -> p a b", a=4)
                S.tt('dve', P_A[:, 4 * k4:4 * k4 + 4, :], pv3, maskA.unsqueeze(1).to_broadcast([128, 4, 128]), ALU.mult)
                S.tt('dve', P_B[:, 4 * k4:4 * k4 + 4, :], pv3, maskB.unsqueeze(1).to_broadcast([128, 4, 128]), ALU.mult)
                yield
            if h == 0:
                dbg("xcTh", xcTh[:, 0, :], [128, NOWN])
                dbg("qT", qT, [128, NOWN])
                dbg("vpA", vpA, [128, NT, 257])
                dbg("P_A", P_A, [128, NT, 128])
                dbg("ktok", ktok, [128, NT, 128])
            for e2 in range(2):
                et = 2 * h + e2
                S.ts('dve', xcTh[:, e2, :], xcTh[:, e2, :], skp[:, et:et + 1], None, ALU.mult)

        def head_chain(h):
            S.act(Cbf[0][0], U_A[:, h, :], AF.Copy, scale=dprevA[:, h:h + 1])
            S.act(Cbf[1][0], U_B[:, h, :], AF.Copy, scale=dprevB[:, h:h + 1])

            def chunk_of(step, di):
                c = step if di == 0 else 31 - step
                return c // 2, c % 2

            def emit_st(step):
                for di in range(2):
                    k, hb = chunk_of(step, di)
                    p0 = 64 * hb
                    vp = vpA if di == 0 else vpB
                    S.mm(ps[2 * di + step % 2][:, 0:257], ktok[p0:p0 + 64, k, :], vp[p0:p0 + 64, k, :])

            emit_st(0)
            for step in range(32):
                if step + 1 < 32:
                    emit_st(step + 1)
                for di in range(2):
                    k, hb = chunk_of(step, di)
                    p0 = 64 * hb
                    vp = vpA if di == 0 else vpB
                    Pm = P_A if di == 0 else P_B
                    pout = ps[4 + 2 * di + step % 2]
                    S.mm(pout[:, 0:257], Pm[p0:p0 + 64, k, :], vp[p0:p0 + 64, k, :], start=True, stop=False)
                    S.mm(pout[:, 0:257], qT[:, 128 * k:128 * k + 128], Cbf[di][step % 2], start=False, stop=True)
                for di in range(2):
                    k, hb = chunk_of(step, di)
                    p0 = 64 * hb
                    U = U_A if di == 0 else U_B
                    hr = hrA if di == 0 else hrB
                    if step == 0:
                        dpv = (dprevA if di == 0 else dprevB)[:, h:h + 1]
                    else:
                        kp, hbp = chunk_of(step - 1, di)
                        dpv = dch[:, di, hbp, kp, h:h + 1]
                    S.stt(U[:, h, :], U[:, h, :], dpv, ps[2 * di + step % 2][:, 0:257], ALU.mult, ALU.add)
                    if step + 1 < 32:
                        S.act(Cbf[di][(step + 1) % 2], U[:, h, :], AF.Copy, scale=dch[:, di, hb, k, h:h + 1])
                lag = [step - 1] if step >= 1 else []
                if step == 31:
                    lag.append(31)
                for s2 in lag:
                    for di in range(2):
                        k2, hb2 = chunk_of(s2, di)
                        hr2 = hrA if di == 0 else hrB
                        S.copy('act' if di == 0 else 'dve', hr2[64 * hb2:64 * hb2 + 64, k2, :],
                               ps[4 + 2 * di + s2 % 2][64 * hb2:64 * hb2 + 64, 0:257])
            if h == 0:
                dbg("hrA", hrA, [128, NT, 257])
                dbg("hrB", hrB, [128, NT, 257])

        def head_post(h):
            uuh = xcThs[h % 2]
            soh = sohs[h % 2]
            for di in range(2):
                hr = hrA if di == 0 else hrB
                S.act(rr[:, di, :], hr[:, :, 256], AF.Abs)
                S.tt('dve', rr[:, di, :], rr[:, di, :], einv[:, di, :, h], ALU.max)
                S.recip(rr[:, di, :], rr[:, di, :])
            for k in range(NT):
                b = k % 2
                S.ts('dve', t1[b], hrA[:, k, 0:256], rr[:, 0, k:k + 1], None, ALU.mult)
                S.stt(hrA[:, k, 0:256], hrB[:, k, 0:256], rr[:, 1, k:k + 1], t1[b], ALU.mult, ALU.add)
                S.act(sqj, hrA[:, k, 0:256], AF.Square, accum=st8[:, k, 0:1])
                if k % 2 == 1:
                    yield
            S.ts('dve', st8[:, :, 1], st8[:, :, 0], 1.0 / DV, EPS, ALU.mult, ALU.add)
            S.act(st8[:, :, 1], st8[:, :, 1], AF.Sqrt)
            S.recip(st8[:, :, 1], st8[:, :, 1])
            for k4 in range(4):
                pT = psb[(k4 % 2)]
                for kk in range(4):
                    k = 4 * k4 + kk
                    b = k % 4
                    S.act(hn[b], hrA[:, k, 0:256], AF.Copy, scale=st8[:, k, 1:2])
                    for e2 in range(2):
                        S.transpose(pT[:, 512 * e2 + 128 * kk:512 * e2 + 128 * kk + 128],
                                    hn[b][:, 128 * e2:128 * e2 + 128], ident)
                for e2 in range(2):
                    et = 2 * h + e2
                    w = (2 * k4 + e2) % 2
                    yv = yaT[:, et, 512 * k4:512 * k4 + 512]
                    S.stt(yv, pT[:, 512 * e2:512 * e2 + 512], mng[:, et:et + 1], uuh[:, e2, 512 * k4:512 * k4 + 512],
                          ALU.mult, ALU.add)
                    S.tt('dve', yv, yv, soh[:, e2, 512 * k4:512 * k4 + 512], ALU.mult)
                yield

        def drive(gens):
            gens = list(gens)
            while gens:
                for g_ in list(gens):
                    try:
                        next(g_)
                    except StopIteration:
                        gens.remove(g_)

        drive([head_pre(0)])
        for h in range(H_M):
            head_chain(h)
            drive([head_post(h)] + ([head_pre(h + 1)] if h + 1 < H_M else []))
        dbg("yaT", yaT, [128, 8, NOWN])
        AR.pop()
        if stop_after == 'mstage':
            return finish(nc, S, y_out, dbg_outs, None)
        ybT = AR.alloc([4, NOWN], BF16)

        AR.push()
        vext2 = AR.alloc([19, 8, 128], BF16)
        AR.push()
        wnv = AR.alloc([8, 512], BF16)
        S.dma('pool', wnv, d_wnv.rearrange("p (a b) -> p a b", a=8))
        AR.pop()
        wqp = AR.alloc([8, 128], BF16)
        wkp = AR.alloc([8, 128], BF16)
        btp = AR.alloc([2, 13, 128], BF16)
        qnT = AR.alloc([NOWN], BF16)
        knT = AR.alloc([2304 + 32], BF16)
        sqT = [AR.alloc([512], BF16) for _ in range(4)]
        rsT4 = AR.alloc([4, 512], F32)
        rsT = [rsT4[:, i, :] for i in range(4)]
        PT = [AR.alloc([768], BF16) for _ in range(2)]
        PTm = AR.alloc([NOWN], BF16)
        lnb = [AR.alloc([512], F32) for _ in range(2)]
        recq = [AR.alloc([512], F32) for _ in range(2)]
        outsb = [rsT4, rsT4]
        zer = AR.alloc([128], BF16)
        qz = [AR.alloc([NOWN], BF16) for _ in range(2)]
        epsq = AR.alloc([2], F32)
        S.memset('dve', zer, 0.0)
        S.memset('dve', epsq[:, 0:1], DH * EPS)
        S.memset('dve', epsq[:, 1:2], EPS)
        vv = vext2[:, :, :, :].rearrange("p j (a b) c -> p j a b c", b=2)
        S.memset('dve', vv[:, :, :, 0, 64:128], 1.0)
        S.memset('dve', vv[:, :, :, 1, 0:64], 1.0)
        for j in range(19):
            npk = 128 if j < 18 else 32
            pv_ = ps[j % 2]
            for dt in range(8):
                lh = xnT[:, dt, OWN0 + 128 * j:OWN0 + 128 * j + 128] if j < 18 else xnTm[:, dt, :]
                S.mm(pv_[0:npk, :], lh, wnv[:, dt, :], start=(dt == 0), stop=(dt == 7))
            src = pv_[0:npk, :].rearrange("p (a b c) -> p a b c", a=4, b=2)
            dst = vext2[0:npk, j, :, :].rearrange("p (a b) c -> p a b c", b=2)
            S.copy('act', dst[:, :, 0, 0:64], src[:, :, 0, :])
            S.copy('dve', dst[:, :, 1, 64:128], src[:, :, 1, :])

        def qk_norm(dst, wts, col0, ncols, which, cnt):
            b = cnt % 4
            pq = ps[b]
            pss = ps[4 + b]
            for dt in range(8):
                S.mm(pq[:, 0:ncols], wts[:, dt, :], col0[dt], start=(dt == 0), stop=(dt == 7))
            S.act(sqT[b][:, 0:ncols], pq[:, 0:ncols], AF.Square)
            S.mm(pss[:, 0:ncols], blk64, sqT[b][:, 0:ncols])
            if which == 0:
                S.act(rsT[b][:, 0:ncols], pss[:, 0:ncols], AF.Ln, bias=epsq[:, 0:1], scale=1.0)
            else:
                S.act(rsT[b][:, 0:ncols], pss[:, 0:ncols], AF.Ln, bias=epsq[:, 1:2], scale=1.0 / DH)
            S.act(rsT[b][:, 0:ncols], rsT[b][:, 0:ncols], AF.Exp, scale=-0.5)
            S.stt(dst, pq[:, 0:ncols], qkg[:, which:which + 1], rsT[b][:, 0:ncols], ALU.mult, ALU.mult)

        def bias_pos(i, j):
            if i == 0:
                return j
            if i == 1:
                return 4 + j
            return 10 - (j - i)

        Iof = {j: [i for i in range(NT) if j in [jj for jj, _ in key_tiles(i)]] for j in range(18)}
        ncnt = 0
        sc = 0
        for pr in range(4):
            S.dma('pool', wqp, d_wnq[pr].rearrange("p (a b) -> p a b", a=8))
            S.dma('pool', wkp, d_wnk[pr].rearrange("p (a b) -> p a b", a=8))
            S.dma('pool', btp, d_bt[pr].rearrange("p (a b c) -> p a b c", a=2, b=13))
            for tb in range(4):
                c0 = OWN0 + 512 * tb
                qk_norm(qnT[:, 512 * tb:512 * tb + 512], wqp, [xnT[:, dt, c0:c0 + 512] for dt in range(8)], 512, 0, ncnt)
                ncnt += 1
            for tb in range(5):
                c0 = OWN0 + 512 * tb
                n = 512 if tb < 4 else 256
                qk_norm(knT[:, 512 * tb:512 * tb + n], wkp, [xnT[:, dt, c0:c0 + n] for dt in range(8)], n, 1, ncnt)
                ncnt += 1
            qk_norm(knT[:, 2304:2336], wkp, [xnTm[:, dt, :] for dt in range(8)], 32, 1, ncnt)
            ncnt += 1
            if pr == 0:
                dbg("qnT", qnT, [128, NOWN])
                dbg("knT", knT, [128, 2336])
            for hh in range(2):
                S.memset('dve', qz[hh][64 - 64 * hh:128 - 64 * hh, :], 0.0)
                S.copy('dve', qz[hh][64 * hh:64 * hh + 64, :], qnT[64 * hh:64 * hh + 64, :])
            for hh in range(2):
                h = 2 * pr + hh
                bp = 64 * hh
                for b in range(4):
                    S.mm(ps[b][:, :], zer, qnT[:, 512 * b:512 * b + 512], start=True, stop=True)
                for b in range(4):
                    sA = ps[4 + 2 * (sc % 2)]
                    sc += 1
                    S.mm(sA[0:32, :], knT[bp:bp + 64, 2304:2336], qnT[bp:bp + 64, 512 * b:512 * b + 512])
                    S.act(PTm[0:32, 512 * b:512 * b + 512], sA[0:32, :], AF.Exp, bias=mbc[0:32, h:h + 1], scale=1.0)
                    S.mm(ps[b][:, :], vext2[0:32, 18, h, :], PTm[0:32, 512 * b:512 * b + 512], start=False, stop=False,
                         sgc=True)
                def emit_scores(j, slot):
                    I = Iof[j]
                    i0, n = I[0], len(I)
                    nq = 128 * n
                    sA = ps[4 + 2 * slot]
                    sB = ps[5 + 2 * slot]
                    pt = PT[slot]
                    segs = [(sA, 0, min(nq, 512))] + ([(sB, 512, nq)] if nq > 512 else [])
                    for (bank, lo, hi) in segs:
                        S.mm(bank[:, 0:hi - lo], knT[:, 128 * j:128 * j + 128],
                             qz[hh][:, 128 * i0 + lo:128 * i0 + hi], start=True, stop=False)
                        idxs = [ix for ix in range(n) if lo <= 128 * ix < hi]
                        runs = []
                        for ix in idxs:
                            p_ = bias_pos(I[ix], j)
                            if runs and runs[-1][1] + runs[-1][2] == p_ and runs[-1][0] + runs[-1][2] == ix:
                                runs[-1][2] += 1
                            else:
                                runs.append([ix, p_, 1])
                        for ri, (ix, p_, r) in enumerate(runs):
                            S.mm(bank[:, 128 * ix - lo:128 * (ix + r) - lo], ident, btp[:, hh, p_:p_ + r, :],
                                 start=False, stop=(ri == len(runs) - 1))
                        S.act(pt[:, lo:hi], bank[:, 0:hi - lo], AF.Exp)

                def emit_pv(j, slot):
                    I = Iof[j]
                    i0, n = I[0], len(I)
                    pt = PT[slot]
                    for b in range(4):
                        ilo, ihi = max(i0, 4 * b), min(i0 + n, 4 * b + 4)
                        if ilo >= ihi:
                            continue
                        S.mm(ps[b][:, 128 * (ilo - 4 * b):128 * (ihi - 4 * b)], vext2[:, j, h, :],
                             pt[:, 128 * (ilo - i0):128 * (ihi - i0)], start=False, stop=False, sgc=True)

                emit_scores(0, 0)
                for j in range(18):
                    if j + 1 < 18:
                        emit_scores(j + 1, (j + 1) % 2)
                    emit_pv(j, j % 2)
                osb = outsb[h % 2]
                for b in range(4):
                    S.copy('act' if b % 2 == 0 else 'dve', osb[:, b, :], ps[b][:, :])
                for b in range(4):
                    bb = b % 2
                    num0, den0 = (0, 64) if hh == 0 else (64, 0)
                    S.act(lnb[bb][num0:num0 + 64, :], osb[den0:den0 + 64, b, :], AF.Ln)
                    S.act(recq[bb][num0:num0 + 64, :], lnb[bb][num0:num0 + 64, :], AF.Exp, scale=-1.0)
                    S.tt('dve', ybT[num0:num0 + 64, pr, 512 * b:512 * b + 512], osb[num0:num0 + 64, b, :],
                         recq[bb][num0:num0 + 64, :], ALU.mult)
        dbg("ybT", ybT, [128, 4, NOWN])
        AR.pop()
        if stop_after == 'na':
            return finish(nc, S, y_out, dbg_outs, None)

        AR.push()
        assert AR.off < MIX_OFF
        mixT = arena_t[:, MIX_OFF // 4:ARENA_BYTES // 4].bitcast(BF16).rearrange("p (a b) -> p a b", a=8)
        wgab = [AR.alloc([8, 128], BF16) for _ in range(3)]
        wgbb = [AR.alloc([8, 128], BF16) for _ in range(3)]
        wab = [AR.alloc([8, 128], BF16) for _ in range(3)]
        wbb = [AR.alloc([4, 128], BF16) for _ in range(3)]
        sga = [AR.alloc([512], F32) for _ in range(2)]
        sgb = [AR.alloc([512], F32) for _ in range(2)]
        tg1 = [AR.alloc([512], F32) for _ in range(2)]
        tg2 = [AR.alloc([512], F32) for _ in range(2)]
        WOUT_OFF = MIX_OFF - 8 * 1024 * 2
        assert AR.off <= WOUT_OFF, (AR.off, WOUT_OFF)
        wout = arena_t[:, WOUT_OFF // 4:MIX_OFF // 4].bitcast(BF16).rearrange("p (a b) -> p a b", a=8)
        for q4 in range(4):
            S.dma('pool', wout[:, 2 * q4:2 * q4 + 2, :],
                  d_wout.rearrange("p (a b) -> p a b", a=8)[:, 2 * q4:2 * q4 + 2, :])
        gc = 0
        for blk in range(8):
            w = blk % 3
            S.dma('pool', wgab[w], d_wga[blk].rearrange("p (a b) -> p a b", a=8))
            S.dma('pool', wgbb[w], d_wgb[blk].rearrange("p (a b) -> p a b", a=8))
            S.dma('pool', wab[w], d_wa[blk].rearrange("p (a b) -> p a b", a=8))
            S.dma('pool', wbb[w], d_wb[blk].rearrange("p (a b) -> p a b", a=4))
            for tb in range(4):
                b = gc % 2
                gc += 1
                c0 = OWN0 + 512 * tb
                pga, pgb, pa_, pb_ = ps[4 * b], ps[4 * b + 1], ps[4 * b + 2], ps[4 * b + 3]
                for dt in range(8):
                    S.mm(pga[:, :], wgab[w][:, dt, :], xnT[:, dt, c0:c0 + 512], start=(dt == 0), stop=(dt == 7))
                for dt in range(8):
                    S.mm(pgb[:, :], wgbb[w][:, dt, :], xnT[:, dt, c0:c0 + 512], start=(dt == 0), stop=(dt == 7))
                for et in range(8):
                    S.mm(pa_[:, :], wab[w][:, et, :], yaT[:, et, 512 * tb:512 * tb + 512], start=(et == 0), stop=(et == 7))
                for c4 in range(4):
                    S.mm(pb_[:, :], wbb[w][:, c4, :], ybT[:, c4, 512 * tb:512 * tb + 512], start=(c4 == 0), stop=(c4 == 3))
                S.act(sga[b], pga[:, :], AF.Sigmoid)
                S.act(sgb[b], pgb[:, :], AF.Sigmoid)
                S.tt('dve', tg1[b], sga[b], pa_[:, :], ALU.mult)
                S.tt('dve', tg2[b], sgb[b], pb_[:, :], ALU.mult)
                S.tt('dve', mixT[:, blk, 512 * tb:512 * tb + 512], tg1[b], tg2[b], ALU.add)
        dbg("mixT", mixT, [128, 8, NOWN])
        AR.pop()
        if stop_after == 'g':
            return finish(nc, S, y_out, dbg_outs, None)

        AR.off = OFF_XNT
        h1 = AR.alloc([NT, 1024], F32)
        xn2T = AR.alloc([8, NOWN], BF16)
        OFF_OTMP = AR.off
        xq = [AR.alloc([1024], F32) for _ in range(4)]
        xn2b = [AR.alloc([1024], BF16) for _ in range(4)]
        sqj2 = AR.alloc([1024], BF16)
        ss2 = AR.alloc([16], F32)
        assert AR.off <= MIX_OFF - 16 * 1024, AR.off
        OFF_FFN = AR.off
        tcnt = [0]

        def O_proj(t0):
            for i in range(4):
                t = t0 + i
                S.dma('sp', xq[i], xe[OWN0 + 128 * t:OWN0 + 128 * t + 128, :])
            for i in range(4):
                t = t0 + i
                for half in range(2):
                    po = ps[(2 * i + half) % 6]
                    for dt in range(8):
                        S.mm(po[:, :], mixT[:, dt, 128 * t:128 * t + 128], wout[:, dt, 512 * half:512 * half + 512],
                             start=(dt == 0), stop=(dt == 7))
                    S.tt('dve', h1[:, t, 512 * half:512 * half + 512], po[:, :], xq[i][:, 512 * half:512 * half + 512],
                         ALU.add)
                S.act(sqj2, h1[:, t, :], AF.Square, accum=ss2[:, 8 * ((t0 // 4) % 2) + i:8 * ((t0 // 4) % 2) + i + 1])

        def O_norm(t0):
            o8 = 8 * ((t0 // 4) % 2)
            rsv = ss2[:, o8 + 4:o8 + 8]
            S.ts('dve', rsv, ss2[:, o8:o8 + 4], 1.0 / D, EPS, ALU.mult, ALU.add)
            S.act(rsv, rsv, AF.Sqrt)
            S.recip(rsv, rsv)
            for i in range(4):
                t = t0 + i
                S.ts('dve', xn2b[i], h1[:, t, :], ss2[:, o8 + 4 + i:o8 + 5 + i], None, ALU.mult)

        def O_tr(t0):
            for i in range(4):
                t = t0 + i
                pb = psb[6 + (tcnt[0] % 2)]
                tcnt[0] += 1
                for dt in range(8):
                    S.transpose(pb[:, dt * 128:dt * 128 + 128], xn2b[i][:, dt * 128:(dt + 1) * 128], ident)
                S.tt('dve', xn2T[:, :, 128 * t:128 * t + 128], pb[:, :].rearrange("p (a b) -> p a b", a=8),
                     g2.unsqueeze(2).to_broadcast([128, 8, 128]), ALU.mult)

        O_proj(0)
        for t0 in range(0, NT, 4):
            O_norm(t0)
            if t0 + 4 < NT:
                O_proj(t0 + 4)
            O_tr(t0)
        dbg("h1", h1, [128, NT, 1024])
        AR.off = WOUT_OFF
        wf2r = [AR.alloc([4, 1024], BF16) for _ in range(2)]
        AR.off = OFF_OTMP
        wf1r = [AR.alloc([4, 8, 128], BF16) for _ in range(2)]
        AR.off = MIX_OFF
        zTg = [AR.alloc([4, NOWN], BF16) for _ in range(2)]
        AR.off = OFF_FFN
        rl = [AR.alloc([512], BF16) for _ in range(2)]
        assert AR.off <= WOUT_OFF

        def ffn_load(G):
            S.dma('pool', wf1r[G % 2], d_wf1[4 * G:4 * G + 4].rearrange("a p (b c) -> p a b c", b=8))
            S.dma('pool', wf2r[G % 2], d_wf2[4 * G:4 * G + 4].rearrange("a p c -> p a c"))

        ffn_load(0)
        zc = 0
        bc = 0
        for G in range(8):
            if G + 1 < 8:
                ffn_load(G + 1)
            g2_ = G % 2
            for fbi in range(4):
                for tb in range(4):
                    b = zc % 2
                    pz = ps[zc % 8]
                    zc += 1
                    for dt in range(8):
                        S.mm(pz[:, :], wf1r[g2_][:, fbi, dt, :], xn2T[:, dt, 512 * tb:512 * tb + 512],
                             start=(dt == 0), stop=(dt == 7))
                    S.act(rl[b], pz[:, :], AF.Relu)
                    S.tt('dve', zTg[g2_][:, fbi, 512 * tb:512 * tb + 512], rl[b], rl[b], ALU.mult)
            for ts_ in range(4):
                for half in range(2):
                    bs = 4 * (bc % 2)
                    bc += 1
                    for fbi in range(4):
                        for k4 in range(4):
                            t = 4 * ts_ + k4
                            S.mm(ps[bs + k4][:, :], zTg[g2_][:, fbi, 128 * t:128 * t + 128],
                                 wf2r[g2_][:, fbi, 512 * half:512 * half + 512], start=(fbi == 0), stop=(fbi == 3))
                    for k4 in range(4):
                        t = 4 * ts_ + k4
                        hv = h1[:, t, 512 * half:512 * half + 512]
                        S.tt('dve', hv, ps[bs + k4][:, :], hv, ALU.add)
        out_toks = []
        for t in range(NT):
            out_toks.append(S.dma('sp' if t % 2 == 0 else 'act', y_out[128 * t:128 * t + 128, :], h1[:, t, :]))
        return finish(nc, S, y_out, dbg_outs, out_toks)


def finish(nc, S, y_out, dbg_outs, out_toks):
    toks = list(dbg_outs.values())
    if out_toks:
        toks += out_toks
    S.wait_tokens('sp', toks)
    S.emit()
    return nc


def _colvec(v):
    return np.ascontiguousarray(np.asarray(v, np.float32).reshape(8, 128).T)


def _rows_ptc(w):
    T = w.shape[0] // 128
    return np.ascontiguousarray(w.reshape(T, 128, w.shape[1]).transpose(1, 0, 2).reshape(128, -1))


def _col_blocks(w, c0, nblk, bw=128):
    return np.ascontiguousarray(np.stack([_rows_ptc(w[:, c0 + bw * i:c0 + bw * (i + 1)]) for i in range(nblk)]))


def _na_tables(hf, rpb, meta_bias):
    def tile(i, j):
        out = np.full((NH, 128, 128), NEG, np.float32)
        cl = np.arange(64)
        for kr in range(2):
            for qr in range(2):
                krow_l, qrow_l = 2 * j + kr, 2 * i + qr
                if hf == 0:
                    krow, qrow, kcol, qcol = krow_l, qrow_l, cl, cl
                else:
                    krow, qrow, kcol, qcol = 63 - krow_l, 63 - qrow_l, 63 - cl, 63 - cl
                r0 = min(max(qrow - 4, 0), 56)
                if not (r0 <= krow < r0 + 8):
                    continue
                win0 = np.clip(qcol - 8, 0, 48)
                ok = (kcol[:, None] >= win0[None, :]) & (kcol[:, None] < win0[None, :] + 16)
                dr = krow - qrow + 7
                dc = np.clip(kcol[:, None] - qcol[None, :], -15, 15) + 15
                vals = rpb[:, dr, dc]
                out[:, kr * 64:(kr + 1) * 64, qr * 64:(qr + 1) * 64] = np.where(ok[None], vals, NEG)
        return out
    kinds = [tile(0, j) for j in range(4)] + [tile(1, j) for j in range(4)] + [tile(8, 8 + d) for d in (2, 1, 0, -1, -2)]
    BT = np.stack(kinds)
    bt = BT.reshape(13, 4, 2, 128, 128).transpose(1, 3, 2, 0, 4).reshape(4, 128, 13 * 2 * 128)
    MB = np.full((NH, 32), NEG, np.float32)
    if hf == 0:
        MB[:, 0:16] = meta_bias
    else:
        MB[:, 16:32] = meta_bias[:, ::-1]
    return np.ascontiguousarray(bt), np.ascontiguousarray(MB.T)


def _const_masks():
    j = np.arange(128)[:, None]
    t = np.arange(128)[None, :]
    same = (j // 64) == (t // 64)
    cm = np.zeros((128, 6, 128), np.float32)
    cm[:, 0, :] = same & (j <= t)
    cm[:, 1, :] = same & (j >= t)
    cm[:, 2, :] = (j < 64) & (t >= 0)
    cm[:, 3, :] = (j >= 64) & (t >= 0)
    cm[:, 4, :] = (j == t)
    cm[:, 5, :] = same
    return cm


def prep_inputs(inp):
    f = lambda a: np.ascontiguousarray(np.asarray(a, np.float32))
    w_in = f(inp['w_in'])
    shared = {
        'wxm': _rows_ptc(w_in[:, 0:1024]),
        'wo': _col_blocks(w_in, 1024, 8),
        'wqm': np.ascontiguousarray(f(inp['mlstm_wq']).reshape(4, 2, 128, 128).transpose(2, 0, 1, 3).reshape(128, -1)),
        'wkm': np.ascontiguousarray(f(inp['mlstm_wk']).reshape(4, 2, 128, 128).transpose(2, 0, 1, 3).reshape(128, -1)),
        'wnq': _col_blocks(w_in, 2064, 4),
        'wnk': _col_blocks(w_in, 2576, 4),
        'wnv': _rows_ptc(w_in[:, 3088:3600]),
        'wga': _col_blocks(w_in, 3600, 8),
        'wgb': _col_blocks(w_in, 4624, 8),
        'wa': _col_blocks(f(inp['w_branch_a']), 0, 8),
        'wb': _col_blocks(f(inp['w_branch_b']), 0, 8),
        'wout': _rows_ptc(f(inp['w_out'])),
        'wf1': _col_blocks(f(inp['w_ff1']), 0, 32),
        'wf2': np.ascontiguousarray(f(inp['w_ff2']).reshape(32, 128, 1024)),
        'cmask': _const_masks(),
        'qkg': np.ascontiguousarray(np.stack([np.tile(f(inp['na_q_norm_g']), 2), np.tile(f(inp['na_k_norm_g']), 2)], axis=1)),
    }
    x = f(inp['x'])
    meta = f(inp['meta_tokens'])
    cwfull = f(inp['mlstm_conv_w'])[:, 0, :]
    gb = f(inp['mlstm_gate_b']).reshape(16)
    gcols = w_in[:, 2048:2064]
    zero1 = np.zeros((1, D), np.float32)
    z16 = np.zeros((16, D), np.float32)
    maps = []
    tabs = {}
    for core in range(8):
        b, hf = core // 2, core % 2
        m = dict(shared)
        if hf == 0:
            xe = np.concatenate([zero1, meta, x[b], z16, zero1])
            vl = (1.0, 0.0)
            cwl, gc, gbl = cwfull, gcols, gb
        else:
            xe = np.concatenate([zero1, z16, x[b][::-1], meta[::-1], zero1])
            vl = (0.0, 1.0)
            cwl = cwfull[::-1]
            gc = np.concatenate([gcols[:, 8:16], gcols[:, 0:8]], axis=1)
            gbl = np.concatenate([gb[8:16], gb[0:8]])
        m['xe'] = np.ascontiguousarray(xe)
        m['valid'] = np.ascontiguousarray(np.tile(np.array(vl, np.float32)[None, :], (128, 1)))
        m['gate_b'] = np.ascontiguousarray(np.tile(gbl[None, :], (128, 1)))
        m['wg'] = _rows_ptc(np.ascontiguousarray(gc))
        m['vecs'] = np.ascontiguousarray(np.concatenate(
            [_colvec(inp['norm1_g']), _colvec(cwl[0]), _colvec(cwl[1]), _colvec(cwl[2]), _colvec(inp['mlstm_conv_b']),
             _colvec(f(inp['mlstm_norm_g']).reshape(-1)), _colvec(inp['mlstm_skip']), _colvec(inp['norm2_g'])], axis=1))
        if hf not in tabs:
            tabs[hf] = _na_tables(hf, f(inp['na_rpb']), f(inp['na_meta_bias']))
        m['bt'], m['mb'] = tabs[hf]
        maps.append(m)
    return maps


_NC_CACHE = {}


def kernel(**inputs):
    maps = prep_inputs(inputs)
    if 'nc' not in _NC_CACHE:
        _NC_CACHE['nc'] = build_nc()
    nc = _NC_CACHE['nc']
    res = run_bass_kernel_spmd(nc, maps, core_ids=list(range(8)))
    out = np.zeros((4, 4096, D), np.float32)
    for core in range(8):
        b, hf = core // 2, core % 2
        y = np.asarray(res.results[core]["y"], np.float32)
        if hf == 0:
            out[b, 0:NOWN] = y
        else:
            out[b, NOWN:] = y[::-1]
    return out
```
